# Optimizing a Trainium2 kernel written in Bass

```python
import jax
import jax.numpy as jnp
from jax import lax
import numpy as np

D_MODEL = 1024
BATCH = 8
SEQ = 4096
DEPTH = 2

GRID_W = 64
CTX_LEN = 256
EPS = 1e-6
N_BRANCH = 3
N_MOD = 6

S5_WIDTH = 256
S5_GROUP = 16
S5_GROUPS = S5_WIDTH // S5_GROUP
S5_STATE = 64
S5_DT_MIN = 1e-3
S5_DT_MAX = 1e-1

LRU_WIDTH = 256
LRU_BLOCKS = 4
LRU_BLOCK = LRU_WIDTH // LRU_BLOCKS
LRU_CONV = 4
LRU_PAD = (1, 2)
LRU_C = 8.0

GLA_HEADS = 4
GLA_KEY = 256
GLA_VAL = 512
GLA_HEAD_K = GLA_KEY // GLA_HEADS
GLA_HEAD_V = GLA_VAL // GLA_HEADS
GLA_RANK = 16
GLA_TAU = 16.0
GLA_CHUNK = 64

FFN_HIDDEN = 2816
FFN_CONV = 3

PROJ_SPLITS = (S5_WIDTH, LRU_WIDTH, LRU_WIDTH, GLA_KEY, GLA_KEY, GLA_VAL, GLA_VAL, 2 * GLA_RANK, N_BRANCH * D_MODEL)
PROJ_WIDTH = 5408

kernel_name = "hybrid_s5_rglru_gla_prefix_dit"


def rms_norm(x, g):
    xf = x.astype(jnp.float32)
    y = xf * lax.rsqrt(jnp.mean(xf * xf, axis=-1, keepdims=True) + EPS)
    return (y * g.astype(jnp.float32)).astype(x.dtype)


def modulate(x, shift, scale):
    return x * (1 + scale) + shift


def dwconv(x, w, b, pad):
    y = lax.conv_general_dilated(x, w[:, None, :], window_strides=(1,), padding=[pad],
                                 dimension_numbers=('NWC', 'WIO', 'NWC'), feature_group_count=x.shape[-1])
    return y + b


def to_col_major(t, rows):
    b, l, ch = t.shape
    return t.reshape(b, rows, GRID_W, ch).transpose(0, 2, 1, 3).reshape(b, l, ch)


def from_col_major(t, rows):
    b, l, ch = t.shape
    return t.reshape(b, GRID_W, rows, ch).transpose(0, 2, 1, 3).reshape(b, l, ch)


def _real_combine(e1, e2):
    a1, b1 = e1
    a2, b2 = e2
    return a1 * a2, a2 * b1 + b2


def _complex_combine(e1, e2):
    ar1, ai1, br1, bi1 = e1
    ar2, ai2, br2, bi2 = e2
    return (ar1 * ar2 - ai1 * ai2, ar1 * ai2 + ai1 * ar2,
            ar2 * br1 - ai2 * bi1 + br2, ar2 * bi1 + ai2 * br1 + bi2)


def linear_scan(a, b, h0, reverse):
    first, last = (-1, 0) if reverse else (0, -1)
    b = b.at[:, first].add(a[:, first] * h0)
    _, h = lax.associative_scan(_real_combine, (a, b), reverse=reverse, axis=1)
    return h, h[:, last]


def s5_direction(u_ctx, u_lat, lam_re, lam_im, log_dt, b_re, b_im, c_re, c_im, reverse):
    f32 = jnp.float32
    lam_re, lam_im = lam_re.astype(f32), lam_im.astype(f32)
    dt = jnp.exp(log_dt.astype(f32))[:, None]
    mag = jnp.exp(lam_re * dt)
    ar, ai = mag * jnp.cos(lam_im * dt), mag * jnp.sin(lam_im * dt)
    den = lam_re * lam_re + lam_im * lam_im
    fr = ((ar - 1.0) * lam_re + ai * lam_im) / den
    fi = (ai * lam_re - (ar - 1.0) * lam_im) / den
    b_re, b_im = b_re.astype(f32), b_im.astype(f32)
    bbr = fr[..., None] * b_re - fi[..., None] * b_im
    bbi = fr[..., None] * b_im + fi[..., None] * b_re
    c_re, c_im = c_re.astype(f32), c_im.astype(f32)
    first, last = (-1, 0) if reverse else (0, -1)

    def run(u, h0r, h0i):
        length = u.shape[1]
        bur = jnp.einsum('blgh,gph->blgp', u, bbr)
        bui = jnp.einsum('blgh,gph->blgp', u, bbi)
        bur = bur.at[:, first].add(ar * h0r - ai * h0i)
        bui = bui.at[:, first].add(ar * h0i + ai * h0r)
        a_r = jnp.broadcast_to(ar, (1, length) + ar.shape)
        a_i = jnp.broadcast_to(ai, (1, length) + ai.shape)
        _, _, hr, hi = lax.associative_scan(_complex_combine, (a_r, a_i, bur, bui), reverse=reverse, axis=1)
        y = jnp.einsum('blgp,ghp->blgh', hr, c_re) - jnp.einsum('blgp,ghp->blgh', hi, c_im)
        return y, hr[:, last], hi[:, last]

    zeros = jnp.zeros((u_ctx.shape[0], S5_GROUPS, S5_STATE), f32)
    y_ctx, hr, hi = run(u_ctx, zeros, zeros)
    y_lat, _, _ = run(u_lat, hr, hi)
    return y_ctx, y_lat


def s5_branch(u_ctx, u_lat, lam_re, lam_im, log_dt, b_re, b_im, c_re, c_im, d_skip, w_glu, b_glu):
    def groups(t):
        bsz, length, _ = t.shape
        return t.astype(jnp.float32).reshape(bsz, length, S5_GROUPS, S5_GROUP)
    gc, gl = groups(u_ctx), groups(u_lat)
    yc0, yl0 = s5_direction(gc, gl, lam_re[0], lam_im[0], log_dt[0], b_re[0], b_im[0], c_re[0], c_im[0], False)
    yc1, yl1 = s5_direction(gc, gl, lam_re[1], lam_im[1], log_dt[1], b_re[1], b_im[1], c_re[1], c_im[1], True)

    def out(y, u):
        bsz, length, _ = u.shape
        y = y.reshape(bsz, length, S5_WIDTH) + d_skip.astype(jnp.float32) * u.astype(jnp.float32)
        y = jax.nn.gelu(y)
        return (y * jax.nn.sigmoid(y @ w_glu.astype(jnp.float32) + b_glu.astype(jnp.float32))).astype(u.dtype)
    return out(yc0 + yc1, u_ctx), out(yl0 + yl1, u_lat)


def rglru_coeffs(x, wa, ba, wi, bi, lam):
    f32 = jnp.float32
    bsz, length, width = x.shape
    xb = x.reshape(bsz, length, LRU_BLOCKS, LRU_BLOCK)
    r = jax.nn.sigmoid(jnp.einsum('blnd,nde->blne', xb, wa.astype(f32)).reshape(bsz, length, width) + ba.astype(f32))
    i = jax.nn.sigmoid(jnp.einsum('blnd,nde->blne', xb, wi.astype(f32)).reshape(bsz, length, width) + bi.astype(f32))
    log_a = -LRU_C * r * jax.nn.softplus(-lam.astype(f32))
    a = jnp.exp(log_a)
    b = jnp.sqrt(-jnp.expm1(2.0 * log_a)) * (i * x)
    return a, b


def rglru_branch(x_ctx, g_ctx, x_lat, g_lat, conv_w, conv_b, wa, ba, wi, bi, lam):
    f32 = jnp.float32
    xc = dwconv(x_ctx, conv_w, conv_b, LRU_PAD).astype(f32)
    xl = dwconv(x_lat, conv_w, conv_b, LRU_PAD).astype(f32)
    hc_sum, hl_sum = [], []
    for d in range(2):
        rev = d == 1
        a_c, b_c = rglru_coeffs(xc, wa[d], ba[d], wi[d], bi[d], lam[d])
        h_c, s = linear_scan(a_c, b_c, jnp.zeros((xc.shape[0], LRU_WIDTH), f32), rev)
        a_l, b_l = rglru_coeffs(xl, wa[d], ba[d], wi[d], bi[d], lam[d])
        h_l, _ = linear_scan(a_l, b_l, s, rev)
        hc_sum.append(h_c)
        hl_sum.append(h_l)
    yc = (hc_sum[0] + hc_sum[1]) * jax.nn.gelu(g_ctx.astype(f32))
    yl = (hl_sum[0] + hl_sum[1]) * jax.nn.gelu(g_lat.astype(f32))
    return yc.astype(x_ctx.dtype), yl.astype(x_lat.dtype)


def gla_chunked(q, k, v, log_a, s0):
    bsz, nh, length, dk = q.shape
    dv = v.shape[-1]
    n = length // GLA_CHUNK
    q, k, log_a = (t.reshape(bsz, nh, n, GLA_CHUNK, dk) for t in (q, k, log_a))
    v = v.reshape(bsz, nh, n, GLA_CHUNK, dv)
    cum = jnp.cumsum(log_a, axis=3)
    cum_last = cum[:, :, :, -1:, :]
    q_in = q * jnp.exp(cum)
    k_in = k * jnp.exp(-cum)
    k_out = k * jnp.exp(cum_last - cum)
    mask = jnp.tril(jnp.ones((GLA_CHUNK, GLA_CHUNK), dtype=bool))
    scores = jnp.where(mask, jnp.einsum('bhnik,bhnjk->bhnij', q_in, k_in), 0.0)
    o_intra = jnp.einsum('bhnij,bhnjv->bhniv', scores, v)
    ds = jnp.einsum('bhnjk,bhnjv->bhnkv', k_out, v)
    decay = jnp.exp(cum_last[:, :, :, 0, :])

    def step(s, inp):
        dec, dsn = inp
        return dec[..., None] * s + dsn, s
    s_fin, s_in = lax.scan(step, s0, (jnp.moveaxis(decay, 2, 0), jnp.moveaxis(ds, 2, 0)))
    s_in = jnp.moveaxis(s_in, 0, 2)
    o_inter = jnp.einsum('bhnik,bhnkv->bhniv', q_in, s_in)
    return (o_intra + o_inter).reshape(bsz, nh, length, dv), s_fin


def gla_branch(q_c, k_c, v_c, g_c, z_c, q_l, k_l, v_l, g_l, z_l, w2, b2, norm_g):
    f32 = jnp.float32

    def heads(t, dh):
        bsz, length, _ = t.shape
        return t.astype(f32).reshape(bsz, length, GLA_HEADS, dh).transpose(0, 2, 1, 3)
    scale = GLA_HEAD_K ** -0.5
    qc, kc, vc = heads(q_c, GLA_HEAD_K) * scale, heads(k_c, GLA_HEAD_K), heads(v_c, GLA_HEAD_V)
    ql, kl, vl = heads(q_l, GLA_HEAD_K) * scale, heads(k_l, GLA_HEAD_K), heads(v_l, GLA_HEAD_V)
    o_c, o_l = [], []
    for d in range(2):
        def log_alpha(z):
            zd = z[..., d * GLA_RANK:(d + 1) * GLA_RANK].astype(f32)
            la = jax.nn.log_sigmoid(zd @ w2[d].astype(f32) + b2[d].astype(f32)) / GLA_TAU
            return heads(la, GLA_HEAD_K)
        flip = (lambda t: jnp.flip(t, axis=2)) if d == 1 else (lambda t: t)
        s0 = jnp.zeros(qc.shape[:2] + (GLA_HEAD_K, GLA_HEAD_V), f32)
        oc, s = gla_chunked(flip(qc), flip(kc), flip(vc), flip(log_alpha(z_c)), s0)
        ol, _ = gla_chunked(flip(ql), flip(kl), flip(vl), flip(log_alpha(z_l)), s)
        o_c.append(flip(oc))
        o_l.append(flip(ol))

    def out(o, g):
        o = o * lax.rsqrt(jnp.mean(o * o, axis=-1, keepdims=True) + EPS) * norm_g.astype(f32)
        bsz, _, length, _ = o.shape
        o = o.transpose(0, 2, 1, 3).reshape(bsz, length, GLA_VAL)
        return (o * jax.nn.silu(g.astype(f32))).astype(g.dtype)
    return out(o_c[0] + o_c[1], g_c), out(o_l[0] + o_l[1], g_l)


def hybrid_mixer(u, uc, rows, need_ctx, w_in,
                 s5_lam_re, s5_lam_im, s5_log_dt, s5_b_re, s5_b_im, s5_c_re, s5_c_im, s5_d, s5_w_glu, s5_b_glu,
                 lru_conv_w, lru_conv_b, lru_wa, lru_ba, lru_wi, lru_bi, lru_lam,
                 gla_w2, gla_b2, gla_norm_g, w_br_s5, w_br_lru, w_br_gla, w_out):
    cuts = [int(i) for i in np.cumsum(PROJ_SPLITS)[:-1]]
    zl = jnp.split(u @ w_in, cuts, axis=-1)
    zc = jnp.split(uc @ w_in, cuts, axis=-1)
    s5_c, s5_l = s5_branch(zc[0], zl[0], s5_lam_re, s5_lam_im, s5_log_dt, s5_b_re, s5_b_im,
                           s5_c_re, s5_c_im, s5_d, s5_w_glu, s5_b_glu)
    lru_c, lru_l = rglru_branch(zc[1], zc[2], to_col_major(zl[1], rows), to_col_major(zl[2], rows),
                                lru_conv_w, lru_conv_b, lru_wa, lru_ba, lru_wi, lru_bi, lru_lam)
    lru_l = from_col_major(lru_l, rows)
    gla_c, gla_l = gla_branch(zc[3], zc[4], zc[5], zc[6], zc[7], zl[3], zl[4], zl[5], zl[6], zl[7],
                              gla_w2, gla_b2, gla_norm_g)

    def merge(y_s5, y_lru, y_gla, gates):
        g = jnp.split(jax.nn.sigmoid(gates), N_BRANCH, axis=-1)
        m = g[0] * (y_s5 @ w_br_s5) + g[1] * (y_lru @ w_br_lru) + g[2] * (y_gla @ w_br_gla)
        return m @ w_out
    y_lat = merge(s5_l, lru_l, gla_l, zl[8])
    y_ctx = merge(s5_c, lru_c, gla_c, zc[8]) if need_ctx else None
    return y_lat, y_ctx


def conv_ffn(u, w_up, conv_w, conv_b, w_down):
    hid = dwconv(u @ w_up, conv_w, conv_b, (1, 1))
    gate, val = jnp.split(hid, 2, axis=-1)
    return (jax.nn.silu(gate) * val) @ w_down


def setup_inputs(seed: int = 0) -> dict:
    key = jax.random.key(seed)
    keys = iter(jax.random.split(key, 48))
    f32 = jnp.float32

    def nrm(shape, s):
        return s * jax.random.normal(next(keys), shape, f32)

    def unif(shape, lo, hi):
        return jax.random.uniform(next(keys), shape, f32, lo, hi)
    D = D_MODEL
    G, P, H = S5_GROUPS, S5_STATE, S5_GROUP
    return {
        'x': nrm((BATCH, SEQ, D), 1.0),
        'c': nrm((BATCH, D), 1.0),
        'ctx': nrm((BATCH, CTX_LEN, D), 1.0),
        'c_ctx': nrm((D,), 1.0),
        'ln1_g': 1.0 + nrm((DEPTH, D), 0.02),
        'ln2_g': 1.0 + nrm((DEPTH, D), 0.02),
        'final_g': 1.0 + nrm((D,), 0.02),
        'ada_w': nrm((DEPTH, D, N_MOD * D), 0.5 * D ** -0.5),
        'ada_b': nrm((DEPTH, N_MOD * D), 0.01),
        'w_in': nrm((DEPTH, D, PROJ_WIDTH), D ** -0.5),
        's5_lam_re': -0.5 + nrm((DEPTH, 2, G, P), 0.01),
        's5_lam_im': jnp.pi * jnp.arange(P, dtype=f32) + nrm((DEPTH, 2, G, P), 0.01),
        's5_log_dt': unif((DEPTH, 2, G), float(np.log(S5_DT_MIN)), float(np.log(S5_DT_MAX))),
        's5_b_re': nrm((DEPTH, 2, G, P, H), (2 * H) ** -0.5),
        's5_b_im': nrm((DEPTH, 2, G, P, H), (2 * H) ** -0.5),
        's5_c_re': nrm((DEPTH, 2, G, H, P), (2 * P) ** -0.5),
        's5_c_im': nrm((DEPTH, 2, G, H, P), (2 * P) ** -0.5),
        's5_d': nrm((DEPTH, S5_WIDTH), 1.0),
        's5_w_glu': nrm((DEPTH, S5_WIDTH, S5_WIDTH), S5_WIDTH ** -0.5),
        's5_b_glu': nrm((DEPTH, S5_WIDTH), 0.01),
        'lru_conv_w': nrm((DEPTH, LRU_CONV, LRU_WIDTH), LRU_CONV ** -0.5),
        'lru_conv_b': nrm((DEPTH, LRU_WIDTH), 0.01),
        'lru_wa': nrm((DEPTH, 2, LRU_BLOCKS, LRU_BLOCK, LRU_BLOCK), LRU_BLOCK ** -0.5),
        'lru_ba': nrm((DEPTH, 2, LRU_WIDTH), 0.01),
        'lru_wi': nrm((DEPTH, 2, LRU_BLOCKS, LRU_BLOCK, LRU_BLOCK), LRU_BLOCK ** -0.5),
        'lru_bi': nrm((DEPTH, 2, LRU_WIDTH), 0.01),
        'lru_lam': (lambda p: jnp.log(p) - jnp.log1p(-p))(unif((DEPTH, 2, LRU_WIDTH), 0.9, 0.999) ** (1.0 / LRU_C)),
        'gla_w2': nrm((DEPTH, 2, GLA_RANK, GLA_KEY), GLA_RANK ** -0.5),
        'gla_b2': nrm((DEPTH, 2, GLA_KEY), 0.01),
        'gla_norm_g': 1.0 + nrm((DEPTH, GLA_HEAD_V), 0.02),
        'w_br_s5': nrm((DEPTH, S5_WIDTH, D), S5_WIDTH ** -0.5),
        'w_br_lru': nrm((DEPTH, LRU_WIDTH, D), LRU_WIDTH ** -0.5),
        'w_br_gla': nrm((DEPTH, GLA_VAL, D), GLA_VAL ** -0.5),
        'w_out': nrm((DEPTH, D, D), D ** -0.5),
        'ffn_w_up': nrm((DEPTH, D, 2 * FFN_HIDDEN), D ** -0.5),
        'ffn_conv_w': nrm((DEPTH, FFN_CONV, 2 * FFN_HIDDEN), FFN_CONV ** -0.5),
        'ffn_conv_b': nrm((DEPTH, 2 * FFN_HIDDEN), 0.01),
        'ffn_w_down': nrm((DEPTH, FFN_HIDDEN, D), FFN_HIDDEN ** -0.5),
    }


def reference(x, c, ctx, c_ctx, ln1_g, ln2_g, final_g, ada_w, ada_b, w_in,
              s5_lam_re, s5_lam_im, s5_log_dt, s5_b_re, s5_b_im, s5_c_re, s5_c_im, s5_d, s5_w_glu, s5_b_glu,
              lru_conv_w, lru_conv_b, lru_wa, lru_ba, lru_wi, lru_bi, lru_lam,
              gla_w2, gla_b2, gla_norm_g, w_br_s5, w_br_lru, w_br_gla, w_out,
              ffn_w_up, ffn_conv_w, ffn_conv_b, ffn_w_down):
    rows = x.shape[1] // GRID_W
    sc, scc = jax.nn.silu(c), jax.nn.silu(c_ctx)
    h, hc = x, ctx
    for l in range(DEPTH):
        need_ctx = l < DEPTH - 1
        mod = jnp.split((sc @ ada_w[l] + ada_b[l])[:, None, :], N_MOD, axis=-1)
        mod_c = jnp.split(scc @ ada_w[l] + ada_b[l], N_MOD, axis=-1)
        u = modulate(rms_norm(h, ln1_g[l]), mod[0], mod[1])
        uc = modulate(rms_norm(hc, ln1_g[l]), mod_c[0], mod_c[1])
        y, yc = hybrid_mixer(u, uc, rows, need_ctx, w_in[l],
                             s5_lam_re[l], s5_lam_im[l], s5_log_dt[l], s5_b_re[l], s5_b_im[l],
                             s5_c_re[l], s5_c_im[l], s5_d[l], s5_w_glu[l], s5_b_glu[l],
                             lru_conv_w[l], lru_conv_b[l], lru_wa[l], lru_ba[l], lru_wi[l], lru_bi[l], lru_lam[l],
                             gla_w2[l], gla_b2[l], gla_norm_g[l], w_br_s5[l], w_br_lru[l], w_br_gla[l], w_out[l])
        h = h + mod[2] * y
        u = modulate(rms_norm(h, ln2_g[l]), mod[3], mod[4])
        h = h + mod[5] * conv_ffn(u, ffn_w_up[l], ffn_conv_w[l], ffn_conv_b[l], ffn_w_down[l])
        if need_ctx:
            hc = hc + mod_c[2] * yc
            uc = modulate(rms_norm(hc, ln2_g[l]), mod_c[3], mod_c[4])
            hc = hc + mod_c[5] * conv_ffn(uc, ffn_w_up[l], ffn_conv_w[l], ffn_conv_b[l], ffn_w_down[l])
    return rms_norm(h, final_g)
```

```python
from contextlib import ExitStack
import numpy as np
import concourse.bass as bass
import concourse.mybir as mybir
from concourse.bass_utils import run_bass_kernel_spmd

F32 = mybir.dt.float32
BF16 = mybir.dt.bfloat16
AF = mybir.ActivationFunctionType
ALU = mybir.AluOpType

D = 1024
SEQ = 4096
CTX = 256
T = SEQ + CTX
DEPTH = 2
PROJ = 5408
FH = 2816
EPS = 1e-6
NCORES = 8


class Buf:
    __slots__ = ("t", "last_w", "readers", "name")

    def __init__(self, t, name=""):
        self.t = t
        self.last_w = None
        self.readers = []
        self.name = name

    def __getitem__(self, idx):
        return self.t[idx]


class Sched:
    ENG = ("pe", "act", "dve", "pool", "sp")

    def __init__(self, nc, stack, same_engine_sync=("act", "dve", "pool", "sp")):
        self.nc = nc
        self.stack = stack
        self.eng = {"pe": nc.tensor, "act": nc.scalar, "dve": nc.vector, "pool": nc.gpsimd, "sp": nc.sync}
        self.sem = {}
        self.cnt = {}
        for e in ("pe", "act", "dve", "pool"):
            self.sem[e] = stack.enter_context(nc.semaphore("s_" + e))
            self.cnt[e] = 0
        self.KD = 28
        self.dma_rr = {"sp": 0, "pool": 0}
        for q in ("sp", "pool"):
            for i in range(self.KD):
                k = f"dma_{q}_{i}"
                self.sem[k] = stack.enter_context(nc.semaphore("s_" + k))
                self.cnt[k] = 0
        self.waited = {e: {k: 0 for k in self.sem} for e in self.ENG}
        self.same = same_engine_sync
        self.n_inst = 0
        self.n_wait = 0

    def sbuf(self, name, shape, dtype):
        self.uid = getattr(self, "uid", 0) + 1
        name = f"sb{self.uid}_{name}"
        t = self.stack.enter_context(self.nc.sbuf_tensor(name, list(shape), dtype))
        return Buf(t, name)

    def psum(self, name, shape, dtype=F32):
        t = self.stack.enter_context(self.nc.psum_tensor(name, list(shape), dtype))
        return Buf(t, name)

    def dram(self, name, shape, dtype, kind="Internal"):
        t = self.nc.dram_tensor(name, list(shape), dtype, kind=kind)
        return Buf(t.ap(), name)

    def _deps(self, reads, writes):
        deps = []
        for b in reads:
            if b.last_w is not None:
                deps.append(b.last_w)
        for b in writes:
            if b.last_w is not None:
                deps.append(b.last_w)
            deps.extend(b.readers)
        return deps

    def _wait(self, e, deps):
        need = {}
        for (k, v) in deps:
            if k == e and (e not in self.same):
                continue
            if self.waited[e][k] < v and need.get(k, 0) < v:
                need[k] = v
        for k, v in need.items():
            self.eng[e].wait_ge(self.sem[k], v)
            self.waited[e][k] = v
            self.n_wait += 1

    def _mark(self, tok, reads, writes):
        for b in writes:
            b.last_w = tok
            b.readers = []
        for b in reads:
            if b.last_w is not tok:
                b.readers.append(tok)
                if len(b.readers) > 64:
                    best = {}
                    for (k, v) in b.readers:
                        if best.get(k, 0) < v:
                            best[k] = v
                    b.readers = list(best.items())

    def op(self, e, fn, reads=(), writes=()):
        self._wait(e, self._deps(reads, writes))
        inst = fn(self.eng[e])
        inst.then_inc(self.sem[e], 1)
        self.cnt[e] += 1
        self.n_inst += 1
        tok = (e, self.cnt[e])
        self._mark(tok, reads, writes)
        return tok

    def dma(self, q, out, in_, reads=(), writes=(), **kw):
        k = f"dma_{q}_{self.dma_rr[q] % self.KD}"
        self.dma_rr[q] += 1
        deps = self._deps(reads, writes)
        if self.cnt[k] > 0:
            deps.append((k, self.cnt[k]))
        self._wait(q, deps)
        inst = self.eng[q].dma_start(out=out, in_=in_, **kw)
        inst.then_inc(self.sem[k], 16)
        self.cnt[k] += 16
        self.n_inst += 1
        tok = (k, self.cnt[k])
        self._mark(tok, reads, writes)
        return tok

    def barrier(self):
        deps = [(k, self.cnt[k]) for k in self.sem if self.cnt[k] > 0]
        for e in self.ENG:
            self._wait(e, deps)

    def finish(self):
        deps = [(k, self.cnt[k]) for k in self.sem if self.cnt[k] > 0]
        self._wait("sp", deps)


TILES_A = [(0, 256)] + [(256 + 512 * i, 512) for i in range(8)]


class K:
    def __init__(self, debug=(), dbg_in=()):
        self.debug = set(debug)
        self.dbg_in = set(dbg_in)
        self.nc = bass.Bass("TRN2", target_bir_lowering=False)
        self.stack = ExitStack()
        self.s = None

    def din(self, name, shape, dtype=F32):
        return Buf(self.nc.dram_tensor(name, list(shape), dtype, kind="ExternalInput").ap(), name)

    def dscr(self, name, shape, dtype):
        kind = "ExternalOutput" if name in self.debug else ("ExternalInput" if name in self.dbg_in else "Internal")
        return Buf(self.nc.dram_tensor(name, list(shape), dtype, kind=kind).ap(), name)

    def build(self, phases=("P0", "A", "B", "C", "D"), layers=(0, 1)):
        nc = self.nc
        with self.stack as st:
            s = self.s = Sched(nc, st)
            I = self.I = {}
            I["hT"] = self.din("hT", [D, T])
            I["cc"] = self.din("cc", [128, 8, 2])
            I["ln1"] = self.din("ln1", [DEPTH, 128, 8])
            I["ln2"] = self.din("ln2", [DEPTH, 128, 8])
            I["lnf"] = self.din("lnf", [128, 8])
            I["ada_w"] = self.din("ada_w", [DEPTH, D, 6 * D])
            I["ada_b"] = self.din("ada_b", [DEPTH, 128, 48])
            I["w_in"] = self.din("w_in", [DEPTH, D, PROJ])
            I["w_br"] = self.din("w_br", [DEPTH, D, D])
            I["w_out"] = self.din("w_out", [DEPTH, D, D])
            I["w_up"] = self.din("w_up", [DEPTH, D, 2 * FH])
            I["w_down"] = self.din("w_down", [DEPTH, FH, D])
            I["fcw"] = self.din("fcw", [DEPTH, 128, 44, 3])
            I["fcb"] = self.din("fcb", [DEPTH, 128, 44])
            I["lcw"] = self.din("lcw", [DEPTH, 128, 2, 4])
            I["lcb"] = self.din("lcb", [DEPTH, 128, 2])
            I["lba"] = self.din("lba", [DEPTH, 128, 2, 2])
            I["lbi"] = self.din("lbi", [DEPTH, 128, 2, 2])
            I["llam"] = self.din("llam", [DEPTH, 128, 2, 2])
            I["lwa"] = self.din("lwa", [DEPTH, 128, 4, 128])
            I["lwi"] = self.din("lwi", [DEPTH, 128, 4, 128])
            I["gconst"] = self.din("gconst", [128, 2, 3, 256])
            I["ident"] = self.din("ident", [128, 128])
            I["s5mask"] = self.din("s5mask", [128, 2, 2, 256])
            I["s5d"] = self.din("s5d", [DEPTH, 128, 2])
            I["s5bg"] = self.din("s5bg", [DEPTH, 128, 2])
            I["s5wg"] = self.din("s5wg", [DEPTH, 256, 256])
            for nm in ("s5lre", "s5lim", "s5ldt"):
                I[nm] = self.din(nm, [DEPTH, 2, 128, 16])
            for nm in ("s5R1", "s5R2", "s5C1", "s5C2"):
                I[nm] = self.din(nm, [DEPTH, 2, 128, 16, 16])
            I["gw2p"] = self.din("gw2p", [DEPTH, 33, 2, 256])
            I["gng"] = self.din("gng", [DEPTH, 128, 1])
            self.out = Buf(nc.dram_tensor("outT", [D, SEQ], F32, kind="ExternalOutput").ap(), "outT")
            Z = self.Z = {}
            Z["h"] = self.dscr("h", [D, T], F32)
            Z["s5u"] = self.dscr("Zs5u", [256, T], BF16)
            Z["lx"] = self.dscr("Zlx", [256, T], F32)
            Z["lg"] = self.dscr("Zlg", [256, T], F32)
            Z["q"] = self.dscr("Zq", [256, T], F32)
            Z["k"] = self.dscr("Zk", [256, T], F32)
            Z["ktok"] = self.dscr("Zktok", [T, 256], F32)
            Z["vtok"] = self.dscr("Zvtok", [T, 512], BF16)
            Z["g"] = self.dscr("Zg", [512, T], BF16)
            Z["r"] = self.dscr("Zr", [32, T], F32)
            Z["m"] = self.dscr("Zm", [3072, T], BF16)
            Z["mod"] = self.dscr("Zmod", [DEPTH, 128, 96], F32)
            Z["y"] = self.dscr("Zy", [D, T], BF16)
            Z["u2"] = self.dscr("Zu2", [D, T], BF16)
            Z["mm"] = self.dscr("Zmm", [D, T], BF16)
            Z["hid"] = self.dscr("Zhid", [FH, T], BF16)
            Z["us"] = self.dscr("Zus", [16, 16, 16, T // 16], BF16)
            for nm_ in ("gq", "gk", "gs"):
                Z[nm_] = self.dscr("Z" + nm_, [2, T // 128, 128, 512], BF16)
            Z["ys"] = self.dscr("Zys", [16, 16, 16, T // 16], F32)
            self.ps = [s.psum(f"ps{i}", [128, 512]) for i in range(8)]
            self.ps_rr = 0
            self.ones_bf = s.sbuf("ones_bf", [128, 128], BF16)
            s.op("dve", lambda e: e.memset(self.ones_bf[:], 1.0), writes=[self.ones_bf])
            self.eps_t = s.sbuf("eps_t", [128, 1], F32)
            s.op("dve", lambda e: e.memset(self.eps_t[:], EPS), writes=[self.eps_t])
            self.mod = [s.sbuf(f"mod{l}", [128, 48, 2], F32) for l in range(DEPTH)]
            self.coef = [s.sbuf(f"coef{l}", [128, 4, 8, 2], F32) for l in range(DEPTH)]
            if "P0" in phases:
                self.phase_p0()
            for l in layers:
                if "A" in phases:
                    self.phase_a(l, Z["h"] if l > 0 else I["hT"])
                if "B" in phases or "LRU" in phases:
                    self.phase_lru(l)
                if "B" in phases or "S5" in phases:
                    self.phase_s5(l)
                if "B" in phases or "GLA" in phases:
                    self.phase_gla(l)
                if "C" in phases or "C1" in phases:
                    self.phase_c1(l, Z["h"] if l > 0 else I["hT"])
                if "C" in phases or "C2" in phases:
                    self.phase_c2(l, final=(l == DEPTH - 1))
            s.finish()
            print("instructions", s.n_inst, "waits", s.n_wait)
        return nc

    def next_ps(self):
        pool = getattr(self, "ps_pool", None) or self.ps[0:6]
        p = pool[self.ps_rr % len(pool)]
        self.ps_rr += 1
        return p

    def phase_p0(self):
        s, I = self.s, self.I
        with ExitStack() as st:
            s_old = s.stack
            s.stack = st
            s.barrier()
            ccf = s.sbuf("ccf", [128, 8, 2], F32)
            scb = s.sbuf("scb", [128, 8, 2], BF16)
            awb = [s.sbuf(f"awb{i}", [128, 8, 1024], BF16) for i in range(2)]
            adab = s.sbuf("adab", [128, 48], F32)
            g1 = s.sbuf("g1", [128, 8], F32)
            g2 = s.sbuf("g2", [128, 8], F32)
            s.dma("sp", ccf[:], I["cc"][:, :, :], reads=[I["cc"]], writes=[ccf])
            s.op("act", lambda e: e.activation(out=scb[:], in_=ccf[:], func=AF.Silu), reads=[ccf], writes=[scb])
            pm = self.ps[7]
            for l in range(DEPTH):
                s.dma("sp", adab[:], I["ada_b"][l], reads=[I["ada_b"]], writes=[adab])
                s.dma("sp", g1[:], I["ln1"][l], reads=[I["ln1"]], writes=[g1])
                s.dma("sp", g2[:], I["ln2"][l], reads=[I["ln2"]], writes=[g2])
                for j in range(6):
                    aw = awb[j % 2]
                    src = I["ada_w"][l, :, j * 1024:(j + 1) * 1024].rearrange("(kc p) n -> p kc n", p=128)
                    s.dma("pool", aw[:, :, :], src[:, :, :], reads=[I["ada_w"]], writes=[aw])
                    for fc in range(8):
                        col = (j * 8 + fc) * 2
                        for kc in range(8):
                            s.op("pe", lambda e: e.matmul(pm[:, col:col + 2], lhsT=aw[:, kc, fc * 128:(fc + 1) * 128],
                                                          rhs=scb[:, kc, :], start=(kc == 0), stop=(kc == 7)),
                                 reads=[aw, scb], writes=[pm])
                mod = self.mod[l]
                s.op("dve", lambda e: e.tensor_tensor(out=mod[:], in0=pm[:, 0:96].rearrange("p (a b) -> p a b", b=2),
                                                      in1=adab[:].unsqueeze(2).to_broadcast([128, 48, 2]), op=ALU.add),
                     reads=[pm, adab], writes=[mod])
                cf = self.coef[l]
                for (dst, g, jm) in ((0, g1, 1), (2, g2, 4)):
                    s.op("dve", lambda e: e.scalar_tensor_tensor(out=cf[:, dst], in0=mod[:, jm * 8:(jm + 1) * 8, :], scalar=1.0,
                                                                 in1=g[:].unsqueeze(2).to_broadcast([128, 8, 2]),
                                                                 op0=ALU.add, op1=ALU.mult),
                         reads=[mod, g], writes=[cf])
                for (dst, jm) in ((1, 0), (3, 3)):
                    s.op("dve", lambda e: e.tensor_copy(out=cf[:, dst], in_=mod[:, jm * 8:(jm + 1) * 8, :]), reads=[mod], writes=[cf])
                if "Zmod" in self.debug:
                    s.dma("sp", self.Z["mod"][l], mod[:].rearrange("p a b -> p (a b)"), reads=[mod], writes=[self.Z["mod"]])
            s.stack = s_old

    def norm_mod(self, hf, sq, u, rstd, n, l, which, is_ctx):
        s = self.s
        cf = self.coef[l]
        col = 1 if is_ctx else 0
        pm = self.ps[6]
        s.op("act", lambda e: e.activation(out=sq[:, :, :n], in_=hf[:, :, :n], func=AF.Square), reads=[hf], writes=[sq])
        for kc in range(8):
            s.op("pe", lambda e: e.matmul(pm[:, :n], lhsT=self.ones_bf[:], rhs=sq[:, kc, :n], start=(kc == 0), stop=(kc == 7)),
                 reads=[sq, self.ones_bf], writes=[pm])
        s.op("act", lambda e: e.activation(out=rstd[:, :n], in_=pm[:, :n], func=AF.Sqrt, bias=self.eps_t[:, 0:1], scale=1.0 / D),
             reads=[pm, self.eps_t], writes=[rstd])
        s.op("dve", lambda e: e.reciprocal(out=rstd[:, :n], in_=rstd[:, :n]), reads=[rstd], writes=[rstd])
        for kc in range(8):
            s.op("dve", lambda e: e.tensor_tensor(out=hf[:, kc, :n], in0=hf[:, kc, :n], in1=rstd[:, :n], op=ALU.mult),
                 reads=[hf, rstd], writes=[hf])
        for kc in range(8):
            s.op("act", lambda e: e.activation(out=u[:, kc, :n], in_=hf[:, kc, :n], func=AF.Identity,
                                               scale=cf[:, 2 * which, kc, col:col + 1], bias=cf[:, 2 * which + 1, kc, col:col + 1]),
                 reads=[hf, cf], writes=[u])

    def norm_mod_gen(self, hf, sq, u, rstd, n, l, which, is_ctx, mul_eng="dve"):
        s = self.s
        cf = self.coef[l]
        col = 1 if is_ctx else 0
        pm = self.ps[6]
        s.op("act", lambda e: e.activation(out=sq[:, :, :n], in_=hf[:, :, :n], func=AF.Square), reads=[hf], writes=[sq])
        yield
        for kc in range(8):
            s.op("pe", lambda e: e.matmul(pm[:, :n], lhsT=self.ones_bf[:], rhs=sq[:, kc, :n], start=(kc == 0), stop=(kc == 7)),
                 reads=[sq, self.ones_bf], writes=[pm])
            yield
        s.op("act", lambda e: e.activation(out=rstd[:, :n], in_=pm[:, :n], func=AF.Sqrt, bias=self.eps_t[:, 0:1], scale=1.0 / D),
             reads=[pm, self.eps_t], writes=[rstd])
        yield
        s.op("dve", lambda e: e.reciprocal(out=rstd[:, :n], in_=rstd[:, :n]), reads=[rstd], writes=[rstd])
        yield
        for kc in range(8):
            s.op(mul_eng, lambda e: e.tensor_tensor(out=hf[:, kc, :n], in0=hf[:, kc, :n], in1=rstd[:, :n], op=ALU.mult),
                 reads=[hf, rstd], writes=[hf])
            yield
        for kc in range(8):
            s.op("act", lambda e: e.activation(out=u[:, kc, :n], in_=hf[:, kc, :n], func=AF.Identity,
                                               scale=cf[:, 2 * which, kc, col:col + 1], bias=cf[:, 2 * which + 1, kc, col:col + 1]),
                 reads=[hf, cf], writes=[u])
            yield

    def phase_a(self, l, hsrc):
        s, I, Z = self.s, self.I, self.Z
        with ExitStack() as st:
            s_old = s.stack
            s.stack = st
            s.barrier()
            w = s.sbuf("w_in_sb", [128, 8, PROJ], BF16)
            wsrc = I["w_in"][l].rearrange("(kc p) n -> p kc n", p=128)
            wbk = [Buf(w.t, f"w_in_blk{i}") for i in range(4)]
            for bi, c0 in enumerate(range(0, PROJ, 1352)):
                s.dma("pool", w[:, :, c0:c0 + 1352], wsrc[:, :, c0:c0 + 1352], reads=[I["w_in"]], writes=[wbk[bi]])
            wblk = lambda lo, hi: [wbk[j] for j in range(lo // 1352, (hi - 1) // 1352 + 1)]
            hf = [s.sbuf(f"hfA{i}", [128, 8, 512], F32) for i in range(2)]
            sqs = [s.sbuf(f"sqA{i}", [128, 8, 512], BF16) for i in range(2)]
            us = [s.sbuf(f"uA{i}", [128, 8, 512], BF16) for i in range(2)]
            rstds = [s.sbuf(f"rstdA{i}", [128, 512], F32) for i in range(2)]
            stg_f = [s.sbuf(f"stgf{i}", [128, 512], F32) for i in range(4)]
            stg_b = [s.sbuf(f"stgb{i}", [128, 512], BF16) for i in range(4)]
            rr = {"f": 0, "b": 0, "e": 0}
            fm = [(0, 256, Z["s5u"], "b", AF.Copy), (256, 256, Z["lx"], "f", AF.Copy), (512, 256, Z["lg"], "f", AF.Copy),
                  (768, 256, Z["q"], "f", AF.Copy), (1024, 256, Z["k"], "f", AF.Copy), (1792, 512, Z["g"], "b", AF.Silu),
                  (2304, 32, Z["r"], "f", AF.Copy), (2336, 3072, Z["m"], "b", AF.Sigmoid)]
            hview = hsrc.t.rearrange("(kc p) t -> p kc t", p=128)

            def load(i):
                t0, n = TILES_A[i]
                s.dma("sp", hf[i % 2][:, :, :n], hview[:, :, t0:t0 + n], reads=[hsrc], writes=[hf[i % 2]])
            load(0)
            load(1)
            self.norm_mod(hf[0], sqs[0], us[0], rstds[0], TILES_A[0][1], l, 0, True)
            for ti, (t0, n) in enumerate(TILES_A):
                u = us[ti % 2]
                ng = None
                if ti + 1 < len(TILES_A):
                    nb = (ti + 1) % 2
                    ng = self.norm_mod_gen(hf[nb], sqs[nb], us[nb], rstds[nb], TILES_A[ti + 1][1], l, 0, False)
                n_mt = 0
                for (c0, ncols, dst, dt, func) in fm:
                    for m0 in range(0, ncols, 128):
                        mm = min(128, ncols - m0)
                        ps = self.next_ps()
                        for kc in range(8):
                            s.op("pe", lambda e: e.matmul(ps[:mm, :n], lhsT=w[:, kc, c0 + m0:c0 + m0 + mm], rhs=u[:, kc, :n],
                                                          start=(kc == 0), stop=(kc == 7)), reads=wblk(c0 + m0, c0 + m0 + mm) + [u], writes=[ps])
                        if dt == "f":
                            sg = stg_f[rr["f"] % 4]; rr["f"] += 1
                        else:
                            sg = stg_b[rr["b"] % 4]; rr["b"] += 1
                        if func == AF.Copy and rr["e"] % 2 == 0:
                            s.op("dve", lambda e: e.tensor_copy(out=sg[:mm, :n], in_=ps[:mm, :n]), reads=[ps], writes=[sg])
                        else:
                            s.op("act", lambda e: e.activation(out=sg[:mm, :n], in_=ps[:mm, :n], func=func), reads=[ps], writes=[sg])
                        rr["e"] += 1
                        s.dma("sp", dst[m0:m0 + mm, t0:t0 + n], sg[:mm, :n], reads=[sg], writes=[dst])
                        n_mt += 1
                        if ng is not None and n_mt >= 4:
                            next(ng, None)
                for sub in range(0, n, 128):
                    ps = self.next_ps()
                    for kc in range(8):
                        s.op("pe", lambda e: e.matmul(ps[:, 0:256], lhsT=u[:, kc, sub:sub + 128], rhs=w[:, kc, 1024:1280],
                                                      start=(kc == 0), stop=(kc == 7)), reads=wblk(1024, 1280) + [u], writes=[ps])
                    sg = stg_f[rr["f"] % 4]; rr["f"] += 1
                    s.op("dve", lambda e: e.tensor_copy(out=sg[:, 0:256], in_=ps[:, 0:256]), reads=[ps], writes=[sg])
                    s.dma("sp", Z["ktok"][t0 + sub:t0 + sub + 128, :], sg[:, 0:256], reads=[sg], writes=[Z["ktok"]])
                    ps = self.next_ps()
                    for kc in range(8):
                        s.op("pe", lambda e: e.matmul(ps[:, 0:512], lhsT=u[:, kc, sub:sub + 128], rhs=w[:, kc, 1280:1792],
                                                      start=(kc == 0), stop=(kc == 7)), reads=wblk(1280, 1792) + [u], writes=[ps])
                    sg = stg_b[rr["b"] % 4]; rr["b"] += 1
                    s.op("act", lambda e: e.activation(out=sg[:, 0:512], in_=ps[:, 0:512], func=AF.Copy), reads=[ps], writes=[sg])
                    s.dma("sp", Z["vtok"][t0 + sub:t0 + sub + 128, :], sg[:, 0:512], reads=[sg], writes=[Z["vtok"]])
                if ng is not None:
                    for _ in ng:
                        pass
                if ti + 2 < len(TILES_A):
                    load(ti + 2)
            s.stack = s_old


    @staticmethod
    def round_robin(gens):
        gens = list(gens)
        while gens:
            for g_ in list(gens):
                try:
                    next(g_)
                except StopIteration:
                    gens.remove(g_)

    def gelu_tanh(self, eng_tt, x, tmp, out, n):
        s = self.s
        s.op("act", lambda e: e.activation(out=tmp[:, :n], in_=x[:, :n], func=AF.Square), reads=[x], writes=[tmp])
        s.op("dve", lambda e: e.tensor_scalar(out=tmp[:, :n], in0=tmp[:, :n], scalar1=0.044715, scalar2=1.0, op0=ALU.mult, op1=ALU.add),
             reads=[tmp], writes=[tmp])
        s.op(eng_tt, lambda e: e.tensor_tensor(out=tmp[:, :n], in0=tmp[:, :n], in1=x[:, :n], op=ALU.mult), reads=[tmp, x], writes=[tmp])
        s.op("act", lambda e: e.activation(out=tmp[:, :n], in_=tmp[:, :n], func=AF.Sigmoid, scale=1.5957691216), reads=[tmp], writes=[tmp])
        s.op(eng_tt, lambda e: e.tensor_tensor(out=out[:, :n], in0=tmp[:, :n], in1=x[:, :n], op=ALU.mult), reads=[tmp, x], writes=[out])

    def phase_lru(self, l):
        s, I, Z = self.s, self.I, self.Z
        NP = 4360
        R0, R1 = 1, 4356
        with ExitStack() as st:
            s_old = s.stack
            s.stack = st
            s.barrier()
            cwt = s.sbuf("lcw", [128, 2, 4], F32)
            cbt = s.sbuf("lcb", [128, 2], F32)
            bat = s.sbuf("lba", [128, 2, 2], F32)
            bit = s.sbuf("lbi", [128, 2, 2], F32)
            lam = s.sbuf("llam", [128, 2, 2], F32)
            wab = s.sbuf("lwa", [128, 4, 128], BF16)
            wib = s.sbuf("lwi", [128, 4, 128], BF16)
            s.dma("sp", cwt[:], I["lcw"][l], reads=[I["lcw"]], writes=[cwt])
            s.dma("sp", cbt[:], I["lcb"][l], reads=[I["lcb"]], writes=[cbt])
            s.dma("sp", bat[:], I["lba"][l], reads=[I["lba"]], writes=[bat])
            s.dma("sp", bit[:], I["lbi"][l], reads=[I["lbi"]], writes=[bit])
            s.dma("sp", lam[:], I["llam"][l], reads=[I["llam"]], writes=[lam])
            s.dma("pool", wab[:], I["lwa"][l], reads=[I["lwa"]], writes=[wab])
            s.dma("pool", wib[:], I["lwi"][l], reads=[I["lwi"]], writes=[wib])
            yv = s.sbuf("l_y", [128, 4], F32)
            ser = s.sbuf("l_ser", [128, 4], F32)
            lnv = s.sbuf("l_ln", [128, 4], F32)
            msk = s.sbuf("l_msk", [128, 4], F32)
            n8 = s.sbuf("l_n8", [128, 2, 2], F32)
            n16 = s.sbuf("l_n16", [128, 2, 2], F32)
            lam2 = lam[:].rearrange("p a b -> p (a b)")
            s.op("act", lambda e: e.activation(out=yv[:], in_=lam2, func=AF.Exp, scale=-1.0), reads=[lam], writes=[yv])
            s.op("act", lambda e: e.activation(out=lnv[:], in_=yv[:], func=AF.Ln, bias=1.0), reads=[yv], writes=[lnv])
            s.op("dve", lambda e: e.tensor_scalar(out=ser[:], in0=yv[:], scalar1=-0.25, scalar2=1.0 / 3.0, op0=ALU.mult, op1=ALU.add), reads=[yv], writes=[ser])
            s.op("dve", lambda e: e.tensor_tensor(out=ser[:], in0=ser[:], in1=yv[:], op=ALU.mult), reads=[ser, yv], writes=[ser])
            s.op("dve", lambda e: e.tensor_scalar(out=ser[:], in0=ser[:], scalar1=-1.0, scalar2=0.5, op0=ALU.mult, op1=ALU.add), reads=[ser], writes=[ser])
            s.op("dve", lambda e: e.tensor_tensor(out=ser[:], in0=ser[:], in1=yv[:], op=ALU.mult), reads=[ser, yv], writes=[ser])
            s.op("dve", lambda e: e.tensor_scalar(out=ser[:], in0=ser[:], scalar1=-1.0, scalar2=1.0, op0=ALU.mult, op1=ALU.add), reads=[ser], writes=[ser])
            s.op("dve", lambda e: e.tensor_tensor(out=ser[:], in0=ser[:], in1=yv[:], op=ALU.mult), reads=[ser, yv], writes=[ser])
            s.op("dve", lambda e: e.tensor_scalar(out=msk[:], in0=yv[:], scalar1=0.03, scalar2=None, op0=ALU.is_lt), reads=[yv], writes=[msk])
            s.op("dve", lambda e: e.tensor_tensor(out=ser[:], in0=ser[:], in1=lnv[:], op=ALU.subtract), reads=[ser, lnv], writes=[ser])
            s.op("dve", lambda e: e.tensor_tensor(out=ser[:], in0=ser[:], in1=msk[:], op=ALU.mult), reads=[ser, msk], writes=[ser])
            s.op("dve", lambda e: e.tensor_tensor(out=ser[:], in0=ser[:], in1=lnv[:], op=ALU.add), reads=[ser, lnv], writes=[ser])
            s.op("dve", lambda e: e.tensor_scalar(out=n8[:].rearrange("p a b -> p (a b)"), in0=ser[:], scalar1=-8.0, scalar2=None, op0=ALU.mult), reads=[ser], writes=[n8])
            s.op("dve", lambda e: e.tensor_scalar(out=n16[:].rearrange("p a b -> p (a b)"), in0=ser[:], scalar1=-16.0, scalar2=None, op0=ALU.mult), reads=[ser], writes=[n16])

            xraw = s.sbuf("l_xraw", [128, SEQ], F32)
            xp = s.sbuf("l_xp", [128, NP], F32)
            xc = s.sbuf("l_xc", [128, NP], F32)
            xcb = s.sbuf("l_xcb", [128, NP], BF16)
            avs = [s.sbuf(f"l_a{i}", [128, NP], F32) for i in range(2)]
            bvs = [s.sbuf(f"l_b{i}", [128, NP], F32) for i in range(2)]
            av, bv = avs[0], bvs[0]
            hv = [s.sbuf(f"l_h{i}", [128, NP], F32) for i in range(2)]
            gl = s.sbuf("l_gl", [128, T], F32)
            ybf = s.sbuf("l_ybf", [128, T], BF16)
            it = [s.sbuf(f"l_it{i}", [128, 512], F32) for i in range(2)]
            pieces = [(a, min(a + 512, R1)) for a in range(R0, R1, 512)]
            for ct in range(2):
                rows = slice(ct * 128, (ct + 1) * 128)
                s.op("pool", lambda e: e.memset(xp[:], 0.0), writes=[xp])
                s.dma("sp", xp[:, 1:257], Z["lx"][rows, 0:256], reads=[Z["lx"]], writes=[xp])
                s.dma("sp", xraw[:], Z["lx"][rows, 256:T], reads=[Z["lx"]], writes=[xraw])
                s.dma("sp", gl[:], Z["lg"][rows, :], reads=[Z["lg"]], writes=[gl])
                s.op("act", lambda e: e.activation(out=xp[:, 260:4356].rearrange("p (c r) -> p r c", r=64),
                                                   in_=xraw[:].rearrange("p (r c) -> p r c", c=64), func=AF.Copy),
                     reads=[xraw], writes=[xp])
                n = R1 - R0
                s.op("act", lambda e: e.activation(out=xc[:, R0:R1], in_=xp[:, R0:R1], func=AF.Identity,
                                                   scale=cwt[:, ct, 1:2], bias=cbt[:, ct:ct + 1]), reads=[xp, cwt, cbt], writes=[xc])
                for (k, off) in ((0, -1), (2, 1), (3, 2)):
                    s.op("dve", lambda e: e.scalar_tensor_tensor(out=xc[:, R0:R1], in0=xp[:, R0 + off:R1 + off], scalar=cwt[:, ct, k:k + 1],
                                                                 in1=xc[:, R0:R1], op0=ALU.mult, op1=ALU.add),
                         reads=[xp, cwt, xc], writes=[xc])
                s.op("act", lambda e: e.activation(out=xcb[:, R0:R1], in_=xc[:, R0:R1], func=AF.Copy), reads=[xc], writes=[xcb])
                for (xs, tmp_b, ts) in ((slice(0, 256), it[0], slice(0, 256)), (slice(256, T), xraw, slice(0, SEQ))):
                    s.op("act", lambda e: e.activation(out=tmp_b[:, ts], in_=gl[:, xs], func=AF.Square), reads=[gl], writes=[tmp_b])
                    s.op("dve", lambda e: e.tensor_scalar(out=tmp_b[:, ts], in0=tmp_b[:, ts], scalar1=0.044715, scalar2=1.0, op0=ALU.mult, op1=ALU.add),
                         reads=[tmp_b], writes=[tmp_b])
                    s.op("pool", lambda e: e.tensor_tensor(out=tmp_b[:, ts], in0=tmp_b[:, ts], in1=gl[:, xs], op=ALU.mult), reads=[tmp_b, gl], writes=[tmp_b])
                    s.op("act", lambda e: e.activation(out=tmp_b[:, ts], in_=tmp_b[:, ts], func=AF.Sigmoid, scale=1.5957691216), reads=[tmp_b], writes=[tmp_b])
                    s.op("pool", lambda e: e.tensor_tensor(out=gl[:, xs], in0=tmp_b[:, ts], in1=gl[:, xs], op=ALU.mult), reads=[tmp_b, gl], writes=[gl])
                for d in range(2):
                    av, bv = avs[d], bvs[d]
                    for pi, (a0, a1) in enumerate(pieces):
                        m = a1 - a0
                        i_t = it[pi % 2]
                        pr, pq = self.next_ps(), self.next_ps()
                        s.op("pe", lambda e: e.matmul(pr[:, :m], lhsT=wab[:, d * 2 + ct, :], rhs=xcb[:, a0:a1], start=True, stop=True),
                             reads=[wab, xcb], writes=[pr])
                        s.op("pe", lambda e: e.matmul(pq[:, :m], lhsT=wib[:, d * 2 + ct, :], rhs=xcb[:, a0:a1], start=True, stop=True),
                             reads=[wib, xcb], writes=[pq])
                        s.op("act", lambda e: e.activation(out=av[:, a0:a1], in_=pr[:, :m], func=AF.Sigmoid, bias=bat[:, d, ct:ct + 1]),
                             reads=[pr, bat], writes=[av])
                        s.op("act", lambda e: e.activation(out=i_t[:, :m], in_=pq[:, :m], func=AF.Sigmoid, bias=bit[:, d, ct:ct + 1]),
                             reads=[pq, bit], writes=[i_t])
                        s.op("dve", lambda e: e.tensor_tensor(out=bv[:, a0:a1], in0=i_t[:, :m], in1=xc[:, a0:a1], op=ALU.mult),
                             reads=[i_t, xc], writes=[bv])
                    s.op("act", lambda e: e.activation(out=hv[d][:, R0:R1], in_=av[:, R0:R1], func=AF.Exp, scale=n16[:, d, ct:ct + 1]),
                         reads=[av, n16], writes=[hv[d]])
                    s.op("act", lambda e: e.activation(out=av[:, R0:R1], in_=av[:, R0:R1], func=AF.Exp, scale=n8[:, d, ct:ct + 1]),
                         reads=[av, n8], writes=[av])
                    s.op("act", lambda e: e.activation(out=hv[d][:, R0:R1], in_=hv[d][:, R0:R1], func=AF.Sqrt, scale=-1.0, bias=1.0),
                         reads=[hv[d]], writes=[hv[d]])
                    s.op("pool", lambda e: e.tensor_tensor(out=bv[:, R0:R1], in0=bv[:, R0:R1], in1=hv[d][:, R0:R1], op=ALU.mult),
                         reads=[bv, hv[d]], writes=[bv])
                    h = hv[d]
                    if d == 0:
                        s.op("dve", lambda e: e.tensor_tensor_scan(out=h[:, 1:257], data0=av[:, 1:257], data1=bv[:, 1:257], initial=0.0,
                                                                   op0=ALU.mult, op1=ALU.add), reads=[av, bv], writes=[h])
                        s.op("dve", lambda e: e.tensor_tensor_scan(out=h[:, 260:4356], data0=av[:, 260:4356], data1=bv[:, 260:4356],
                                                                   initial=h[:, 256:257], op0=ALU.mult, op1=ALU.add), reads=[av, bv, h], writes=[h])
                    else:
                        s.op("dve", lambda e: e.tensor_tensor_scan(out=h[:, 256:0:-1], data0=av[:, 256:0:-1], data1=bv[:, 256:0:-1], initial=0.0,
                                                                   op0=ALU.mult, op1=ALU.add), reads=[av, bv], writes=[h])
                        s.op("dve", lambda e: e.tensor_tensor_scan(out=h[:, 4355:259:-1], data0=av[:, 4355:259:-1], data1=bv[:, 4355:259:-1],
                                                                   initial=h[:, 1:2], op0=ALU.mult, op1=ALU.add), reads=[av, bv, h], writes=[h])
                s.op("pool", lambda e: e.tensor_tensor(out=hv[0][:, R0:R1], in0=hv[0][:, R0:R1], in1=hv[1][:, R0:R1], op=ALU.add),
                     reads=[hv[0], hv[1]], writes=[hv[0]])
                bv = gl
                s.op("dve", lambda e: e.tensor_tensor(out=ybf[:, 0:256], in0=hv[0][:, 1:257], in1=bv[:, 0:256], op=ALU.mult),
                     reads=[hv[0], bv], writes=[ybf])
                s.op("dve", lambda e: e.tensor_tensor(out=ybf[:, 256:T].rearrange("p (r c) -> p r c", c=64),
                                                      in0=hv[0][:, 260:4356].rearrange("p (c r) -> p r c", r=64),
                                                      in1=bv[:, 256:T].rearrange("p (r c) -> p r c", c=64), op=ALU.mult),
                     reads=[hv[0], bv], writes=[ybf])
                s.dma("sp", Z["y"][256 + ct * 128:256 + (ct + 1) * 128, :], ybf[:], reads=[ybf], writes=[Z["y"]])
            s.stack = s_old

    def phase_gla(self, l):
        s, I, Z = self.s, self.I, self.Z
        NTL = T // 128
        NS = 4
        with ExitStack() as st:
            s_old = s.stack
            s.stack = st
            s.barrier()
            gc = s.sbuf("g_const", [128, 2, 3, 256], F32)
            s.dma("sp", gc[:], I["gconst"][:, :, :, :], reads=[I["gconst"]], writes=[gc])
            gcb = s.sbuf("g_constb", [128, 2, 3, 256], BF16)
            s.op("act", lambda e: e.activation(out=gcb[:].rearrange("p a b c -> p (a b c)"), in_=gc[:].rearrange("p a b c -> p (a b c)"), func=AF.Copy),
                 reads=[gc], writes=[gcb])
            w2p = s.sbuf("g_w2p", [33, 2, 256], BF16)
            s.dma("pool", w2p[:], I["gw2p"][l], reads=[I["gw2p"]], writes=[w2p])
            ng = s.sbuf("g_ng", [128, 1], F32)
            s.dma("sp", ng[:], I["gng"][l], reads=[I["gng"]], writes=[ng])
            lnq = s.sbuf("g_lnq", [128, 1], F32)
            s.op("dve", lambda e: e.memset(lnq[:], float(np.log(0.125))), writes=[lnq])
            zrfs = [s.sbuf(f"g_zrf{i}", [32, 1088], F32) for i in range(4)]
            zrb = s.sbuf("g_zrb", [33, T], BF16)
            s.op("dve", lambda e: e.memset(zrb[32:33, :], 1.0), writes=[zrb])
            for ai_, a0 in enumerate(range(0, T, 1088)):
                zrf = zrfs[ai_]
                s.dma("sp", zrf[:], Z["r"][:, a0:a0 + 1088], reads=[Z["r"]], writes=[zrf])
            for ai_, a0 in enumerate(range(0, T, 1088)):
                zrf = zrfs[ai_]
                s.op("act", lambda e: e.activation(out=zrb[0:32, a0:a0 + 1088], in_=zrf[:], func=AF.Copy), reads=[zrf], writes=[zrb])
            osum_t = s.sbuf("g_osum", [128, 4, T], F32)
            osum = osum_t
            osb = [Buf(osum_t.t, f"osum{i}") for i in range(NTL)]
            s.op("dve", lambda e: e.memset(osum_t[:], 0.0), writes=[osum_t] + osb)
            dec_all = [s.sbuf(f"g_decall{dd}", [128, NTL, 2, 2], F32) for dd in range(2)]
            GQ = [Z["gq"].t[dd] for dd in range(2)]
            GK = [Z["gk"].t[dd] for dd in range(2)]
            GS = [Z["gs"].t[dd] for dd in range(2)]
            GQb = [[Buf(None, f"gq{dd}_{i}") for i in range(NTL)] for dd in range(2)]
            GKb = [[Buf(None, f"gk{dd}_{i}") for i in range(NTL)] for dd in range(2)]
            GSb = [[Buf(None, f"gs{dd}_{i}") for i in range(NTL)] for dd in range(2)]
            DB = {}
            for dd in range(2):
                B_ = {}
                B_["qbm"] = [s.sbuf(f"g_qbm{dd}{i}", [128, 2, 2, 2, 64], BF16) for i in range(NS)]
                B_["kin"] = [s.sbuf(f"g_kin{dd}{i}", [128, 2, 128], BF16) for i in range(NS)]
                B_["ko2"] = [s.sbuf(f"g_ko2{dd}{i}", [128, 2, 256], BF16) for i in range(NS)]
                B_["scb"] = [s.sbuf(f"g_scb{dd}{i}", [128, 4, 128], BF16) for i in range(NS)]
                B_["vt"] = [s.sbuf(f"g_vt{dd}{i}", [128, 512], BF16) for i in range(NS)]
                B_["dec"] = [s.sbuf(f"g_dec{dd}{i}", [128, 2, 2], F32) for i in range(NS)]
                for i in range(NS):
                    for bb in (B_["qbm"][i], B_["ko2"][i], B_["scb"][i]):
                        s.op("dve", lambda e: e.memset(bb[:], 0.0), writes=[bb])
                B_["Sf"] = [s.sbuf(f"g_S{dd}{i}", [128, 128], F32) for i in range(2)]
                B_["Sb"] = [s.sbuf(f"g_Sb{dd}{i}", [128, 128], BF16) for i in range(2)]
                B_["qt"] = [s.sbuf(f"g_qt{dd}{i}", [128, 2, 128], F32) for i in range(2)]
                B_["kt"] = [s.sbuf(f"g_kt{dd}{i}", [128, 2, 128], F32) for i in range(2)]
                B_["ktk"] = [s.sbuf(f"g_ktk{dd}{i}", [128, 256], F32) for i in range(2)]
                for nm_ in ("tA", "tB", "tC"):
                    B_[nm_] = [s.sbuf(f"g_{nm_}{dd}{i}", [128, 256], F32) for i in range(2)]
                B_["Lt"] = [s.sbuf(f"g_Lt{dd}{i}", [128, 256], BF16) for i in range(2)]
                DB[dd] = B_
            qv = Z["q"].t.rearrange("(ct p) t -> p ct t", p=128)
            kv = Z["k"].t.rearrange("(ct p) t -> p ct t", p=128)

            def p1(d, i, sl, b2):
                B_ = DB[d]
                qbm, kin, ko2, scb = B_["qbm"], B_["kin"], B_["ko2"], B_["scb"]
                qt, kt, ktk, tA, Lt, tB, tC = B_["qt"], B_["kt"], B_["ktk"], B_["tA"], B_["Lt"], B_["tB"], B_["tC"]
                bkA, bkB = self.ps[2 * (2 * d + b2)], self.ps[2 * (2 * d + b2) + 1]
                Mtri = gcb[:, d, 0, 0:128]
                Msuf = gcb[:, d, 0, 128:256]
                Mind = gcb[:, d, 2, 0:2]
                t0 = i * 128
                s.dma("sp", qt[b2][:], qv[:, :, t0:t0 + 128], reads=[Z["q"]], writes=[qt[b2]])
                s.dma("sp", kt[b2][:], kv[:, :, t0:t0 + 128], reads=[Z["k"]], writes=[kt[b2]])
                s.dma("sp", ktk[b2][:], Z["ktok"][t0:t0 + 128, :], reads=[Z["ktok"]], writes=[ktk[b2]])
                s.op("pe", lambda e: e.matmul(bkA[:, 0:256], lhsT=zrb[0:33, t0:t0 + 128], rhs=w2p[0:33, d, :], start=True, stop=True),
                     reads=[zrb, w2p], writes=[bkA])
                yield
                s.op("act", lambda e: e.activation(out=tA[b2][:], in_=bkA[:, 0:256], func=AF.Exp, scale=-1.0), reads=[bkA], writes=[tA[b2]])
                s.op("act", lambda e: e.activation(out=Lt[b2][:], in_=tA[b2][:], func=AF.Ln, bias=1.0), reads=[tA[b2]], writes=[Lt[b2]])
                yield
                s.op("pe", lambda e: e.matmul(bkA[:, 256:512], lhsT=Msuf, rhs=Lt[b2][:], start=True, stop=True), reads=[gcb, Lt[b2]], writes=[bkA])
                for ct in range(2):
                    s.op("pe", lambda e: e.matmul(bkB[:, ct * 128:(ct + 1) * 128], lhsT=Lt[b2][:, ct * 128:(ct + 1) * 128], rhs=Mtri,
                                                  start=True, stop=True), reads=[gcb, Lt[b2]], writes=[bkB])
                for ct in range(2):
                    s.op("pe", lambda e: e.matmul(bkB[:, 256 + 2 * ct:258 + 2 * ct], lhsT=Lt[b2][:, ct * 128:(ct + 1) * 128], rhs=Mind,
                                                  start=True, stop=True), reads=[gcb, Lt[b2]], writes=[bkB])
                yield
                s.op("act", lambda e: e.activation(out=tA[b2][:], in_=bkA[:, 256:512], func=AF.Exp, scale=-1.0 / 16), reads=[bkA], writes=[tA[b2]])
                s.op("act", lambda e: e.activation(out=tB[b2][:], in_=bkB[:, 0:256], func=AF.Exp, scale=-1.0 / 16, bias=lnq[:, 0:1]),
                     reads=[bkB, lnq], writes=[tB[b2]])
                s.op("act", lambda e: e.activation(out=tC[b2][:], in_=bkB[:, 0:256], func=AF.Exp, scale=1.0 / 16), reads=[bkB], writes=[tC[b2]])
                s.op("act", lambda e: e.activation(out=dec_all[d][:, i], in_=bkB[:, 256:260].rearrange("p (c k) -> p c k", k=2),
                                                   func=AF.Exp, scale=-1.0 / 16), reads=[bkB], writes=[dec_all[d]])
                yield
                for c in range(2):
                    s.op("dve", lambda e: e.tensor_tensor(out=ko2[sl][64 * c:64 * c + 64, c, :], in0=tA[b2][64 * c:64 * c + 64, :],
                                                          in1=ktk[b2][64 * c:64 * c + 64, :], op=ALU.mult),
                         reads=[tA[b2], ktk[b2]], writes=[ko2[sl]])
                for hh in range(2):
                    rs_ = slice(64 * hh, 64 * hh + 64)
                    s.op("dve", lambda e: e.tensor_tensor(out=qbm[sl][rs_, :, :, hh, :],
                                                          in0=tB[b2][rs_, :].rearrange("p (ct c i) -> p ct c i", ct=2, c=2),
                                                          in1=qt[b2][rs_, :, :].rearrange("p ct (c i) -> p ct c i", c=2), op=ALU.mult),
                         reads=[tB[b2], qt[b2]], writes=[qbm[sl]])
                s.op("pool", lambda e: e.tensor_tensor(out=kin[sl][:], in0=tC[b2][:].rearrange("p (c t) -> p c t", t=128),
                                                       in1=kt[b2][:], op=ALU.mult), reads=[tC[b2], kt[b2]], writes=[kin[sl]])
                yield
                for ct in range(2):
                    for c in range(2):
                        s.op("pe", lambda e: e.matmul(bkA[64 * c:64 * c + 64, ct * 128:(ct + 1) * 128],
                                                      lhsT=kin[sl][:, ct, 64 * c:64 * c + 64],
                                                      rhs=qbm[sl][:, ct, c, :, :].rearrange("p h i -> p (h i)"), start=True, stop=True),
                             reads=[kin[sl], qbm[sl]], writes=[bkA])
                yield
                for c in range(2):
                    rs_ = slice(64 * c, 64 * c + 64)
                    s.op("dve", lambda e: e.tensor_tensor(out=scb[sl][rs_, :, 64 * c:64 * c + 64],
                                                          in0=bkA[rs_, 0:256].rearrange("p (h i) -> p h i", i=64),
                                                          in1=gc[rs_, d, 1, :].rearrange("p (h i) -> p h i", i=64), op=ALU.mult),
                         reads=[bkA, gc], writes=[scb[sl]])
                s.dma("pool", GQ[d][i, :, :], qbm[sl][:].rearrange("p a b c e -> p (a b c e)"), reads=[qbm[sl]], writes=[GQb[d][i]])
                s.dma("pool", GK[d][i, :, :], ko2[sl][:].rearrange("p a b -> p (a b)"), reads=[ko2[sl]], writes=[GKb[d][i]])
                s.dma("pool", GS[d][i, :, :], scb[sl][:].rearrange("p a b -> p (a b)"), reads=[scb[sl]], writes=[GSb[d][i]])
                yield

            def p2load(d, i, sl):
                B_ = DB[d]
                t0 = i * 128
                s.dma("sp", B_["qbm"][sl][:].rearrange("p a b c e -> p (a b c e)"), GQ[d][i, :, :], reads=[GQb[d][i]], writes=[B_["qbm"][sl]])
                s.dma("sp", B_["ko2"][sl][:].rearrange("p a b -> p (a b)"), GK[d][i, :, :], reads=[GKb[d][i]], writes=[B_["ko2"][sl]])
                s.dma("sp", B_["scb"][sl][:].rearrange("p a b -> p (a b)"), GS[d][i, :, :], reads=[GSb[d][i]], writes=[B_["scb"][sl]])
                s.dma("sp", B_["vt"][sl][:], Z["vtok"][t0:t0 + 128, :], reads=[Z["vtok"]], writes=[B_["vt"][sl]])

            def p2(d, i, sl, par=0):
                B_ = DB[d]
                qbm, kin, ko2, scb, vt, Sf, Sb = B_["qbm"], B_["kin"], B_["ko2"], B_["scb"], B_["vt"], B_["Sf"], B_["Sb"]
                t0 = i * 128
                poB = self.ps[3 * d + par]
                for hd in range(4):
                    s.op("pe", lambda e: e.matmul(poB[:, hd * 128:(hd + 1) * 128], lhsT=vt[sl][:, hd * 128:(hd + 1) * 128],
                                                  rhs=scb[sl][:, hd, :], start=(hd == 0), stop=False), reads=[vt[sl], scb[sl]], writes=[poB])
                ov = osum[:, :, t0:t0 + 128]
                corder = (0, 1) if d == 0 else (1, 0)
                for c in corder:
                    for hd in range(4):
                        ct, hh = hd // 2, hd % 2
                        s.op("pe", lambda e: e.matmul(poB[:, hd * 128 + c * 64:hd * 128 + c * 64 + 64], lhsT=Sb[ct][:, :],
                                                      rhs=qbm[sl][:, ct, c, hh, :], start=False, stop=(c == corder[1] and hd == 3)),
                             reads=[Sb[ct], qbm[sl]], writes=[poB])
                    pd = self.ps[3 * d + 2]
                    for hd in range(4):
                        ct, hh = hd // 2, hd % 2
                        s.op("pe", lambda e: e.matmul(pd[64 * hh:64 * hh + 64, ct * 128:(ct + 1) * 128],
                                                      lhsT=ko2[sl][:, c, hd * 64:(hd + 1) * 64],
                                                      rhs=vt[sl][:, hd * 128:(hd + 1) * 128], start=True, stop=True),
                             reads=[ko2[sl], vt[sl]], writes=[pd])
                    yield
                    for ct in range(2):
                        s.op("dve", lambda e: e.scalar_tensor_tensor(out=Sb[ct][:], in0=Sf[ct][:], scalar=dec_all[d][:, i, ct, c:c + 1],
                                                                     in1=pd[:, ct * 128:(ct + 1) * 128], op0=ALU.mult, op1=ALU.add),
                             reads=[Sf[ct], dec_all[d], pd], writes=[Sb[ct]])
                    for ct in range(2):
                        s.op("dve", lambda e: e.scalar_tensor_tensor(out=Sf[ct][:], in0=Sf[ct][:], scalar=dec_all[d][:, i, ct, c:c + 1],
                                                                     in1=pd[:, ct * 128:(ct + 1) * 128], op0=ALU.mult, op1=ALU.add),
                             reads=[Sf[ct], dec_all[d], pd], writes=[Sf[ct]])
                    yield
                pbv = poB[:, 0:512].rearrange("p (h t) -> p h t", t=128)
                s.op("dve", lambda e: e.tensor_tensor(out=ov, in0=pbv, in1=ov, op=ALU.add), reads=[poB, osb[i]], writes=[osb[i]])
                yield

            def order_of(d):
                return list(range(NTL)) if d == 0 else [1, 0] + list(range(NTL - 1, 1, -1))

            def stream1(d, par):
                od = order_of(d)
                for n in range(par, len(od), 2):
                    yield from p1(d, od[n], n % NS, par)
            self.round_robin([stream1(0, 0), stream1(1, 0), stream1(0, 1), stream1(1, 1)])

            def dirgen(d):
                for ct in range(2):
                    s.op("pool", lambda e: e.memset(DB[d]["Sf"][ct][:], 0.0), writes=[DB[d]["Sf"][ct]])
                    s.op("pool", lambda e: e.memset(DB[d]["Sb"][ct][:], 0.0), writes=[DB[d]["Sb"][ct]])
                od = order_of(d)
                LOOK = 2
                for n in range(len(od) + LOOK):
                    if n < len(od):
                        p2load(d, od[n], n % NS)
                    if n >= LOOK:
                        yield from p2(d, od[n - LOOK], (n - LOOK) % NS, (n - LOOK) % 2)
            self.round_robin([dirgen(0), dirgen(1)])
            NSL = 3
            sq = [s.sbuf(f"g_sq{i}", [128, 512], BF16) for i in range(NSL)]
            rs = [s.sbuf(f"g_rs{i}", [128, 512], F32) for i in range(NSL)]
            gt = [s.sbuf(f"g_gt{i}", [128, 512], BF16) for i in range(NSL)]
            yb = [s.sbuf(f"g_yb{i}", [128, 512], BF16) for i in range(NSL)]
            ones128 = self.ones_bf

            def gepi(hd, t0, n, b2):
                pm = self.ps[b2]
                s.dma("sp", gt[b2][:, :n], Z["g"][hd * 128:(hd + 1) * 128, t0:t0 + n], reads=[Z["g"]], writes=[gt[b2]])
                s.op("act", lambda e: e.activation(out=sq[b2][:, :n], in_=osum[:, hd, t0:t0 + n], func=AF.Square), reads=osb, writes=[sq[b2]])
                yield
                s.op("pe", lambda e: e.matmul(pm[:, :n], lhsT=ones128[:], rhs=sq[b2][:, :n], start=True, stop=True), reads=[ones128, sq[b2]], writes=[pm])
                yield
                s.op("act", lambda e: e.activation(out=rs[b2][:, :n], in_=pm[:, :n], func=AF.Sqrt, bias=self.eps_t[:, 0:1], scale=1.0 / 128),
                     reads=[pm, self.eps_t], writes=[rs[b2]])
                yield
                s.op("dve", lambda e: e.reciprocal(out=rs[b2][:, :n], in_=rs[b2][:, :n]), reads=[rs[b2]], writes=[rs[b2]])
                yield
                s.op("pool", lambda e: e.tensor_tensor(out=rs[b2][:, :n], in0=rs[b2][:, :n], in1=osum[:, hd, t0:t0 + n], op=ALU.mult),
                     reads=[rs[b2]] + osb, writes=[rs[b2]])
                yield
                s.op("dve", lambda e: e.scalar_tensor_tensor(out=yb[b2][:, :n], in0=rs[b2][:, :n], scalar=ng[:, 0:1], in1=gt[b2][:, :n],
                                                             op0=ALU.mult, op1=ALU.mult), reads=[rs[b2], ng, gt[b2]], writes=[yb[b2]])
                s.dma("sp", Z["y"][512 + hd * 128:512 + (hd + 1) * 128, t0:t0 + n], yb[b2][:, :n], reads=[yb[b2]], writes=[Z["y"]])
                yield
            items = [(hd, t0, n) for hd in range(4) for (t0, n) in TILES_A]
            for g0 in range(0, len(items), NSL):
                self.round_robin([gepi(hd, t0, n, j) for j, (hd, t0, n) in enumerate(items[g0:g0 + NSL])])
            s.stack = s_old

    def phase_s5(self, l):
        s, I, Z = self.s, self.I, self.Z
        NCH = T // 16
        with ExitStack() as st:
            s_old = s.stack
            s.stack = st
            s.barrier()
            TT = lambda eng, out, a, b, op, rd, wr: s.op(eng, lambda e: e.tensor_tensor(out=out, in0=a, in1=b, op=op), reads=rd, writes=wr)
            ident = s.sbuf("s_ident", [128, 128], F32)
            s.dma("sp", ident[:], I["ident"][:, :], reads=[I["ident"]], writes=[ident])
            msk = s.sbuf("s_msk", [128, 2, 2, 256], F32)
            s.dma("sp", msk[:], I["s5mask"][:, :, :, :], reads=[I["s5mask"]], writes=[msk])
            sgn = s.sbuf("s_sgn", [128, 2], F32)
            s.op("dve", lambda e: e.memset(sgn[0:64, 0:1], -1.0), writes=[sgn])
            s.op("dve", lambda e: e.memset(sgn[64:128, 0:1], 1.0), writes=[sgn])
            s.op("dve", lambda e: e.memset(sgn[0:64, 1:2], 1.0), writes=[sgn])
            s.op("dve", lambda e: e.memset(sgn[64:128, 1:2], -1.0), writes=[sgn])
            dsk = s.sbuf("s_dsk", [128, 2], F32)
            bgl = s.sbuf("s_bgl", [128, 2], F32)
            s.dma("sp", dsk[:], I["s5d"][l], reads=[I["s5d"]], writes=[dsk])
            s.dma("sp", bgl[:], I["s5bg"][l], reads=[I["s5bg"]], writes=[bgl])
            wgl = s.sbuf("s_wgl", [128, 2, 256], BF16)
            self.load_w(wgl, I["s5wg"][l], I["s5wg"], 2, 256)
            ufm = s.sbuf("s_ufm", [128, 2, T], BF16)
            s.dma("sp", ufm[:], Z["s5u"].t.rearrange("(kc p) t -> p kc t", p=128), reads=[Z["s5u"]], writes=[ufm])
            ud = s.sbuf("s_ud", [128, 2, 16, NCH], BF16)
            for kc in range(2):
                s.op("act" if kc == 0 else "pool",
                     (lambda e: e.activation(out=ud[:, kc], in_=ufm[:, kc, :].rearrange("p (n s) -> p s n", s=16), func=AF.Copy)) if kc == 0 else
                     (lambda e: e.tensor_copy(out=ud[:, kc], in_=ufm[:, kc, :].rearrange("p (n s) -> p s n", s=16))),
                     reads=[ufm], writes=[ud])
            for g in range(16):
                kc, g8 = g // 8, g % 8
                s.dma("sp", Z["us"][g].rearrange("s h n -> h s n"), ud[g8 * 16:(g8 + 1) * 16, kc, :, :], reads=[ud], writes=[Z["us"]])
            def tl(name, shape):
                return s.sbuf("s_" + name, [128] + shape, F32)
            lre, lim, dt = tl("lre", [2, 16]), tl("lim", [2, 16]), tl("dt", [2, 16])
            for (dst, nm) in ((lre, "s5lre"), (lim, "s5lim"), (dt, "s5ldt")):
                s.dma("sp", dst[:], I[nm][l].rearrange("d p g -> p d g"), reads=[I[nm]], writes=[dst])
            f2 = lambda b_: b_[:].rearrange("p d g -> p (d g)")
            s.op("act", lambda e: e.activation(out=f2(dt), in_=f2(dt), func=AF.Exp), reads=[dt], writes=[dt])
            rho, th, mg, cs, sn = tl("rho", [2, 16]), tl("th", [2, 16]), tl("mg", [2, 16]), tl("cs", [2, 16]), tl("sn", [2, 16])
            ar, ai, t1, t2 = tl("ar", [2, 16]), tl("ai", [2, 16]), tl("t1", [2, 16]), tl("t2", [2, 16])
            TT("dve", f2(rho), f2(lre), f2(dt), ALU.mult, [lre, dt], [rho])
            TT("dve", f2(th), f2(lim), f2(dt), ALU.mult, [lim, dt], [th])
            hp = s.sbuf("s_hp", [128, 1], F32)
            s.op("dve", lambda e: e.memset(hp[:], float(np.pi / 2)), writes=[hp])
            s.op("act", lambda e: e.activation(out=f2(mg), in_=f2(rho), func=AF.Exp, scale=1.0 / 32), reads=[rho], writes=[mg])
            s.op("act", lambda e: e.activation(out=f2(cs), in_=f2(th), func=AF.Sin, scale=1.0 / 32, bias=hp[:, 0:1]), reads=[th, hp], writes=[cs])
            s.op("act", lambda e: e.activation(out=f2(sn), in_=f2(th), func=AF.Sin, scale=1.0 / 32), reads=[th], writes=[sn])
            TT("dve", f2(ar), f2(mg), f2(cs), ALU.mult, [mg, cs], [ar])
            TT("dve", f2(ai), f2(mg), f2(sn), ALU.mult, [mg, sn], [ai])

            def cmul(orr, oi, xr, xi, yr, yi, rd, wr, ta, tb_):
                TT("dve", ta, xr, yr, ALU.mult, rd, [t1])
                TT("dve", tb_, xi, yi, ALU.mult, rd, [t2])
                TT("dve", orr, ta, tb_, ALU.subtract, [t1, t2], wr)
                TT("dve", ta, xr, yi, ALU.mult, rd, [t1])
                TT("dve", tb_, xi, yr, ALU.mult, rd, [t2])
                TT("dve", oi, ta, tb_, ALU.add, [t1, t2], wr)

            def csq(xr, xi):
                TT("dve", f2(t1), f2(xr), f2(xr), ALU.mult, [xr], [t1])
                TT("dve", f2(t2), f2(xi), f2(xi), ALU.mult, [xi], [t2])
                TT("dve", f2(xi), f2(xr), f2(xi), ALU.mult, [xr, xi], [xi])
                TT("dve", f2(xr), f2(t1), f2(t2), ALU.subtract, [t1, t2], [xr])
                s.op("dve", lambda e: e.tensor_scalar(out=f2(xi), in0=f2(xi), scalar1=2.0, scalar2=None, op0=ALU.mult), reads=[xi], writes=[xi])
            for _ in range(5):
                csq(ar, ai)
            den, fr, fi, am1 = tl("den", [2, 16]), tl("fr", [2, 16]), tl("fi", [2, 16]), tl("am1", [2, 16])
            TT("dve", f2(t1), f2(lre), f2(lre), ALU.mult, [lre], [t1])
            TT("dve", f2(t2), f2(lim), f2(lim), ALU.mult, [lim], [t2])
            TT("dve", f2(den), f2(t1), f2(t2), ALU.add, [t1, t2], [den])
            s.op("dve", lambda e: e.reciprocal(out=f2(den), in_=f2(den)), reads=[den], writes=[den])
            s.op("dve", lambda e: e.tensor_scalar(out=f2(am1), in0=f2(ar), scalar1=-1.0, scalar2=None, op0=ALU.add), reads=[ar], writes=[am1])
            TT("dve", f2(t1), f2(am1), f2(lre), ALU.mult, [am1, lre], [t1])
            TT("dve", f2(t2), f2(ai), f2(lim), ALU.mult, [ai, lim], [t2])
            TT("dve", f2(fr), f2(t1), f2(t2), ALU.add, [t1, t2], [fr])
            TT("dve", f2(fr), f2(fr), f2(den), ALU.mult, [fr, den], [fr])
            TT("dve", f2(t1), f2(ai), f2(lre), ALU.mult, [ai, lre], [t1])
            TT("dve", f2(t2), f2(am1), f2(lim), ALU.mult, [am1, lim], [t2])
            TT("dve", f2(fi), f2(t1), f2(t2), ALU.subtract, [t1, t2], [fi])
            TT("dve", f2(fi), f2(fi), f2(den), ALU.mult, [fi, den], [fi])
            R1, R2, C1, C2 = tl("R1", [2, 16, 16]), tl("R2", [2, 16, 16]), tl("C1", [2, 16, 16]), tl("C2", [2, 16, 16])
            BB2, BBx, t3 = tl("BB2", [2, 16, 16]), tl("BBx", [2, 16, 16]), tl("t3", [2, 16, 16])
            for (dst, nm) in ((R1, "s5R1"), (R2, "s5R2"), (C1, "s5C1"), (C2, "s5C2")):
                for d in range(2):
                    s.dma("sp", dst[:, d], I[nm][l, d], reads=[I[nm]], writes=[dst])
            f3 = lambda b_: b_[:].rearrange("p d g h -> p (d g) h")
            f3f = lambda b_: b_[:].rearrange("p d g h -> p (d g h)")
            s.op("dve", lambda e: e.tensor_scalar(out=f3f(R2), in0=f3f(R2), scalar1=sgn[:, 0:1], scalar2=None, op0=ALU.mult), reads=[R2, sgn], writes=[R2])
            s.op("dve", lambda e: e.tensor_scalar(out=f3f(C1), in0=f3f(C1), scalar1=sgn[:, 1:2], scalar2=None, op0=ALU.mult), reads=[C1, sgn], writes=[C1])
            s.op("dve", lambda e: e.tensor_scalar(out=f3f(C2), in0=f3f(C2), scalar1=-1.0, scalar2=None, op0=ALU.mult), reads=[C2], writes=[C2])
            frb = f2(fr).unsqueeze(2).to_broadcast([128, 32, 16])
            fib = f2(fi).unsqueeze(2).to_broadcast([128, 32, 16])
            TT("dve", f3(BB2), f3(R1), frb, ALU.mult, [R1, fr], [BB2])
            TT("dve", f3(t3), f3(R2), fib, ALU.mult, [R2, fi], [t3])
            TT("dve", f3(BB2), f3(BB2), f3(t3), ALU.add, [BB2, t3], [BB2])
            TT("dve", f3(BBx), f3(R2), frb, ALU.mult, [R2, fr], [BBx])
            TT("dve", f3(t3), f3(R1), fib, ALU.mult, [R1, fi], [t3])
            TT("dve", f3(BBx), f3(BBx), f3(t3), ALU.subtract, [BBx, t3], [BBx])
            pwr, pwi = tl("pwr", [2, 16, 17]), tl("pwi", [2, 16, 17])
            pk = lambda b_, k0, k1: b_[:, :, :, k0:k1].rearrange("p d g k -> p (d g) k")
            tp1, tp2 = tl("tp1", [2, 16, 8]), tl("tp2", [2, 16, 8])
            tk = lambda b_, n_: b_[:, :, :, 0:n_].rearrange("p d g k -> p (d g) k")
            s.op("dve", lambda e: e.memset(pk(pwr, 0, 1), 1.0), writes=[pwr])
            s.op("dve", lambda e: e.memset(pk(pwi, 0, 1), 0.0), writes=[pwi])
            s.op("dve", lambda e: e.tensor_copy(out=pk(pwr, 1, 2), in_=f2(ar).unsqueeze(2)), reads=[ar], writes=[pwr])
            s.op("dve", lambda e: e.tensor_copy(out=pk(pwi, 1, 2), in_=f2(ai).unsqueeze(2)), reads=[ai], writes=[pwi])
            m_ = 1
            while m_ < 16:
                yr = pk(pwr, m_, m_ + 1).to_broadcast([128, 32, m_])
                yi = pk(pwi, m_, m_ + 1).to_broadcast([128, 32, m_])
                TT("dve", tk(tp1, m_), pk(pwr, 1, m_ + 1), yr, ALU.mult, [pwr], [tp1])
                TT("dve", tk(tp2, m_), pk(pwi, 1, m_ + 1), yi, ALU.mult, [pwi], [tp2])
                TT("dve", pk(pwr, m_ + 1, 2 * m_ + 1), tk(tp1, m_), tk(tp2, m_), ALU.subtract, [tp1, tp2, pwr], [pwr])
                TT("dve", tk(tp1, m_), pk(pwr, 1, m_ + 1), yi, ALU.mult, [pwr, pwi], [tp1])
                TT("dve", tk(tp2, m_), pk(pwi, 1, m_ + 1), yr, ALU.mult, [pwr, pwi], [tp2])
                TT("dve", pk(pwi, m_ + 1, 2 * m_ + 1), tk(tp1, m_), tk(tp2, m_), ALU.add, [tp1, tp2, pwi], [pwi])
                m_ *= 2
            nr, ni = tl("nr", [2, 16]), tl("ni", [2, 16])
            TT("dve", f2(t1), f2(ar), f2(ar), ALU.mult, [ar], [t1])
            TT("dve", f2(t2), f2(ai), f2(ai), ALU.mult, [ai], [t2])
            TT("dve", f2(t1), f2(t1), f2(t2), ALU.add, [t1, t2], [t1])
            s.op("dve", lambda e: e.reciprocal(out=f2(t1), in_=f2(t1)), reads=[t1], writes=[t1])
            TT("dve", f2(nr), f2(ar), f2(t1), ALU.mult, [ar, t1], [nr])
            TT("dve", f2(ni), f2(ai), f2(t1), ALU.mult, [ai, t1], [ni])
            s.op("dve", lambda e: e.tensor_scalar(out=f2(ni), in0=f2(ni), scalar1=-1.0, scalar2=None, op0=ALU.mult), reads=[ni], writes=[ni])
            for _ in range(4):
                csq(nr, ni)
            ckr, cki, ckn = tl("ckr", [2, 16, 9]), tl("cki", [2, 16, 9]), tl("ckn", [2, 16, 9])
            xr, xi = tl("xr", [2, 16]), tl("xi", [2, 16])
            s.op("dve", lambda e: e.tensor_copy(out=xr[:], in_=pwr[:, :, :, 16]), reads=[pwr], writes=[xr])
            s.op("dve", lambda e: e.tensor_copy(out=xi[:], in_=pwi[:, :, :, 16]), reads=[pwi], writes=[xi])
            for k in range(9):
                s.op("act", lambda e: e.activation(out=ckr[:, :, :, k], in_=xr[:], func=AF.Copy), reads=[xr], writes=[ckr])
                s.op("act", lambda e: e.activation(out=cki[:, :, :, k], in_=xi[:], func=AF.Copy), reads=[xi], writes=[cki])
                if k < 8:
                    csq(xr, xi)
            s.op("dve", lambda e: e.tensor_scalar(out=ckn[:].rearrange("p d g k -> p (d g k)"), in0=cki[:].rearrange("p d g k -> p (d g k)"),
                                                  scalar1=-1.0, scalar2=None, op0=ALU.mult), reads=[cki], writes=[ckn])
            tb = {}
            for d in range(2):
                tb[d] = dict(BB2=BB2, BBx=BBx, C1=C1, C2=C2, pwr=pwr, pwi=pwi, nr=nr, ni=ni, ckr=ckr, cki=cki, ckn=ckn)

            N = NCH
            CH = {}
            for par in range(2):
                for d in range(2):
                    nm = f"{par}{d}"
                    CH[(par, d)] = dict(
                        PinT=s.sbuf("s_PinT" + nm, [128, 16, 16], F32), PinX=s.sbuf("s_PinX" + nm, [128, 16, 16], F32),
                        Pin2=s.sbuf("s_Pin2" + nm, [128, 16, 16], F32), Qm=s.sbuf("s_Qm" + nm, [128, 16, 16], F32),
                        tq=s.sbuf("s_tq" + nm, [128, 16, 16], F32), tq2=s.sbuf("s_tq2" + nm, [128, 16, 16], F32),
                        PinL=s.sbuf("s_PinL" + nm, [128, 2, 128], BF16), PinLX=s.sbuf("s_PinLX" + nm, [128, 2, 128], BF16),
                        ML=s.sbuf("s_ML" + nm, [128, 2, 256], BF16), Qb=s.sbuf("s_Qb" + nm, [128, 256], BF16),
                        W=[s.sbuf(f"s_W{nm}{i}", [128, NCH], F32) for i in range(2)],
                        WX=[s.sbuf(f"s_WX{nm}{i}", [128, NCH], F32) for i in range(2)],
                        Sp=s.sbuf("s_Sp" + nm, [128, NCH], BF16))
            Ug = [s.sbuf(f"s_Ug{i}", [128, 2, NCH], BF16) for i in range(2)]
            Ysb = [s.sbuf(f"s_Ysb{i}", [128, NCH], F32) for i in range(4)]

            def chain(g, d, par):
                c_ = CH[(par, d)]
                t = tb[d]
                u_g = Ug[par]
                PinT, PinX, Pin2, Qm, tq, tq2 = c_["PinT"], c_["PinX"], c_["Pin2"], c_["Qm"], c_["tq"], c_["tq2"]
                PinL, PinLX, ML, Qb, W, WX, Sp = c_["PinL"], c_["PinLX"], c_["ML"], c_["Qb"], c_["W"], c_["WX"], c_["Sp"]
                if d == 0:
                    pin_r, pin_i = t["pwr"][:, d, g, 15::-1], t["pwi"][:, d, g, 15::-1]
                    q_r, q_i = t["pwr"][:, d, g, 1:17], t["pwi"][:, d, g, 1:17]
                else:
                    pin_r, pin_i = t["pwr"][:, d, g, 0:16], t["pwi"][:, d, g, 0:16]
                    q_r, q_i = t["pwr"][:, d, g, 16:0:-1], t["pwi"][:, d, g, 16:0:-1]
                b16 = lambda ap: ap.unsqueeze(2).to_broadcast([128, 16, 16])
                bs = lambda ap: ap.unsqueeze(1).to_broadcast([128, 16, 16])
                BB2g, BBxg, C1g, C2g = t["BB2"][:, d, g, :], t["BBx"][:, d, g, :], t["C1"][:, d, g, :], t["C2"][:, d, g, :]
                rdT = [t["pwr"], t["pwi"], t["BB2"], t["BBx"]]
                TT("dve", PinT[:], b16(pin_r), bs(BB2g), ALU.mult, rdT, [PinT]); yield
                TT("pool", tq[:], b16(pin_i), bs(BBxg), ALU.mult, rdT, [tq]); yield
                TT("pool", PinT[:], PinT[:], tq[:], ALU.add, [PinT, tq], [PinT])
                TT("dve", PinX[:], b16(pin_r), bs(BBxg), ALU.mult, rdT, [PinX]); yield
                TT("pool", tq2[:], b16(pin_i), bs(BB2g), ALU.mult, rdT, [tq2]); yield
                TT("pool", PinX[:], PinX[:], tq2[:], ALU.subtract, [PinX, tq2], [PinX])
                rdC = [t["pwr"], t["pwi"], t["C1"], t["C2"]]
                TT("dve", Qm[:], b16(q_r), bs(C1g), ALU.mult, rdC, [Qm]); yield
                TT("pool", tq[:], b16(q_i), bs(C2g), ALU.mult, rdC, [tq]); yield
                TT("pool", Qm[:], Qm[:], tq[:], ALU.add, [Qm, tq], [Qm])
                s.op("dve", lambda e: e.tensor_scalar(out=Pin2[:], in0=PinT[:], scalar1=t["nr"][:, d, g:g + 1], scalar2=None, op0=ALU.mult),
                     reads=[PinT, t["nr"]], writes=[Pin2]); yield
                s.op("dve", lambda e: e.scalar_tensor_tensor(out=Pin2[:].rearrange("p a b -> p (a b)"), in0=PinX[:].rearrange("p a b -> p (a b)"),
                                                             scalar=t["ni"][:, d, g:g + 1], in1=Pin2[:].rearrange("p a b -> p (a b)"),
                                                             op0=ALU.mult, op1=ALU.add), reads=[PinX, t["ni"], Pin2], writes=[Pin2]); yield
                s.op("act", lambda e: e.activation(out=Qb[:], in_=Qm[:].rearrange("p a b -> p (a b)"), func=AF.Copy), reads=[Qm], writes=[Qb])
                for (src, dstL) in ((PinT, PinL), (PinX, PinLX)):
                    pt = self.next_ps()
                    sv = src[:].rearrange("p a b -> p (a b)")
                    for kt in range(2):
                        s.op("pe", lambda e: e.matmul(pt[:, kt * 128:(kt + 1) * 128], lhsT=sv[:, kt * 128:(kt + 1) * 128], rhs=ident[:],
                                                      start=True, stop=True), reads=[src, ident], writes=[pt])
                    s.op("act", lambda e: e.activation(out=dstL[:].rearrange("p a b -> p (a b)"), in_=pt[:, 0:256], func=AF.Copy), reads=[pt], writes=[dstL])
                yield
                p2v = Pin2[:].rearrange("p a b -> p (a b)")
                for kt in range(2):
                    pm = self.next_ps()
                    s.op("pe", lambda e: e.matmul(pm[:, 0:256], lhsT=p2v[:, kt * 128:(kt + 1) * 128], rhs=Qm[:].rearrange("p a b -> p (a b)"),
                                                  start=True, stop=True), reads=[Pin2, Qm], writes=[pm])
                    s.op("dve", lambda e: e.tensor_tensor(out=ML[:, kt, :], in0=pm[:, 0:256], in1=msk[:, d, kt, :], op=ALU.mult),
                         reads=[pm, msk], writes=[ML]); yield
                pw_, px_ = self.next_ps(), self.next_ps()
                for (pp, L_) in ((pw_, PinL), (px_, PinLX)):
                    for kt in range(2):
                        s.op("pe", lambda e: e.matmul(pp[:, 0:N], lhsT=L_[:, kt, :], rhs=u_g[:, kt, :], start=(kt == 0), stop=(kt == 1)),
                             reads=[L_, u_g], writes=[pp])
                for (pp, dstW, eng) in ((pw_, W[0], "act"), (px_, WX[0], "pool")):
                    if d == 0:
                        parts = [(dstW[:, 0:N], pp[:, 0:N])]
                    else:
                        parts = [(dstW[:, 0:16], pp[:, 15::-1]), (dstW[:, 16:N], pp[:, N - 1:15:-1])]
                    for (o_, i_) in parts:
                        s.op("act", lambda e: e.activation(out=o_, in_=i_, func=AF.Copy), reads=[pp], writes=[dstW])
                yield
                cur = 0
                for k in range(9):
                    dk = 1 << k
                    a_, b_ = cur, 1 - cur
                    cr = t["ckr"][:, d, g, k:k + 1]
                    ci = t["cki"][:, d, g, k:k + 1]
                    cn = t["ckn"][:, d, g, k:k + 1]
                    rd = [W[a_], WX[a_], t["ckr"], t["cki"], t["ckn"]]
                    s.op("dve", lambda e: e.scalar_tensor_tensor(out=W[b_][:, dk:N], in0=W[a_][:, 0:N - dk], scalar=cr, in1=W[a_][:, dk:N],
                                                                 op0=ALU.mult, op1=ALU.add), reads=rd, writes=[W[b_]]); yield
                    if k < 8:
                        s.op("dve", lambda e: e.scalar_tensor_tensor(out=WX[b_][:, dk:N], in0=WX[a_][:, 0:N - dk], scalar=cr, in1=WX[a_][:, dk:N],
                                                                     op0=ALU.mult, op1=ALU.add), reads=rd, writes=[WX[b_]]); yield
                    s.op("dve", lambda e: e.scalar_tensor_tensor(out=W[b_][:, dk:N], in0=WX[a_][:, 0:N - dk], scalar=ci, in1=W[b_][:, dk:N],
                                                                 op0=ALU.mult, op1=ALU.add), reads=rd + [W[b_]], writes=[W[b_]]); yield
                    s.op("act", lambda e: e.activation(out=W[b_][:, 0:dk], in_=W[a_][:, 0:dk], func=AF.Copy), reads=[W[a_]], writes=[W[b_]])
                    if k < 8:
                        s.op("dve", lambda e: e.scalar_tensor_tensor(out=WX[b_][:, dk:N], in0=W[a_][:, 0:N - dk], scalar=cn, in1=WX[b_][:, dk:N],
                                                                     op0=ALU.mult, op1=ALU.add), reads=rd + [WX[b_]], writes=[WX[b_]]); yield
                        s.op("act", lambda e: e.activation(out=WX[b_][:, 0:dk], in_=WX[a_][:, 0:dk], func=AF.Copy), reads=[WX[a_]], writes=[WX[b_]])
                    cur = b_
                Wf = W[cur]
                if d == 0:
                    s.op("pool", lambda e: e.memset(Sp[:, 0:1], 0.0), writes=[Sp])
                    s.op("act", lambda e: e.activation(out=Sp[:, 1:N], in_=Wf[:, 0:N - 1], func=AF.Copy), reads=[Wf], writes=[Sp])
                else:
                    s.op("pool", lambda e: e.memset(Sp[:, 15:16], 0.0), writes=[Sp])
                    s.op("act", lambda e: e.activation(out=Sp[:, 0:15], in_=Wf[:, 14::-1], func=AF.Copy), reads=[Wf], writes=[Sp])
                    s.op("act", lambda e: e.activation(out=Sp[:, 16:N], in_=Wf[:, N - 2:14:-1], func=AF.Copy), reads=[Wf], writes=[Sp])
                yield

            def fin(g, par):
                u_g = Ug[par]
                for mt in range(2):
                    py = self.next_ps()
                    first = True
                    for d in range(2):
                        c_ = CH[(par, d)]
                        for kt in range(2):
                            s.op("pe", lambda e: e.matmul(py[:, 0:N], lhsT=c_["ML"][:, kt, mt * 128:(mt + 1) * 128], rhs=u_g[:, kt, :],
                                                          start=first, stop=False), reads=[c_["ML"], u_g], writes=[py])
                            first = False
                        s.op("pe", lambda e: e.matmul(py[:, 0:N], lhsT=c_["Qb"][:, mt * 128:(mt + 1) * 128], rhs=c_["Sp"][:, :],
                                                      start=False, stop=(d == 1)), reads=[c_["Qb"], c_["Sp"]], writes=[py])
                    yb_ = Ysb[par * 2 + mt]
                    s.op("act", lambda e: e.activation(out=yb_[:], in_=py[:, 0:N], func=AF.Copy), reads=[py], writes=[yb_])
                    s.dma("sp", Z["ys"][g, mt * 8:(mt + 1) * 8].rearrange("j h n -> (j h) n"), yb_[:], reads=[yb_], writes=[Z["ys"]])

            for g0 in range(0, 16, 2):
                gens = []
                for par in range(2):
                    g = g0 + par
                    s.dma("sp", Ug[par][:], Z["us"][g].rearrange("s h n -> (s h) n").rearrange("(kt p) n -> p kt n", p=128),
                          reads=[Z["us"]], writes=[Ug[par]])
                    for d in range(2):
                        gens.append(chain(g, d, par))
                self.round_robin(gens)
                for par in range(2):
                    fin(g0 + par, par)
            st3 = ExitStack()
            s.stack = st3
            Yfm = s.sbuf("s_Yfm", [128, 2, 16, NCH], F32)
            for g in range(16):
                kc, g8 = g // 8, g % 8
                s.dma("sp", Yfm[g8 * 16:(g8 + 1) * 16, kc, :, :], Z["ys"][g].rearrange("j h n -> h j n"), reads=[Z["ys"]], writes=[Yfm])
            NSL = 2
            yg = [[s.sbuf(f"s_yg{sl}{i}", [128, 512], F32) for i in range(2)] for sl in range(NSL)]
            ygb = [[s.sbuf(f"s_ygb{sl}{i}", [128, 512], BF16) for i in range(2)] for sl in range(NSL)]
            ytmp = [[s.sbuf(f"s_ytmp{sl}{i}", [128, 512], F32) for i in range(2)] for sl in range(NSL)]
            sgl = [[s.sbuf(f"s_sgl{sl}{i}", [128, 512], F32) for i in range(2)] for sl in range(NSL)]
            yo = [[s.sbuf(f"s_yo{sl}{i}", [128, 512], BF16) for i in range(2)] for sl in range(NSL)]

            def epi(t0, n, sl):
                n0, nn = t0 // 16, n // 16
                for kc in range(2):
                    x, tmp = yg[sl][kc], ytmp[sl][kc]
                    s.op("dve", lambda e: e.scalar_tensor_tensor(out=x[:, :n].rearrange("p (n j) -> p n j", j=16),
                                                                 in0=ufm[:, kc, t0:t0 + n].rearrange("p (n j) -> p n j", j=16), scalar=dsk[:, kc:kc + 1],
                                                                 in1=Yfm[:, kc, :, n0:n0 + nn].rearrange("p j n -> p n j"), op0=ALU.mult, op1=ALU.add),
                         reads=[ufm, dsk, Yfm], writes=[x])
                    yield
                    s.op("act", lambda e: e.activation(out=tmp[:, :n], in_=x[:, :n], func=AF.Square), reads=[x], writes=[tmp])
                    yield
                    s.op("dve", lambda e: e.tensor_scalar(out=tmp[:, :n], in0=tmp[:, :n], scalar1=0.044715, scalar2=1.0, op0=ALU.mult, op1=ALU.add),
                         reads=[tmp], writes=[tmp])
                    yield
                    s.op("pool", lambda e: e.tensor_tensor(out=tmp[:, :n], in0=tmp[:, :n], in1=x[:, :n], op=ALU.mult), reads=[tmp, x], writes=[tmp])
                    yield
                    s.op("act", lambda e: e.activation(out=tmp[:, :n], in_=tmp[:, :n], func=AF.Sigmoid, scale=1.5957691216), reads=[tmp], writes=[tmp])
                    yield
                    s.op("pool", lambda e: e.tensor_tensor(out=x[:, :n], in0=tmp[:, :n], in1=x[:, :n], op=ALU.mult), reads=[tmp, x], writes=[x])
                    yield
                    s.op("act", lambda e: e.activation(out=ygb[sl][kc][:, :n], in_=x[:, :n], func=AF.Copy), reads=[x], writes=[ygb[sl][kc]])
                    yield
                for mo in range(2):
                    pg = self.ps[sl * 2 + mo]
                    for kc in range(2):
                        s.op("pe", lambda e: e.matmul(pg[:, :n], lhsT=wgl[:, kc, mo * 128:(mo + 1) * 128], rhs=ygb[sl][kc][:, :n], start=(kc == 0), stop=(kc == 1)),
                             reads=[wgl, ygb[sl][kc]], writes=[pg])
                    yield
                    s.op("act", lambda e: e.activation(out=sgl[sl][mo][:, :n], in_=pg[:, :n], func=AF.Sigmoid, bias=bgl[:, mo:mo + 1]),
                         reads=[pg, bgl], writes=[sgl[sl][mo]])
                    yield
                    s.op("pool", lambda e: e.tensor_tensor(out=yo[sl][mo][:, :n], in0=sgl[sl][mo][:, :n], in1=yg[sl][mo][:, :n], op=ALU.mult),
                         reads=[sgl[sl][mo], yg[sl][mo]], writes=[yo[sl][mo]])
                    s.dma("sp", Z["y"][mo * 128:(mo + 1) * 128, t0:t0 + n], yo[sl][mo][:, :n], reads=[yo[sl][mo]], writes=[Z["y"]])
                    yield
            for g0 in range(0, len(TILES_A), NSL):
                self.round_robin([epi(t0, n, j) for j, (t0, n) in enumerate(TILES_A[g0:g0 + NSL])])
            s.barrier()
            st3.close()
            s.stack = s_old

    def load_w(self, dst, src_ap, src_buf, nkc, ncols, step=1024):
        s = self.s
        v = src_ap.rearrange("(kc p) n -> p kc n", p=128)
        for k0 in range(0, nkc, 8):
            k1 = min(nkc, k0 + 8)
            for c0 in range(0, ncols, step):
                c1 = min(ncols, c0 + step)
                s.dma("pool", dst[:, k0:k1, c0:c1], v[:, k0:k1, c0:c1], reads=[src_buf], writes=[dst])

    def phase_c1(self, l, hsrc):
        s, I, Z = self.s, self.I, self.Z
        with ExitStack() as st:
            s_old = s.stack
            s.stack = st
            s.barrier()
            wbr = s.sbuf("wbr", [128, 8, D], BF16)
            wout = s.sbuf("wout", [128, 8, D], BF16)
            self.load_w(wbr, I["w_br"][l], I["w_br"], 8, D)
            self.load_w(wout, I["w_out"][l], I["w_out"], 8, D)
            hf = [s.sbuf(f"hfC{i}", [128, 8, 512], F32) for i in range(2)]
            yt = [s.sbuf(f"ytC{i}", [128, 8, 512], BF16) for i in range(2)]
            gt = [s.sbuf(f"gtC{i}", [128, 24, 512], BF16) for i in range(2)]
            mt_ = s.sbuf("mC", [128, 8, 512], BF16)
            tts = [[s.sbuf(f"t{j}C{r}", [128, 512], F32) for j in range(3)] for r in range(3)]
            sq = s.sbuf("sqC", [128, 8, 512], BF16)
            u = s.sbuf("uC", [128, 8, 512], BF16)
            rstd = s.sbuf("rstdC", [128, 512], F32)
            hview = hsrc.t.rearrange("(kc p) t -> p kc t", p=128)
            hdst = Z["h"].t.rearrange("(kc p) t -> p kc t", p=128)
            yview = Z["y"].t.rearrange("(kc p) t -> p kc t", p=128)
            gview = Z["m"].t.rearrange("(kc p) t -> p kc t", p=128)
            u2view = Z["u2"].t.rearrange("(kc p) t -> p kc t", p=128)
            mod = self.mod[l]

            tiles_ = TILES_A[1:] if l == DEPTH - 1 else TILES_A

            def load(i):
                t0, n = tiles_[i]
                s.dma("sp", hf[i % 2][:, :, :n], hview[:, :, t0:t0 + n], reads=[hsrc], writes=[hf[i % 2]])
                s.dma("sp", yt[i % 2][:, :, :n], yview[:, :, t0:t0 + n], reads=[Z["y"]], writes=[yt[i % 2]])
                for a in range(0, 24, 8):
                    s.dma("sp", gt[i % 2][:, a:a + 8, :n], gview[:, a:a + 8, t0:t0 + n], reads=[Z["m"]], writes=[gt[i % 2]])
            def ngen(h_t_, n_, t0_):
                yield from self.norm_mod_gen(h_t_, sq, u, rstd, n_, l, 1, t0_ == 0)
                s.dma("sp", u2view[:, :, t0_:t0_ + n_], u[:, :, :n_], reads=[u], writes=[Z["u2"]])
                yield
            load(0)
            if len(tiles_) > 1:
                load(1)
            pend = None
            for ti, (t0, n) in enumerate(tiles_):
                h_t, y_t, g_t = hf[ti % 2], yt[ti % 2], gt[ti % 2]
                col = 1 if t0 == 0 else 0
                for mo in range(8):
                    if pend is not None:
                        for _ in range(4):
                            next(pend, None)
                    pa, pb, pc = self.next_ps(), self.next_ps(), self.next_ps()
                    for (pp, k0, k1) in ((pa, 0, 2), (pb, 2, 4), (pc, 4, 8)):
                        for kc in range(k0, k1):
                            s.op("pe", lambda e: e.matmul(pp[:, :n], lhsT=wbr[:, kc, mo * 128:(mo + 1) * 128], rhs=y_t[:, kc, :n],
                                                          start=(kc == k0), stop=(kc == k1 - 1)), reads=[wbr, y_t], writes=[pp])
                    t1, t2, t3 = tts[mo % 3]
                    s.op("dve", lambda e: e.tensor_tensor(out=t1[:, :n], in0=pa[:, :n], in1=g_t[:, mo, :n], op=ALU.mult),
                         reads=[pa, g_t], writes=[t1])
                    s.op("dve", lambda e: e.tensor_tensor(out=t2[:, :n], in0=pb[:, :n], in1=g_t[:, 8 + mo, :n], op=ALU.mult),
                         reads=[pb, g_t], writes=[t2])
                    s.op("dve", lambda e: e.tensor_tensor(out=t3[:, :n], in0=pc[:, :n], in1=g_t[:, 16 + mo, :n], op=ALU.mult),
                         reads=[pc, g_t], writes=[t3])
                    s.op("pool", lambda e: e.tensor_tensor(out=t1[:, :n], in0=t1[:, :n], in1=t2[:, :n], op=ALU.add),
                         reads=[t1, t2], writes=[t1])
                    s.op("pool", lambda e: e.tensor_tensor(out=mt_[:, mo, :n], in0=t1[:, :n], in1=t3[:, :n], op=ALU.add),
                         reads=[t1, t3], writes=[mt_])
                if pend is not None:
                    for _ in pend:
                        pass
                    pend = None
                    if ti + 1 < len(tiles_):
                        load(ti + 1)
                for mo in range(8):
                    pp = self.next_ps()
                    for kc in range(8):
                        s.op("pe", lambda e: e.matmul(pp[:, :n], lhsT=wout[:, kc, mo * 128:(mo + 1) * 128], rhs=mt_[:, kc, :n],
                                                      start=(kc == 0), stop=(kc == 7)), reads=[wout, mt_], writes=[pp])
                    s.op("dve", lambda e: e.scalar_tensor_tensor(out=h_t[:, mo, :n], in0=pp[:, :n], scalar=mod[:, 16 + mo, col:col + 1],
                                                                 in1=h_t[:, mo, :n], op0=ALU.mult, op1=ALU.add),
                         reads=[pp, mod, h_t], writes=[h_t])
                s.dma("sp", hdst[:, :, t0:t0 + n], h_t[:, :, :n], reads=[h_t], writes=[Z["h"]])
                pend = ngen(h_t, n, t0)
            for _ in pend:
                pass
            s.stack = s_old

    def phase_c2(self, l, final):
        s, I, Z = self.s, self.I, self.Z
        NT = 456
        tiles = [(0, 256, 0, 256)] + [(256 + NT * i, min(NT, T - 256 - NT * i), 256, T) for i in range(9)]
        if final:
            tiles = tiles[1:]
        with ExitStack() as st:
            s_old = s.stack
            s.stack = st
            s.barrier()
            wup = s.sbuf("wup", [128, 8, 2 * FH], BF16)
            wdn = s.sbuf("wdn", [128, 22, D], BF16)
            wupb = [Buf(wup.t, f"wup_blk{i}") for i in range(4)]
            wdnb = [Buf(wdn.t, f"wdn_blk{i}") for i in range(3)]
            vup = I["w_up"][l].rearrange("(kc p) n -> p kc n", p=128)
            for bi in (0, 2, 1, 3):
                s.dma("pool", wup[:, :, bi * 1408:(bi + 1) * 1408], vup[:, :, bi * 1408:(bi + 1) * 1408], reads=[I["w_up"]], writes=[wupb[bi]])
            vdn = I["w_down"][l].rearrange("(kc p) n -> p kc n", p=128)
            for bi, k0 in enumerate((0, 8, 16)):
                k1 = min(22, k0 + 8)
                s.dma("pool", wdn[:, k0:k1, :], vdn[:, k0:k1, :], reads=[I["w_down"]], writes=[wdnb[bi]])
            cw = s.sbuf("cw", [128, 44, 3], F32)
            cb = s.sbuf("cb", [128, 44], F32)
            s.dma("sp", cw[:], I["fcw"][l], reads=[I["fcw"]], writes=[cw])
            s.dma("sp", cb[:], I["fcb"][l], reads=[I["fcb"]], writes=[cb])
            gf = s.sbuf("gfin", [128, 8], F32)
            s.dma("sp", gf[:], I["lnf"][:, :], reads=[I["lnf"]], writes=[gf])
            NW = NT + 2
            ut = [s.sbuf(f"utF{i}", [128, 8, NW], BF16) for i in range(2)]
            hf = [s.sbuf("hfF0", [128, 8, NT], F32)] * 2
            hid = s.sbuf("hidF", [128, 22, NT], BF16)
            cg = [s.sbuf(f"cgF{i}", [128, NT], F32) for i in range(2)]
            cv = [s.sbuf(f"cvF{i}", [128, NT], F32) for i in range(2)]
            sq = s.sbuf("sqF", [128, 8, NT], BF16)
            rstd = s.sbuf("rstdF", [128, NT], F32)
            hview = Z["h"].t.rearrange("(kc p) t -> p kc t", p=128)
            uview = Z["u2"].t.rearrange("(kc p) t -> p kc t", p=128)
            oview = self.out.t.rearrange("(kc p) t -> p kc t", p=128)
            mod = self.mod[l]

            def load(i):
                t0, n, s0, s1 = tiles[i]
                u_t = ut[i % 2]
                lo, hi = max(t0 - 1, s0), min(t0 + n + 1, s1)
                if t0 - 1 < s0:
                    s.op("pool", lambda e: e.memset(u_t[:, :, 0:1], 0.0), writes=[u_t])
                if t0 + n + 1 > s1:
                    s.op("pool", lambda e: e.memset(u_t[:, :, n + 1:n + 2], 0.0), writes=[u_t])
                o = lo - (t0 - 1)
                s.dma("sp", u_t[:, :, o:o + hi - lo], uview[:, :, lo:hi], reads=[Z["u2"]], writes=[u_t])
            load(0)
            for ti, (t0, n, s0, s1) in enumerate(tiles):
                if ti + 1 < len(tiles):
                    load(ti + 1)
                u_t, h_t = ut[ti % 2], hf[ti % 2]
                s.dma("sp", h_t[:, :, :n], hview[:, :, t0:t0 + n], reads=[Z["h"]], writes=[h_t])
                col = 1 if t0 == 0 else 0
                for hc in range(22):
                    cgt, cvt = cg[hc % 2], cv[hc % 2]
                    for (dst, cch) in ((cgt, hc), (cvt, 22 + hc)):
                        pp = self.next_ps()
                        for kc in range(8):
                            s.op("pe", lambda e: e.matmul(pp[:, :n + 2], lhsT=wup[:, kc, cch * 128:(cch + 1) * 128], rhs=u_t[:, kc, :n + 2],
                                                          start=(kc == 0), stop=(kc == 7)), reads=[wupb[(cch * 128) // 1408], wupb[(cch * 128 + 127) // 1408], u_t], writes=[pp])
                        s.op("act", lambda e: e.activation(out=dst[:, :n], in_=pp[:, 1:n + 1], func=AF.Identity,
                                                           scale=cw[:, cch, 1:2], bias=cb[:, cch:cch + 1]),
                             reads=[pp, cw, cb], writes=[dst])
                        s.op("dve", lambda e: e.scalar_tensor_tensor(out=dst[:, :n], in0=pp[:, 0:n], scalar=cw[:, cch, 0:1],
                                                                     in1=dst[:, :n], op0=ALU.mult, op1=ALU.add),
                             reads=[pp, cw, dst], writes=[dst])
                        s.op("dve", lambda e: e.scalar_tensor_tensor(out=dst[:, :n], in0=pp[:, 2:n + 2], scalar=cw[:, cch, 2:3],
                                                                     in1=dst[:, :n], op0=ALU.mult, op1=ALU.add),
                             reads=[pp, cw, dst], writes=[dst])
                    s.op("act", lambda e: e.activation(out=cgt[:, :n], in_=cgt[:, :n], func=AF.Silu), reads=[cgt], writes=[cgt])
                    s.op("pool", lambda e: e.tensor_tensor(out=hid[:, hc, :n], in0=cgt[:, :n], in1=cvt[:, :n], op=ALU.mult),
                         reads=[cgt, cvt], writes=[hid])
                if "Zhid" in self.debug:
                    for a in range(0, 22, 6):
                        b_ = min(22, a + 6)
                        s.dma("sp", Z["hid"].t.rearrange("(kc p) t -> p kc t", p=128)[:, a:b_, t0:t0 + n], hid[:, a:b_, :n], reads=[hid], writes=[Z["hid"]])
                for mo in range(8):
                    pp = self.next_ps()
                    for hc in range(22):
                        s.op("pe", lambda e: e.matmul(pp[:, :n], lhsT=wdn[:, hc, mo * 128:(mo + 1) * 128], rhs=hid[:, hc, :n],
                                                      start=(hc == 0), stop=(hc == 21)), reads=[wdnb[hc // 8], hid], writes=[pp])
                    s.op("dve", lambda e: e.scalar_tensor_tensor(out=h_t[:, mo, :n], in0=pp[:, :n], scalar=mod[:, 40 + mo, col:col + 1],
                                                                 in1=h_t[:, mo, :n], op0=ALU.mult, op1=ALU.add),
                         reads=[pp, mod, h_t], writes=[h_t])
                if not final:
                    s.dma("sp", hview[:, :, t0:t0 + n], h_t[:, :, :n], reads=[h_t], writes=[Z["h"]])
                else:
                    if "h" in self.debug:
                        s.dma("sp", hview[:, :, t0:t0 + n], h_t[:, :, :n], reads=[h_t], writes=[Z["h"]])
                    pm = self.ps[6]
                    s.op("act", lambda e: e.activation(out=sq[:, :, :n], in_=h_t[:, :, :n], func=AF.Square), reads=[h_t], writes=[sq])
                    for kc in range(8):
                        s.op("pe", lambda e: e.matmul(pm[:, :n], lhsT=self.ones_bf[:], rhs=sq[:, kc, :n], start=(kc == 0), stop=(kc == 7)),
                             reads=[sq, self.ones_bf], writes=[pm])
                    s.op("act", lambda e: e.activation(out=rstd[:, :n], in_=pm[:, :n], func=AF.Sqrt, bias=self.eps_t[:, 0:1], scale=1.0 / D),
                         reads=[pm, self.eps_t], writes=[rstd])
                    s.op("dve", lambda e: e.reciprocal(out=rstd[:, :n], in_=rstd[:, :n]), reads=[rstd], writes=[rstd])
                    for kc in range(8):
                        s.op("dve", lambda e: e.scalar_tensor_tensor(out=h_t[:, kc, :n], in0=h_t[:, kc, :n], scalar=gf[:, kc:kc + 1],
                                                                     in1=rstd[:, :n], op0=ALU.mult, op1=ALU.mult),
                             reads=[h_t, gf, rstd], writes=[h_t])
                    s.dma("sp", oview[:, :, t0 - 256:t0 - 256 + n], h_t[:, :, :n], reads=[h_t], writes=[self.out])
            s.stack = s_old


def prep_inputs(inp, b):
    f = np.float32
    m = {}
    m["hT"] = np.ascontiguousarray(np.concatenate([inp["ctx"][b].T, inp["x"][b].T], axis=1)).astype(f, copy=False)
    cc = np.stack([inp["c"][b].reshape(8, 128).T, inp["c_ctx"].reshape(8, 128).T], axis=-1)
    m["cc"] = np.ascontiguousarray(cc).astype(f, copy=False)
    return m


def prep_shared(inp):
    f = np.float32
    m = {}
    m["ln1"] = np.ascontiguousarray(inp["ln1_g"].reshape(DEPTH, 8, 128).transpose(0, 2, 1))
    m["ln2"] = np.ascontiguousarray(inp["ln2_g"].reshape(DEPTH, 8, 128).transpose(0, 2, 1))
    m["lnf"] = np.ascontiguousarray(inp["final_g"].reshape(8, 128).T)
    m["ada_w"] = inp["ada_w"]
    m["ada_b"] = np.ascontiguousarray(inp["ada_b"].reshape(DEPTH, 48, 128).transpose(0, 2, 1))
    m["w_in"] = inp["w_in"]
    m["w_br"] = np.concatenate([inp["w_br_s5"], inp["w_br_lru"], inp["w_br_gla"]], axis=1)
    m["w_out"] = inp["w_out"]
    m["w_up"] = inp["ffn_w_up"]
    m["w_down"] = inp["ffn_w_down"]
    m["fcw"] = inp["ffn_conv_w"].transpose(0, 2, 1).reshape(DEPTH, 44, 128, 3).transpose(0, 2, 1, 3)
    m["fcb"] = inp["ffn_conv_b"].reshape(DEPTH, 44, 128).transpose(0, 2, 1)
    m["lcw"] = inp["lru_conv_w"].reshape(DEPTH, 4, 2, 128).transpose(0, 3, 2, 1)
    m["lcb"] = inp["lru_conv_b"].reshape(DEPTH, 2, 128).transpose(0, 2, 1)
    for nm, src in (("lba", "lru_ba"), ("lbi", "lru_bi"), ("llam", "lru_lam")):
        m[nm] = inp[src].reshape(DEPTH, 2, 2, 128).transpose(0, 3, 1, 2)
    for nm, src in (("lwa", "lru_wa"), ("lwi", "lru_wi")):
        bd = np.zeros((DEPTH, 128, 4, 128), f)
        for d in range(2):
            for ct in range(2):
                for nb in range(2):
                    bd[:, nb * 64:(nb + 1) * 64, d * 2 + ct, nb * 64:(nb + 1) * 64] = inp[src][:, d, 2 * ct + nb]
        m[nm] = bd
    w2p = np.zeros((DEPTH, 33, 2, 256), f)
    for d in range(2):
        w2p[:, 16 * d:16 * d + 16, d, :] = inp["gla_w2"][:, d]
        w2p[:, 32, d, :] = inp["gla_b2"][:, d]
    m["gw2p"] = w2p
    m["gng"] = inp["gla_norm_g"].reshape(DEPTH, 128, 1)
    gc = np.zeros((128, 2, 3, 256), f)
    tt = np.arange(128)
    same = (tt[:, None] // 64) == (tt[None, :] // 64)
    jj = np.arange(64)
    for d in range(2):
        if d == 0:
            tri = same & (tt[:, None] <= tt[None, :])
            suf = same & (tt[:, None] > tt[None, :])
            mk = jj[:, None] <= jj[None, :]
        else:
            tri = same & (tt[:, None] >= tt[None, :])
            suf = same & (tt[:, None] < tt[None, :])
            mk = jj[:, None] >= jj[None, :]
        gc[:, d, 0, 0:128] = tri
        gc[:, d, 0, 128:256] = suf
        gc[:, d, 1, :] = np.tile(np.tile(mk, (2, 1)), (1, 4))
        gc[:, d, 2, 0] = tt < 64
        gc[:, d, 2, 1] = tt >= 64
    m["gconst"] = gc
    m["ident"] = np.eye(128, dtype=f)
    pp = np.arange(128)
    s_of = (pp[:, None] // 16)[None, :, :] + 8 * np.arange(2)[:, None, None]
    j_of = (np.arange(256) // 16)[None, None, :]
    mk = np.zeros((128, 2, 2, 256), f)
    mk[:, 0] = (j_of >= s_of).transpose(1, 0, 2)
    mk[:, 1] = (s_of >= j_of).transpose(1, 0, 2)
    m["s5mask"] = mk
    m["s5d"] = inp["s5_d"].reshape(DEPTH, 2, 128).transpose(0, 2, 1)
    m["s5bg"] = inp["s5_b_glu"].reshape(DEPTH, 2, 128).transpose(0, 2, 1)
    m["s5wg"] = inp["s5_w_glu"]
    rep = lambda a: np.concatenate([a, a], axis=2)
    m["s5lre"] = rep(inp["s5_lam_re"].transpose(0, 1, 3, 2))
    m["s5lim"] = rep(inp["s5_lam_im"].transpose(0, 1, 3, 2))
    m["s5ldt"] = np.broadcast_to(inp["s5_log_dt"][:, :, None, :], (DEPTH, 2, 128, 16))
    bre = inp["s5_b_re"].transpose(0, 1, 3, 2, 4)
    bim = inp["s5_b_im"].transpose(0, 1, 3, 2, 4)
    m["s5R1"] = np.concatenate([bre, bim], axis=2)
    m["s5R2"] = np.concatenate([bim, bre], axis=2)
    cre = inp["s5_c_re"].transpose(0, 1, 4, 2, 3)
    cim = inp["s5_c_im"].transpose(0, 1, 4, 2, 3)
    m["s5C1"] = np.concatenate([cre, cim], axis=2)
    m["s5C2"] = np.concatenate([cim, cre], axis=2)
    return {k: np.ascontiguousarray(v, dtype=f) for k, v in m.items()}


def kernel(**inputs):
    inp = {k: np.asarray(v) for k, v in inputs.items()}
    kb = K()
    nc = kb.build()
    shared = prep_shared(inp)
    in_maps = []
    for b in range(NCORES):
        m = dict(shared)
        m.update(prep_inputs(inp, b))
        in_maps.append(m)
    res = run_bass_kernel_spmd(nc, in_maps, core_ids=list(range(NCORES)))
    out = np.stack([np.ascontiguousarray(r["outT"].T) for r in res.results], axis=0)
    return out.astype(np.float32, copy=False)
```

```python
from contextlib import ExitStack
import numpy as np
import concourse.bass as bass
import concourse.mybir as mybir
from concourse.bass_utils import run_bass_kernel_spmd

F32 = mybir.dt.float32
BF16 = mybir.dt.bfloat16
AF = mybir.ActivationFunctionType
ALU = mybir.AluOpType

D = 1024
SEQ = 4096
CTX = 256
T = SEQ + CTX
DEPTH = 2
PROJ = 5408
FH = 2816
EPS = 1e-6
NCORES = 8


class Buf:
    __slots__ = ("t", "last_w", "readers", "name")

    def __init__(self, t, name=""):
        self.t = t
        self.last_w = None
        self.readers = []
        self.name = name

    def __getitem__(self, idx):
        return self.t[idx]


class Sched:
    ENG = ("pe", "act", "dve", "pool", "sp")

    def __init__(self, nc, stack, same_engine_sync=("act", "dve", "pool", "sp")):
        self.nc = nc
        self.stack = stack
        self.eng = {"pe": nc.tensor, "act": nc.scalar, "dve": nc.vector, "pool": nc.gpsimd, "sp": nc.sync}
        self.sem = {}
        self.cnt = {}
        for e in ("pe", "act", "dve", "pool"):
            self.sem[e] = stack.enter_context(nc.semaphore("s_" + e))
            self.cnt[e] = 0
        self.KD = 28
        self.dma_rr = {"sp": 0, "pool": 0}
        for q in ("sp", "pool"):
            for i in range(self.KD):
                k = f"dma_{q}_{i}"
                self.sem[k] = stack.enter_context(nc.semaphore("s_" + k))
                self.cnt[k] = 0
        self.waited = {e: {k: 0 for k in self.sem} for e in self.ENG}
        self.same = same_engine_sync
        self.n_inst = 0
        self.n_wait = 0

    def sbuf(self, name, shape, dtype):
        self.uid = getattr(self, "uid", 0) + 1
        name = f"sb{self.uid}_{name}"
        t = self.stack.enter_context(self.nc.sbuf_tensor(name, list(shape), dtype))
        return Buf(t, name)

    def psum(self, name, shape, dtype=F32):
        t = self.stack.enter_context(self.nc.psum_tensor(name, list(shape), dtype))
        return Buf(t, name)

    def dram(self, name, shape, dtype, kind="Internal"):
        t = self.nc.dram_tensor(name, list(shape), dtype, kind=kind)
        return Buf(t.ap(), name)

    def _deps(self, reads, writes):
        deps = []
        for b in reads:
            if b.last_w is not None:
                deps.append(b.last_w)
        for b in writes:
            if b.last_w is not None:
                deps.append(b.last_w)
            deps.extend(b.readers)
        return deps

    def _wait(self, e, deps):
        need = {}
        for (k, v) in deps:
            if k == e and (e not in self.same):
                continue
            if self.waited[e][k] < v and need.get(k, 0) < v:
                need[k] = v
        for k, v in need.items():
            self.eng[e].wait_ge(self.sem[k], v)
            self.waited[e][k] = v
            self.n_wait += 1

    def _mark(self, tok, reads, writes):
        for b in writes:
            b.last_w = tok
            b.readers = []
        for b in reads:
            if b.last_w is not tok:
                b.readers.append(tok)
                if len(b.readers) > 64:
                    best = {}
                    for (k, v) in b.readers:
                        if best.get(k, 0) < v:
                            best[k] = v
                    b.readers = list(best.items())

    def op(self, e, fn, reads=(), writes=()):
        self._wait(e, self._deps(reads, writes))
        inst = fn(self.eng[e])
        inst.then_inc(self.sem[e], 1)
        self.cnt[e] += 1
        self.n_inst += 1
        tok = (e, self.cnt[e])
        self._mark(tok, reads, writes)
        return tok

    def dma(self, q, out, in_, reads=(), writes=(), **kw):
        k = f"dma_{q}_{self.dma_rr[q] % self.KD}"
        self.dma_rr[q] += 1
        deps = self._deps(reads, writes)
        if self.cnt[k] > 0:
            deps.append((k, self.cnt[k]))
        self._wait(q, deps)
        inst = self.eng[q].dma_start(out=out, in_=in_, **kw)
        inst.then_inc(self.sem[k], 16)
        self.cnt[k] += 16
        self.n_inst += 1
        tok = (k, self.cnt[k])
        self._mark(tok, reads, writes)
        return tok

    def barrier(self):
        deps = [(k, self.cnt[k]) for k in self.sem if self.cnt[k] > 0]
        for e in self.ENG:
            self._wait(e, deps)

    def finish(self):
        deps = [(k, self.cnt[k]) for k in self.sem if self.cnt[k] > 0]
        self._wait("sp", deps)


TILES_A = [(0, 256)] + [(256 + 512 * i, 512) for i in range(8)]


class K:
    def __init__(self, debug=(), dbg_in=()):
        self.debug = set(debug)
        self.dbg_in = set(dbg_in)
        self.nc = bass.Bass("TRN2", target_bir_lowering=False)
        self.stack = ExitStack()
        self.s = None

    def din(self, name, shape, dtype=F32):
        return Buf(self.nc.dram_tensor(name, list(shape), dtype, kind="ExternalInput").ap(), name)

    def dscr(self, name, shape, dtype):
        kind = "ExternalOutput" if name in self.debug else ("ExternalInput" if name in self.dbg_in else "Internal")
        return Buf(self.nc.dram_tensor(name, list(shape), dtype, kind=kind).ap(), name)

    def build(self, phases=("P0", "A", "B", "C", "D"), layers=(0, 1)):
        nc = self.nc
        with self.stack as st:
            s = self.s = Sched(nc, st)
            I = self.I = {}
            I["hT"] = self.din("hT", [D, T])
            I["cc"] = self.din("cc", [128, 8, 2])
            I["ln1"] = self.din("ln1", [DEPTH, 128, 8])
            I["ln2"] = self.din("ln2", [DEPTH, 128, 8])
            I["lnf"] = self.din("lnf", [128, 8])
            I["ada_w"] = self.din("ada_w", [DEPTH, D, 6 * D])
            I["ada_b"] = self.din("ada_b", [DEPTH, 128, 48])
            I["w_in"] = self.din("w_in", [DEPTH, D, PROJ])
            I["w_br"] = self.din("w_br", [DEPTH, D, D])
            I["w_out"] = self.din("w_out", [DEPTH, D, D])
            I["w_up"] = self.din("w_up", [DEPTH, D, 2 * FH])
            I["w_down"] = self.din("w_down", [DEPTH, FH, D])
            I["fcw"] = self.din("fcw", [DEPTH, 128, 44, 3])
            I["fcb"] = self.din("fcb", [DEPTH, 128, 44])
            I["lcw"] = self.din("lcw", [DEPTH, 128, 2, 4])
            I["lcb"] = self.din("lcb", [DEPTH, 128, 2])
            I["lba"] = self.din("lba", [DEPTH, 128, 2, 2])
            I["lbi"] = self.din("lbi", [DEPTH, 128, 2, 2])
            I["llam"] = self.din("llam", [DEPTH, 128, 2, 2])
            I["lwa"] = self.din("lwa", [DEPTH, 128, 4, 128])
            I["lwi"] = self.din("lwi", [DEPTH, 128, 4, 128])
            I["gconst"] = self.din("gconst", [128, 2, 3, 256])
            I["ident"] = self.din("ident", [128, 128])
            I["s5mask"] = self.din("s5mask", [128, 2, 2, 256])
            I["s5d"] = self.din("s5d", [DEPTH, 128, 2])
            I["s5bg"] = self.din("s5bg", [DEPTH, 128, 2])
            I["s5wg"] = self.din("s5wg", [DEPTH, 256, 256])
            for nm in ("s5lre", "s5lim", "s5ldt"):
                I[nm] = self.din(nm, [DEPTH, 2, 128, 16])
            for nm in ("s5R1", "s5R2", "s5C1", "s5C2"):
                I[nm] = self.din(nm, [DEPTH, 2, 128, 16, 16])
            I["gw2p"] = self.din("gw2p", [DEPTH, 33, 2, 256])
            I["gng"] = self.din("gng", [DEPTH, 128, 1])
            self.out = Buf(nc.dram_tensor("outT", [D, SEQ], F32, kind="ExternalOutput").ap(), "outT")
            Z = self.Z = {}
            Z["h"] = self.dscr("h", [D, T], F32)
            Z["s5u"] = self.dscr("Zs5u", [256, T], BF16)
            Z["lx"] = self.dscr("Zlx", [256, T], F32)
            Z["lg"] = self.dscr("Zlg", [256, T], F32)
            Z["q"] = self.dscr("Zq", [256, T], F32)
            Z["k"] = self.dscr("Zk", [256, T], F32)
            Z["ktok"] = self.dscr("Zktok", [T, 256], F32)
            Z["vtok"] = self.dscr("Zvtok", [T, 512], BF16)
            Z["g"] = self.dscr("Zg", [512, T], BF16)
            Z["r"] = self.dscr("Zr", [32, T], F32)
            Z["m"] = self.dscr("Zm", [3072, T], BF16)
            Z["mod"] = self.dscr("Zmod", [DEPTH, 128, 96], F32)
            Z["y"] = self.dscr("Zy", [D, T], BF16)
            Z["u2"] = self.dscr("Zu2", [D, T], BF16)
            Z["mm"] = self.dscr("Zmm", [D, T], BF16)
            Z["hid"] = self.dscr("Zhid", [FH, T], BF16)
            Z["us"] = self.dscr("Zus", [16, 16, 16, T // 16], BF16)
            for nm_ in ("gq", "gk", "gs"):
                Z[nm_] = self.dscr("Z" + nm_, [2, T // 128, 128, 512], BF16)
            Z["ys"] = self.dscr("Zys", [16, 16, 16, T // 16], F32)
            self.ps = [s.psum(f"ps{i}", [128, 512]) for i in range(8)]
            self.ps_rr = 0
            self.ones_bf = s.sbuf("ones_bf", [128, 128], BF16)
            s.op("dve", lambda e: e.memset(self.ones_bf[:], 1.0), writes=[self.ones_bf])
            self.eps_t = s.sbuf("eps_t", [128, 1], F32)
            s.op("dve", lambda e: e.memset(self.eps_t[:], EPS), writes=[self.eps_t])
            self.mod = [s.sbuf(f"mod{l}", [128, 48, 2], F32) for l in range(DEPTH)]
            self.coef = [s.sbuf(f"coef{l}", [128, 4, 8, 2], F32) for l in range(DEPTH)]
            if "P0" in phases:
                self.phase_p0()
            for l in layers:
                if "A" in phases:
                    self.phase_a(l, Z["h"] if l > 0 else I["hT"])
                if "B" in phases or "LRU" in phases:
                    self.phase_lru(l)
                if "B" in phases or "S5" in phases:
                    self.phase_s5(l)
                if "B" in phases or "GLA" in phases:
                    self.phase_gla(l)
                if "C" in phases or "C1" in phases:
                    self.phase_c1(l, Z["h"] if l > 0 else I["hT"])
                if "C" in phases or "C2" in phases:
                    self.phase_c2(l, final=(l == DEPTH - 1))
            s.finish()
            print("instructions", s.n_inst, "waits", s.n_wait)
        return nc

    def next_ps(self):
        pool = getattr(self, "ps_pool", None) or self.ps[0:6]
        p = pool[self.ps_rr % len(pool)]
        self.ps_rr += 1
        return p

    def phase_p0(self):
        s, I = self.s, self.I
        with ExitStack() as st:
            s_old = s.stack
            s.stack = st
            s.barrier()
            ccf = s.sbuf("ccf", [128, 8, 2], F32)
            scb = s.sbuf("scb", [128, 8, 2], BF16)
            awb = [s.sbuf(f"awb{i}", [128, 8, 1024], BF16) for i in range(2)]
            adab = s.sbuf("adab", [128, 48], F32)
            g1 = s.sbuf("g1", [128, 8], F32)
            g2 = s.sbuf("g2", [128, 8], F32)
            s.dma("sp", ccf[:], I["cc"][:, :, :], reads=[I["cc"]], writes=[ccf])
            s.op("act", lambda e: e.activation(out=scb[:], in_=ccf[:], func=AF.Silu), reads=[ccf], writes=[scb])
            pm = self.ps[7]
            for l in range(DEPTH):
                s.dma("sp", adab[:], I["ada_b"][l], reads=[I["ada_b"]], writes=[adab])
                s.dma("sp", g1[:], I["ln1"][l], reads=[I["ln1"]], writes=[g1])
                s.dma("sp", g2[:], I["ln2"][l], reads=[I["ln2"]], writes=[g2])
                for j in range(6):
                    aw = awb[j % 2]
                    src = I["ada_w"][l, :, j * 1024:(j + 1) * 1024].rearrange("(kc p) n -> p kc n", p=128)
                    s.dma("pool", aw[:, :, :], src[:, :, :], reads=[I["ada_w"]], writes=[aw])
                    for fc in range(8):
                        col = (j * 8 + fc) * 2
                        for kc in range(8):
                            s.op("pe", lambda e: e.matmul(pm[:, col:col + 2], lhsT=aw[:, kc, fc * 128:(fc + 1) * 128],
                                                          rhs=scb[:, kc, :], start=(kc == 0), stop=(kc == 7)),
                                 reads=[aw, scb], writes=[pm])
                mod = self.mod[l]
                s.op("dve", lambda e: e.tensor_tensor(out=mod[:], in0=pm[:, 0:96].rearrange("p (a b) -> p a b", b=2),
                                                      in1=adab[:].unsqueeze(2).to_broadcast([128, 48, 2]), op=ALU.add),
                     reads=[pm, adab], writes=[mod])
                cf = self.coef[l]
                for (dst, g, jm) in ((0, g1, 1), (2, g2, 4)):
                    s.op("dve", lambda e: e.scalar_tensor_tensor(out=cf[:, dst], in0=mod[:, jm * 8:(jm + 1) * 8, :], scalar=1.0,
                                                                 in1=g[:].unsqueeze(2).to_broadcast([128, 8, 2]),
                                                                 op0=ALU.add, op1=ALU.mult),
                         reads=[mod, g], writes=[cf])
                for (dst, jm) in ((1, 0), (3, 3)):
                    s.op("dve", lambda e: e.tensor_copy(out=cf[:, dst], in_=mod[:, jm * 8:(jm + 1) * 8, :]), reads=[mod], writes=[cf])
                if "Zmod" in self.debug:
                    s.dma("sp", self.Z["mod"][l], mod[:].rearrange("p a b -> p (a b)"), reads=[mod], writes=[self.Z["mod"]])
            s.stack = s_old

    def norm_mod(self, hf, sq, u, rstd, n, l, which, is_ctx):
        s = self.s
        cf = self.coef[l]
        col = 1 if is_ctx else 0
        pm = self.ps[6]
        s.op("act", lambda e: e.activation(out=sq[:, :, :n], in_=hf[:, :, :n], func=AF.Square), reads=[hf], writes=[sq])
        for kc in range(8):
            s.op("pe", lambda e: e.matmul(pm[:, :n], lhsT=self.ones_bf[:], rhs=sq[:, kc, :n], start=(kc == 0), stop=(kc == 7)),
                 reads=[sq, self.ones_bf], writes=[pm])
        s.op("act", lambda e: e.activation(out=rstd[:, :n], in_=pm[:, :n], func=AF.Sqrt, bias=self.eps_t[:, 0:1], scale=1.0 / D),
             reads=[pm, self.eps_t], writes=[rstd])
        s.op("dve", lambda e: e.reciprocal(out=rstd[:, :n], in_=rstd[:, :n]), reads=[rstd], writes=[rstd])
        for kc in range(8):
            s.op("dve", lambda e: e.tensor_tensor(out=hf[:, kc, :n], in0=hf[:, kc, :n], in1=rstd[:, :n], op=ALU.mult),
                 reads=[hf, rstd], writes=[hf])
        for kc in range(8):
            s.op("act", lambda e: e.activation(out=u[:, kc, :n], in_=hf[:, kc, :n], func=AF.Identity,
                                               scale=cf[:, 2 * which, kc, col:col + 1], bias=cf[:, 2 * which + 1, kc, col:col + 1]),
                 reads=[hf, cf], writes=[u])

    def norm_mod_gen(self, hf, sq, u, rstd, n, l, which, is_ctx, mul_eng="dve"):
        s = self.s
        cf = self.coef[l]
        col = 1 if is_ctx else 0
        pm = self.ps[6]
        s.op("act", lambda e: e.activation(out=sq[:, :, :n], in_=hf[:, :, :n], func=AF.Square), reads=[hf], writes=[sq])
        yield
        for kc in range(8):
            s.op("pe", lambda e: e.matmul(pm[:, :n], lhsT=self.ones_bf[:], rhs=sq[:, kc, :n], start=(kc == 0), stop=(kc == 7)),
                 reads=[sq, self.ones_bf], writes=[pm])
            yield
        s.op("act", lambda e: e.activation(out=rstd[:, :n], in_=pm[:, :n], func=AF.Sqrt, bias=self.eps_t[:, 0:1], scale=1.0 / D),
             reads=[pm, self.eps_t], writes=[rstd])
        yield
        s.op("dve", lambda e: e.reciprocal(out=rstd[:, :n], in_=rstd[:, :n]), reads=[rstd], writes=[rstd])
        yield
        for kc in range(8):
            s.op(mul_eng, lambda e: e.tensor_tensor(out=hf[:, kc, :n], in0=hf[:, kc, :n], in1=rstd[:, :n], op=ALU.mult),
                 reads=[hf, rstd], writes=[hf])
            yield
        for kc in range(8):
            s.op("act", lambda e: e.activation(out=u[:, kc, :n], in_=hf[:, kc, :n], func=AF.Identity,
                                               scale=cf[:, 2 * which, kc, col:col + 1], bias=cf[:, 2 * which + 1, kc, col:col + 1]),
                 reads=[hf, cf], writes=[u])
            yield

    def phase_a(self, l, hsrc):
        s, I, Z = self.s, self.I, self.Z
        with ExitStack() as st:
            s_old = s.stack
            s.stack = st
            s.barrier()
            w = s.sbuf("w_in_sb", [128, 8, PROJ], BF16)
            wsrc = I["w_in"][l].rearrange("(kc p) n -> p kc n", p=128)
            wbk = [Buf(w.t, f"w_in_blk{i}") for i in range(4)]
            for bi, c0 in enumerate(range(0, PROJ, 1352)):
                s.dma("pool", w[:, :, c0:c0 + 1352], wsrc[:, :, c0:c0 + 1352], reads=[I["w_in"]], writes=[wbk[bi]])
            wblk = lambda lo, hi: [wbk[j] for j in range(lo // 1352, (hi - 1) // 1352 + 1)]
            hf = [s.sbuf(f"hfA{i}", [128, 8, 512], F32) for i in range(2)]
            sqs = [s.sbuf(f"sqA{i}", [128, 8, 512], BF16) for i in range(2)]
            us = [s.sbuf(f"uA{i}", [128, 8, 512], BF16) for i in range(2)]
            rstds = [s.sbuf(f"rstdA{i}", [128, 512], F32) for i in range(2)]
            stg_f = [s.sbuf(f"stgf{i}", [128, 512], F32) for i in range(4)]
            stg_b = [s.sbuf(f"stgb{i}", [128, 512], BF16) for i in range(4)]
            rr = {"f": 0, "b": 0, "e": 0}
            fm = [(0, 256, Z["s5u"], "b", AF.Copy), (256, 256, Z["lx"], "f", AF.Copy), (512, 256, Z["lg"], "f", AF.Copy),
                  (768, 256, Z["q"], "f", AF.Copy), (1024, 256, Z["k"], "f", AF.Copy), (1792, 512, Z["g"], "b", AF.Silu),
                  (2304, 32, Z["r"], "f", AF.Copy), (2336, 3072, Z["m"], "b", AF.Sigmoid)]
            hview = hsrc.t.rearrange("(kc p) t -> p kc t", p=128)

            def load(i):
                t0, n = TILES_A[i]
                s.dma("sp", hf[i % 2][:, :, :n], hview[:, :, t0:t0 + n], reads=[hsrc], writes=[hf[i % 2]])
            load(0)
            load(1)
            self.norm_mod(hf[0], sqs[0], us[0], rstds[0], TILES_A[0][1], l, 0, True)
            for ti, (t0, n) in enumerate(TILES_A):
                u = us[ti % 2]
                ng = None
                if ti + 1 < len(TILES_A):
                    nb = (ti + 1) % 2
                    ng = self.norm_mod_gen(hf[nb], sqs[nb], us[nb], rstds[nb], TILES_A[ti + 1][1], l, 0, False)
                n_mt = 0
                for (c0, ncols, dst, dt, func) in fm:
                    for m0 in range(0, ncols, 128):
                        mm = min(128, ncols - m0)
                        ps = self.next_ps()
                        for kc in range(8):
                            s.op("pe", lambda e: e.matmul(ps[:mm, :n], lhsT=w[:, kc, c0 + m0:c0 + m0 + mm], rhs=u[:, kc, :n],
                                                          start=(kc == 0), stop=(kc == 7)), reads=wblk(c0 + m0, c0 + m0 + mm) + [u], writes=[ps])
                        if dt == "f":
                            sg = stg_f[rr["f"] % 4]; rr["f"] += 1
                        else:
                            sg = stg_b[rr["b"] % 4]; rr["b"] += 1
                        if func == AF.Copy and rr["e"] % 2 == 0:
                            s.op("dve", lambda e: e.tensor_copy(out=sg[:mm, :n], in_=ps[:mm, :n]), reads=[ps], writes=[sg])
                        else:
                            s.op("act", lambda e: e.activation(out=sg[:mm, :n], in_=ps[:mm, :n], func=func), reads=[ps], writes=[sg])
                        rr["e"] += 1
                        s.dma("sp", dst[m0:m0 + mm, t0:t0 + n], sg[:mm, :n], reads=[sg], writes=[dst])
                        n_mt += 1
                        if ng is not None and n_mt >= 4:
                            next(ng, None)
                for sub in range(0, n, 128):
                    ps = self.next_ps()
                    for kc in range(8):
                        s.op("pe", lambda e: e.matmul(ps[:, 0:256], lhsT=u[:, kc, sub:sub + 128], rhs=w[:, kc, 1024:1280],
                                                      start=(kc == 0), stop=(kc == 7)), reads=wblk(1024, 1280) + [u], writes=[ps])
                    sg = stg_f[rr["f"] % 4]; rr["f"] += 1
                    s.op("dve", lambda e: e.tensor_copy(out=sg[:, 0:256], in_=ps[:, 0:256]), reads=[ps], writes=[sg])
                    s.dma("sp", Z["ktok"][t0 + sub:t0 + sub + 128, :], sg[:, 0:256], reads=[sg], writes=[Z["ktok"]])
                    ps = self.next_ps()
                    for kc in range(8):
                        s.op("pe", lambda e: e.matmul(ps[:, 0:512], lhsT=u[:, kc, sub:sub + 128], rhs=w[:, kc, 1280:1792],
                                                      start=(kc == 0), stop=(kc == 7)), reads=wblk(1280, 1792) + [u], writes=[ps])
                    sg = stg_b[rr["b"] % 4]; rr["b"] += 1
                    s.op("act", lambda e: e.activation(out=sg[:, 0:512], in_=ps[:, 0:512], func=AF.Copy), reads=[ps], writes=[sg])
                    s.dma("sp", Z["vtok"][t0 + sub:t0 + sub + 128, :], sg[:, 0:512], reads=[sg], writes=[Z["vtok"]])
                if ng is not None:
                    for _ in ng:
                        pass
                if ti + 2 < len(TILES_A):
                    load(ti + 2)
            s.stack = s_old


    @staticmethod
    def round_robin(gens):
        gens = list(gens)
        while gens:
            for g_ in list(gens):
                try:
                    next(g_)
                except StopIteration:
                    gens.remove(g_)

    def gelu_tanh(self, eng_tt, x, tmp, out, n):
        s = self.s
        s.op("act", lambda e: e.activation(out=tmp[:, :n], in_=x[:, :n], func=AF.Square), reads=[x], writes=[tmp])
        s.op("dve", lambda e: e.tensor_scalar(out=tmp[:, :n], in0=tmp[:, :n], scalar1=0.044715, scalar2=1.0, op0=ALU.mult, op1=ALU.add),
             reads=[tmp], writes=[tmp])
        s.op(eng_tt, lambda e: e.tensor_tensor(out=tmp[:, :n], in0=tmp[:, :n], in1=x[:, :n], op=ALU.mult), reads=[tmp, x], writes=[tmp])
        s.op("act", lambda e: e.activation(out=tmp[:, :n], in_=tmp[:, :n], func=AF.Sigmoid, scale=1.5957691216), reads=[tmp], writes=[tmp])
        s.op(eng_tt, lambda e: e.tensor_tensor(out=out[:, :n], in0=tmp[:, :n], in1=x[:, :n], op=ALU.mult), reads=[tmp, x], writes=[out])

    def phase_lru(self, l):
        s, I, Z = self.s, self.I, self.Z
        NP = 4360
        R0, R1 = 1, 4356
        with ExitStack() as st:
            s_old = s.stack
            s.stack = st
            s.barrier()
            cwt = s.sbuf("lcw", [128, 2, 4], F32)
            cbt = s.sbuf("lcb", [128, 2], F32)
            bat = s.sbuf("lba", [128, 2, 2], F32)
            bit = s.sbuf("lbi", [128, 2, 2], F32)
            lam = s.sbuf("llam", [128, 2, 2], F32)
            wab = s.sbuf("lwa", [128, 4, 128], BF16)
            wib = s.sbuf("lwi", [128, 4, 128], BF16)
            s.dma("sp", cwt[:], I["lcw"][l], reads=[I["lcw"]], writes=[cwt])
            s.dma("sp", cbt[:], I["lcb"][l], reads=[I["lcb"]], writes=[cbt])
            s.dma("sp", bat[:], I["lba"][l], reads=[I["lba"]], writes=[bat])
            s.dma("sp", bit[:], I["lbi"][l], reads=[I["lbi"]], writes=[bit])
            s.dma("sp", lam[:], I["llam"][l], reads=[I["llam"]], writes=[lam])
            s.dma("pool", wab[:], I["lwa"][l], reads=[I["lwa"]], writes=[wab])
            s.dma("pool", wib[:], I["lwi"][l], reads=[I["lwi"]], writes=[wib])
            yv = s.sbuf("l_y", [128, 4], F32)
            ser = s.sbuf("l_ser", [128, 4], F32)
            lnv = s.sbuf("l_ln", [128, 4], F32)
            msk = s.sbuf("l_msk", [128, 4], F32)
            n8 = s.sbuf("l_n8", [128, 2, 2], F32)
            n16 = s.sbuf("l_n16", [128, 2, 2], F32)
            lam2 = lam[:].rearrange("p a b -> p (a b)")
            s.op("act", lambda e: e.activation(out=yv[:], in_=lam2, func=AF.Exp, scale=-1.0), reads=[lam], writes=[yv])
            s.op("act", lambda e: e.activation(out=lnv[:], in_=yv[:], func=AF.Ln, bias=1.0), reads=[yv], writes=[lnv])
            s.op("dve", lambda e: e.tensor_scalar(out=ser[:], in0=yv[:], scalar1=-0.25, scalar2=1.0 / 3.0, op0=ALU.mult, op1=ALU.add), reads=[yv], writes=[ser])
            s.op("dve", lambda e: e.tensor_tensor(out=ser[:], in0=ser[:], in1=yv[:], op=ALU.mult), reads=[ser, yv], writes=[ser])
            s.op("dve", lambda e: e.tensor_scalar(out=ser[:], in0=ser[:], scalar1=-1.0, scalar2=0.5, op0=ALU.mult, op1=ALU.add), reads=[ser], writes=[ser])
            s.op("dve", lambda e: e.tensor_tensor(out=ser[:], in0=ser[:], in1=yv[:], op=ALU.mult), reads=[ser, yv], writes=[ser])
            s.op("dve", lambda e: e.tensor_scalar(out=ser[:], in0=ser[:], scalar1=-1.0, scalar2=1.0, op0=ALU.mult, op1=ALU.add), reads=[ser], writes=[ser])
            s.op("dve", lambda e: e.tensor_tensor(out=ser[:], in0=ser[:], in1=yv[:], op=ALU.mult), reads=[ser, yv], writes=[ser])
            s.op("dve", lambda e: e.tensor_scalar(out=msk[:], in0=yv[:], scalar1=0.03, scalar2=None, op0=ALU.is_lt), reads=[yv], writes=[msk])
            s.op("dve", lambda e: e.tensor_tensor(out=ser[:], in0=ser[:], in1=lnv[:], op=ALU.subtract), reads=[ser, lnv], writes=[ser])
            s.op("dve", lambda e: e.tensor_tensor(out=ser[:], in0=ser[:], in1=msk[:], op=ALU.mult), reads=[ser, msk], writes=[ser])
            s.op("dve", lambda e: e.tensor_tensor(out=ser[:], in0=ser[:], in1=lnv[:], op=ALU.add), reads=[ser, lnv], writes=[ser])
            s.op("dve", lambda e: e.tensor_scalar(out=n8[:].rearrange("p a b -> p (a b)"), in0=ser[:], scalar1=-8.0, scalar2=None, op0=ALU.mult), reads=[ser], writes=[n8])
            s.op("dve", lambda e: e.tensor_scalar(out=n16[:].rearrange("p a b -> p (a b)"), in0=ser[:], scalar1=-16.0, scalar2=None, op0=ALU.mult), reads=[ser], writes=[n16])

            xraw = s.sbuf("l_xraw", [128, SEQ], F32)
            xp = s.sbuf("l_xp", [128, NP], F32)
            xc = s.sbuf("l_xc", [128, NP], F32)
            xcb = s.sbuf("l_xcb", [128, NP], BF16)
            avs = [s.sbuf(f"l_a{i}", [128, NP], F32) for i in range(2)]
            bvs = [s.sbuf(f"l_b{i}", [128, NP], F32) for i in range(2)]
            av, bv = avs[0], bvs[0]
            hv = [s.sbuf(f"l_h{i}", [128, NP], F32) for i in range(2)]
            gl = s.sbuf("l_gl", [128, T], F32)
            ybf = s.sbuf("l_ybf", [128, T], BF16)
            it = [s.sbuf(f"l_it{i}", [128, 512], F32) for i in range(2)]
            pieces = [(a, min(a + 512, R1)) for a in range(R0, R1, 512)]
            for ct in range(2):
                rows = slice(ct * 128, (ct + 1) * 128)
                s.op("pool", lambda e: e.memset(xp[:], 0.0), writes=[xp])
                s.dma("sp", xp[:, 1:257], Z["lx"][rows, 0:256], reads=[Z["lx"]], writes=[xp])
                s.dma("sp", xraw[:], Z["lx"][rows, 256:T], reads=[Z["lx"]], writes=[xraw])
                s.dma("sp", gl[:], Z["lg"][rows, :], reads=[Z["lg"]], writes=[gl])
                s.op("act", lambda e: e.activation(out=xp[:, 260:4356].rearrange("p (c r) -> p r c", r=64),
                                                   in_=xraw[:].rearrange("p (r c) -> p r c", c=64), func=AF.Copy),
                     reads=[xraw], writes=[xp])
                n = R1 - R0
                s.op("act", lambda e: e.activation(out=xc[:, R0:R1], in_=xp[:, R0:R1], func=AF.Identity,
                                                   scale=cwt[:, ct, 1:2], bias=cbt[:, ct:ct + 1]), reads=[xp, cwt, cbt], writes=[xc])
                for (k, off) in ((0, -1), (2, 1), (3, 2)):
                    s.op("dve", lambda e: e.scalar_tensor_tensor(out=xc[:, R0:R1], in0=xp[:, R0 + off:R1 + off], scalar=cwt[:, ct, k:k + 1],
                                                                 in1=xc[:, R0:R1], op0=ALU.mult, op1=ALU.add),
                         reads=[xp, cwt, xc], writes=[xc])
                s.op("act", lambda e: e.activation(out=xcb[:, R0:R1], in_=xc[:, R0:R1], func=AF.Copy), reads=[xc], writes=[xcb])
                for (xs, tmp_b, ts) in ((slice(0, 256), it[0], slice(0, 256)), (slice(256, T), xraw, slice(0, SEQ))):
                    s.op("act", lambda e: e.activation(out=tmp_b[:, ts], in_=gl[:, xs], func=AF.Square), reads=[gl], writes=[tmp_b])
                    s.op("dve", lambda e: e.tensor_scalar(out=tmp_b[:, ts], in0=tmp_b[:, ts], scalar1=0.044715, scalar2=1.0, op0=ALU.mult, op1=ALU.add),
                         reads=[tmp_b], writes=[tmp_b])
                    s.op("pool", lambda e: e.tensor_tensor(out=tmp_b[:, ts], in0=tmp_b[:, ts], in1=gl[:, xs], op=ALU.mult), reads=[tmp_b, gl], writes=[tmp_b])
                    s.op("act", lambda e: e.activation(out=tmp_b[:, ts], in_=tmp_b[:, ts], func=AF.Sigmoid, scale=1.5957691216), reads=[tmp_b], writes=[tmp_b])
                    s.op("pool", lambda e: e.tensor_tensor(out=gl[:, xs], in0=tmp_b[:, ts], in1=gl[:, xs], op=ALU.mult), reads=[tmp_b, gl], writes=[gl])
                for d in range(2):
                    av, bv = avs[d], bvs[d]
                    for pi, (a0, a1) in enumerate(pieces):
                        m = a1 - a0
                        i_t = it[pi % 2]
                        pr, pq = self.next_ps(), self.next_ps()
                        s.op("pe", lambda e: e.matmul(pr[:, :m], lhsT=wab[:, d * 2 + ct, :], rhs=xcb[:, a0:a1], start=True, stop=True),
                             reads=[wab, xcb], writes=[pr])
                        s.op("pe", lambda e: e.matmul(pq[:, :m], lhsT=wib[:, d * 2 + ct, :], rhs=xcb[:, a0:a1], start=True, stop=True),
                             reads=[wib, xcb], writes=[pq])
                        s.op("act", lambda e: e.activation(out=av[:, a0:a1], in_=pr[:, :m], func=AF.Sigmoid, bias=bat[:, d, ct:ct + 1]),
                             reads=[pr, bat], writes=[av])
                        s.op("act", lambda e: e.activation(out=i_t[:, :m], in_=pq[:, :m], func=AF.Sigmoid, bias=bit[:, d, ct:ct + 1]),
                             reads=[pq, bit], writes=[i_t])
                        s.op("dve", lambda e: e.tensor_tensor(out=bv[:, a0:a1], in0=i_t[:, :m], in1=xc[:, a0:a1], op=ALU.mult),
                             reads=[i_t, xc], writes=[bv])
                    s.op("act", lambda e: e.activation(out=hv[d][:, R0:R1], in_=av[:, R0:R1], func=AF.Exp, scale=n16[:, d, ct:ct + 1]),
                         reads=[av, n16], writes=[hv[d]])
                    s.op("act", lambda e: e.activation(out=av[:, R0:R1], in_=av[:, R0:R1], func=AF.Exp, scale=n8[:, d, ct:ct + 1]),
                         reads=[av, n8], writes=[av])
                    s.op("act", lambda e: e.activation(out=hv[d][:, R0:R1], in_=hv[d][:, R0:R1], func=AF.Sqrt, scale=-1.0, bias=1.0),
                         reads=[hv[d]], writes=[hv[d]])
                    s.op("pool", lambda e: e.tensor_tensor(out=bv[:, R0:R1], in0=bv[:, R0:R1], in1=hv[d][:, R0:R1], op=ALU.mult),
                         reads=[bv, hv[d]], writes=[bv])
                    h = hv[d]
                    if d == 0:
                        s.op("dve", lambda e: e.tensor_tensor_scan(out=h[:, 1:257], data0=av[:, 1:257], data1=bv[:, 1:257], initial=0.0,
                                                                   op0=ALU.mult, op1=ALU.add), reads=[av, bv], writes=[h])
                        s.op("dve", lambda e: e.tensor_tensor_scan(out=h[:, 260:4356], data0=av[:, 260:4356], data1=bv[:, 260:4356],
                                                                   initial=h[:, 256:257], op0=ALU.mult, op1=ALU.add), reads=[av, bv, h], writes=[h])
                    else:
                        s.op("dve", lambda e: e.tensor_tensor_scan(out=h[:, 256:0:-1], data0=av[:, 256:0:-1], data1=bv[:, 256:0:-1], initial=0.0,
                                                                   op0=ALU.mult, op1=ALU.add), reads=[av, bv], writes=[h])
                        s.op("dve", lambda e: e.tensor_tensor_scan(out=h[:, 4355:259:-1], data0=av[:, 4355:259:-1], data1=bv[:, 4355:259:-1],
                                                                   initial=h[:, 1:2], op0=ALU.mult, op1=ALU.add), reads=[av, bv, h], writes=[h])
                s.op("pool", lambda e: e.tensor_tensor(out=hv[0][:, R0:R1], in0=hv[0][:, R0:R1], in1=hv[1][:, R0:R1], op=ALU.add),
                     reads=[hv[0], hv[1]], writes=[hv[0]])
                bv = gl
                s.op("dve", lambda e: e.tensor_tensor(out=ybf[:, 0:256], in0=hv[0][:, 1:257], in1=bv[:, 0:256], op=ALU.mult),
                     reads=[hv[0], bv], writes=[ybf])
                s.op("dve", lambda e: e.tensor_tensor(out=ybf[:, 256:T].rearrange("p (r c) -> p r c", c=64),
                                                      in0=hv[0][:, 260:4356].rearrange("p (c r) -> p r c", r=64),
                                                      in1=bv[:, 256:T].rearrange("p (r c) -> p r c", c=64), op=ALU.mult),
                     reads=[hv[0], bv], writes=[ybf])
                s.dma("sp", Z["y"][256 + ct * 128:256 + (ct + 1) * 128, :], ybf[:], reads=[ybf], writes=[Z["y"]])
            s.stack = s_old

    def phase_gla(self, l):
        s, I, Z = self.s, self.I, self.Z
        NTL = T // 128
        NS = 4
        with ExitStack() as st:
            s_old = s.stack
            s.stack = st
            s.barrier()
            gc = s.sbuf("g_const", [128, 2, 3, 256], F32)
            s.dma("sp", gc[:], I["gconst"][:, :, :, :], reads=[I["gconst"]], writes=[gc])
            gcb = s.sbuf("g_constb", [128, 2, 3, 256], BF16)
            s.op("act", lambda e: e.activation(out=gcb[:].rearrange("p a b c -> p (a b c)"), in_=gc[:].rearrange("p a b c -> p (a b c)"), func=AF.Copy),
                 reads=[gc], writes=[gcb])
            w2p = s.sbuf("g_w2p", [33, 2, 256], BF16)
            s.dma("pool", w2p[:], I["gw2p"][l], reads=[I["gw2p"]], writes=[w2p])
            ng = s.sbuf("g_ng", [128, 1], F32)
            s.dma("sp", ng[:], I["gng"][l], reads=[I["gng"]], writes=[ng])
            lnq = s.sbuf("g_lnq", [128, 1], F32)
            s.op("dve", lambda e: e.memset(lnq[:], float(np.log(0.125))), writes=[lnq])
            zrfs = [s.sbuf(f"g_zrf{i}", [32, 1088], F32) for i in range(4)]
            zrb = s.sbuf("g_zrb", [33, T], BF16)
            s.op("dve", lambda e: e.memset(zrb[32:33, :], 1.0), writes=[zrb])
            for ai_, a0 in enumerate(range(0, T, 1088)):
                zrf = zrfs[ai_]
                s.dma("sp", zrf[:], Z["r"][:, a0:a0 + 1088], reads=[Z["r"]], writes=[zrf])
            for ai_, a0 in enumerate(range(0, T, 1088)):
                zrf = zrfs[ai_]
                s.op("act", lambda e: e.activation(out=zrb[0:32, a0:a0 + 1088], in_=zrf[:], func=AF.Copy), reads=[zrf], writes=[zrb])
            osum_t = s.sbuf("g_osum", [128, 4, T], F32)
            osum = osum_t
            osb = [Buf(osum_t.t, f"osum{i}") for i in range(NTL)]
            s.op("dve", lambda e: e.memset(osum_t[:], 0.0), writes=[osum_t] + osb)
            dec_all = [s.sbuf(f"g_decall{dd}", [128, NTL, 2, 2], F32) for dd in range(2)]
            GQ = [Z["gq"].t[dd] for dd in range(2)]
            GK = [Z["gk"].t[dd] for dd in range(2)]
            GS = [Z["gs"].t[dd] for dd in range(2)]
            GQb = [[Buf(None, f"gq{dd}_{i}") for i in range(NTL)] for dd in range(2)]
            GKb = [[Buf(None, f"gk{dd}_{i}") for i in range(NTL)] for dd in range(2)]
            GSb = [[Buf(None, f"gs{dd}_{i}") for i in range(NTL)] for dd in range(2)]
            DB = {}
            for dd in range(2):
                B_ = {}
                B_["qbm"] = [s.sbuf(f"g_qbm{dd}{i}", [128, 2, 2, 2, 64], BF16) for i in range(NS)]
                B_["kin"] = [s.sbuf(f"g_kin{dd}{i}", [128, 2, 128], BF16) for i in range(NS)]
                B_["ko2"] = [s.sbuf(f"g_ko2{dd}{i}", [128, 2, 256], BF16) for i in range(NS)]
                B_["scb"] = [s.sbuf(f"g_scb{dd}{i}", [128, 4, 128], BF16) for i in range(NS)]
                B_["vt"] = [s.sbuf(f"g_vt{dd}{i}", [128, 512], BF16) for i in range(NS)]
                B_["dec"] = [s.sbuf(f"g_dec{dd}{i}", [128, 2, 2], F32) for i in range(NS)]
                for i in range(NS):
                    for bb in (B_["qbm"][i], B_["ko2"][i], B_["scb"][i]):
                        s.op("dve", lambda e: e.memset(bb[:], 0.0), writes=[bb])
                B_["Sf"] = [s.sbuf(f"g_S{dd}{i}", [128, 128], F32) for i in range(2)]
                B_["Sb"] = [s.sbuf(f"g_Sb{dd}{i}", [128, 128], BF16) for i in range(2)]
                B_["qt"] = [s.sbuf(f"g_qt{dd}{i}", [128, 2, 128], F32) for i in range(2)]
                B_["kt"] = [s.sbuf(f"g_kt{dd}{i}", [128, 2, 128], F32) for i in range(2)]
                B_["ktk"] = [s.sbuf(f"g_ktk{dd}{i}", [128, 256], F32) for i in range(2)]
                for nm_ in ("tA", "tB", "tC"):
                    B_[nm_] = [s.sbuf(f"g_{nm_}{dd}{i}", [128, 256], F32) for i in range(2)]
                B_["Lt"] = [s.sbuf(f"g_Lt{dd}{i}", [128, 256], BF16) for i in range(2)]
                DB[dd] = B_
            qv = Z["q"].t.rearrange("(ct p) t -> p ct t", p=128)
            kv = Z["k"].t.rearrange("(ct p) t -> p ct t", p=128)

            def p1(d, i, sl, b2):
                B_ = DB[d]
                qbm, kin, ko2, scb = B_["qbm"], B_["kin"], B_["ko2"], B_["scb"]
                qt, kt, ktk, tA, Lt, tB, tC = B_["qt"], B_["kt"], B_["ktk"], B_["tA"], B_["Lt"], B_["tB"], B_["tC"]
                bkA, bkB = self.ps[2 * (2 * d + b2)], self.ps[2 * (2 * d + b2) + 1]
                Mtri = gcb[:, d, 0, 0:128]
                Msuf = gcb[:, d, 0, 128:256]
                Mind = gcb[:, d, 2, 0:2]
                t0 = i * 128
                s.dma("sp", qt[b2][:], qv[:, :, t0:t0 + 128], reads=[Z["q"]], writes=[qt[b2]])
                s.dma("sp", kt[b2][:], kv[:, :, t0:t0 + 128], reads=[Z["k"]], writes=[kt[b2]])
                s.dma("sp", ktk[b2][:], Z["ktok"][t0:t0 + 128, :], reads=[Z["ktok"]], writes=[ktk[b2]])
                s.op("pe", lambda e: e.matmul(bkA[:, 0:256], lhsT=zrb[0:33, t0:t0 + 128], rhs=w2p[0:33, d, :], start=True, stop=True),
                     reads=[zrb, w2p], writes=[bkA])
                yield
                s.op("act", lambda e: e.activation(out=tA[b2][:], in_=bkA[:, 0:256], func=AF.Exp, scale=-1.0), reads=[bkA], writes=[tA[b2]])
                s.op("act", lambda e: e.activation(out=Lt[b2][:], in_=tA[b2][:], func=AF.Ln, bias=1.0), reads=[tA[b2]], writes=[Lt[b2]])
                yield
                s.op("pe", lambda e: e.matmul(bkA[:, 256:512], lhsT=Msuf, rhs=Lt[b2][:], start=True, stop=True), reads=[gcb, Lt[b2]], writes=[bkA])
                for ct in range(2):
                    s.op("pe", lambda e: e.matmul(bkB[:, ct * 128:(ct + 1) * 128], lhsT=Lt[b2][:, ct * 128:(ct + 1) * 128], rhs=Mtri,
                                                  start=True, stop=True), reads=[gcb, Lt[b2]], writes=[bkB])
                for ct in range(2):
                    s.op("pe", lambda e: e.matmul(bkB[:, 256 + 2 * ct:258 + 2 * ct], lhsT=Lt[b2][:, ct * 128:(ct + 1) * 128], rhs=Mind,
                                                  start=True, stop=True), reads=[gcb, Lt[b2]], writes=[bkB])
                yield
                s.op("act", lambda e: e.activation(out=tA[b2][:], in_=bkA[:, 256:512], func=AF.Exp, scale=-1.0 / 16), reads=[bkA], writes=[tA[b2]])
                s.op("act", lambda e: e.activation(out=tB[b2][:], in_=bkB[:, 0:256], func=AF.Exp, scale=-1.0 / 16, bias=lnq[:, 0:1]),
                     reads=[bkB, lnq], writes=[tB[b2]])
                s.op("act", lambda e: e.activation(out=tC[b2][:], in_=bkB[:, 0:256], func=AF.Exp, scale=1.0 / 16), reads=[bkB], writes=[tC[b2]])
                s.op("act", lambda e: e.activation(out=dec_all[d][:, i], in_=bkB[:, 256:260].rearrange("p (c k) -> p c k", k=2),
                                                   func=AF.Exp, scale=-1.0 / 16), reads=[bkB], writes=[dec_all[d]])
                yield
                for c in range(2):
                    s.op("dve", lambda e: e.tensor_tensor(out=ko2[sl][64 * c:64 * c + 64, c, :], in0=tA[b2][64 * c:64 * c + 64, :],
                                                          in1=ktk[b2][64 * c:64 * c + 64, :], op=ALU.mult),
                         reads=[tA[b2], ktk[b2]], writes=[ko2[sl]])
                for hh in range(2):
                    rs_ = slice(64 * hh, 64 * hh + 64)
                    s.op("dve", lambda e: e.tensor_tensor(out=qbm[sl][rs_, :, :, hh, :],
                                                          in0=tB[b2][rs_, :].rearrange("p (ct c i) -> p ct c i", ct=2, c=2),
                                                          in1=qt[b2][rs_, :, :].rearrange("p ct (c i) -> p ct c i", c=2), op=ALU.mult),
                         reads=[tB[b2], qt[b2]], writes=[qbm[sl]])
                s.op("pool", lambda e: e.tensor_tensor(out=kin[sl][:], in0=tC[b2][:].rearrange("p (c t) -> p c t", t=128),
                                                       in1=kt[b2][:], op=ALU.mult), reads=[tC[b2], kt[b2]], writes=[kin[sl]])
                yield
                for ct in range(2):
                    for c in range(2):
                        s.op("pe", lambda e: e.matmul(bkA[64 * c:64 * c + 64, ct * 128:(ct + 1) * 128],
                                                      lhsT=kin[sl][:, ct, 64 * c:64 * c + 64],
                                                      rhs=qbm[sl][:, ct, c, :, :].rearrange("p h i -> p (h i)"), start=True, stop=True),
                             reads=[kin[sl], qbm[sl]], writes=[bkA])
                yield
                for c in range(2):
                    rs_ = slice(64 * c, 64 * c + 64)
                    s.op("dve", lambda e: e.tensor_tensor(out=scb[sl][rs_, :, 64 * c:64 * c + 64],
                                                          in0=bkA[rs_, 0:256].rearrange("p (h i) -> p h i", i=64),
                                                          in1=gc[rs_, d, 1, :].rearrange("p (h i) -> p h i", i=64), op=ALU.mult),
                         reads=[bkA, gc], writes=[scb[sl]])
                s.dma("pool", GQ[d][i, :, :], qbm[sl][:].rearrange("p a b c e -> p (a b c e)"), reads=[qbm[sl]], writes=[GQb[d][i]])
                s.dma("pool", GK[d][i, :, :], ko2[sl][:].rearrange("p a b -> p (a b)"), reads=[ko2[sl]], writes=[GKb[d][i]])
                s.dma("pool", GS[d][i, :, :], scb[sl][:].rearrange("p a b -> p (a b)"), reads=[scb[sl]], writes=[GSb[d][i]])
                yield

            def p2load(d, i, sl):
                B_ = DB[d]
                t0 = i * 128
                s.dma("sp", B_["qbm"][sl][:].rearrange("p a b c e -> p (a b c e)"), GQ[d][i, :, :], reads=[GQb[d][i]], writes=[B_["qbm"][sl]])
                s.dma("sp", B_["ko2"][sl][:].rearrange("p a b -> p (a b)"), GK[d][i, :, :], reads=[GKb[d][i]], writes=[B_["ko2"][sl]])
                s.dma("sp", B_["scb"][sl][:].rearrange("p a b -> p (a b)"), GS[d][i, :, :], reads=[GSb[d][i]], writes=[B_["scb"][sl]])
                s.dma("sp", B_["vt"][sl][:], Z["vtok"][t0:t0 + 128, :], reads=[Z["vtok"]], writes=[B_["vt"][sl]])

            def p2(d, i, sl, par=0):
                B_ = DB[d]
                qbm, kin, ko2, scb, vt, Sf, Sb = B_["qbm"], B_["kin"], B_["ko2"], B_["scb"], B_["vt"], B_["Sf"], B_["Sb"]
                t0 = i * 128
                poB = self.ps[3 * d + par]
                for hd in range(4):
                    s.op("pe", lambda e: e.matmul(poB[:, hd * 128:(hd + 1) * 128], lhsT=vt[sl][:, hd * 128:(hd + 1) * 128],
                                                  rhs=scb[sl][:, hd, :], start=(hd == 0), stop=False), reads=[vt[sl], scb[sl]], writes=[poB])
                ov = osum[:, :, t0:t0 + 128]
                corder = (0, 1) if d == 0 else (1, 0)
                for c in corder:
                    for hd in range(4):
                        ct, hh = hd // 2, hd % 2
                        s.op("pe", lambda e: e.matmul(poB[:, hd * 128 + c * 64:hd * 128 + c * 64 + 64], lhsT=Sb[ct][:, :],
                                                      rhs=qbm[sl][:, ct, c, hh, :], start=False, stop=(c == corder[1] and hd == 3)),
                             reads=[Sb[ct], qbm[sl]], writes=[poB])
                    pd = self.ps[3 * d + 2]
                    for hd in range(4):
                        ct, hh = hd // 2, hd % 2
                        s.op("pe", lambda e: e.matmul(pd[64 * hh:64 * hh + 64, ct * 128:(ct + 1) * 128],
                                                      lhsT=ko2[sl][:, c, hd * 64:(hd + 1) * 64],
                                                      rhs=vt[sl][:, hd * 128:(hd + 1) * 128], start=True, stop=True),
                             reads=[ko2[sl], vt[sl]], writes=[pd])
                    yield
                    for ct in range(2):
                        s.op("dve", lambda e: e.scalar_tensor_tensor(out=Sb[ct][:], in0=Sf[ct][:], scalar=dec_all[d][:, i, ct, c:c + 1],
                                                                     in1=pd[:, ct * 128:(ct + 1) * 128], op0=ALU.mult, op1=ALU.add),
                             reads=[Sf[ct], dec_all[d], pd], writes=[Sb[ct]])
                    for ct in range(2):
                        s.op("dve", lambda e: e.scalar_tensor_tensor(out=Sf[ct][:], in0=Sf[ct][:], scalar=dec_all[d][:, i, ct, c:c + 1],
                                                                     in1=pd[:, ct * 128:(ct + 1) * 128], op0=ALU.mult, op1=ALU.add),
                             reads=[Sf[ct], dec_all[d], pd], writes=[Sf[ct]])
                    yield
                pbv = poB[:, 0:512].rearrange("p (h t) -> p h t", t=128)
                s.op("dve", lambda e: e.tensor_tensor(out=ov, in0=pbv, in1=ov, op=ALU.add), reads=[poB, osb[i]], writes=[osb[i]])
                yield

            def order_of(d):
                return list(range(NTL)) if d == 0 else [1, 0] + list(range(NTL - 1, 1, -1))

            def stream1(d, par):
                od = order_of(d)
                for n in range(par, len(od), 2):
                    yield from p1(d, od[n], n % NS, par)
            self.round_robin([stream1(0, 0), stream1(1, 0), stream1(0, 1), stream1(1, 1)])

            def dirgen(d):
                for ct in range(2):
                    s.op("pool", lambda e: e.memset(DB[d]["Sf"][ct][:], 0.0), writes=[DB[d]["Sf"][ct]])
                    s.op("pool", lambda e: e.memset(DB[d]["Sb"][ct][:], 0.0), writes=[DB[d]["Sb"][ct]])
                od = order_of(d)
                LOOK = 2
                for n in range(len(od) + LOOK):
                    if n < len(od):
                        p2load(d, od[n], n % NS)
                    if n >= LOOK:
                        yield from p2(d, od[n - LOOK], (n - LOOK) % NS, (n - LOOK) % 2)
            self.round_robin([dirgen(0), dirgen(1)])
            NSL = 3
            sq = [s.sbuf(f"g_sq{i}", [128, 512], BF16) for i in range(NSL)]
            rs = [s.sbuf(f"g_rs{i}", [128, 512], F32) for i in range(NSL)]
            gt = [s.sbuf(f"g_gt{i}", [128, 512], BF16) for i in range(NSL)]
            yb = [s.sbuf(f"g_yb{i}", [128, 512], BF16) for i in range(NSL)]
            ones128 = self.ones_bf

            def gepi(hd, t0, n, b2):
                pm = self.ps[b2]
                s.dma("sp", gt[b2][:, :n], Z["g"][hd * 128:(hd + 1) * 128, t0:t0 + n], reads=[Z["g"]], writes=[gt[b2]])
                s.op("act", lambda e: e.activation(out=sq[b2][:, :n], in_=osum[:, hd, t0:t0 + n], func=AF.Square), reads=osb, writes=[sq[b2]])
                yield
                s.op("pe", lambda e: e.matmul(pm[:, :n], lhsT=ones128[:], rhs=sq[b2][:, :n], start=True, stop=True), reads=[ones128, sq[b2]], writes=[pm])
                yield
                s.op("act", lambda e: e.activation(out=rs[b2][:, :n], in_=pm[:, :n], func=AF.Sqrt, bias=self.eps_t[:, 0:1], scale=1.0 / 128),
                     reads=[pm, self.eps_t], writes=[rs[b2]])
                yield
                s.op("dve", lambda e: e.reciprocal(out=rs[b2][:, :n], in_=rs[b2][:, :n]), reads=[rs[b2]], writes=[rs[b2]])
                yield
                s.op("pool", lambda e: e.tensor_tensor(out=rs[b2][:, :n], in0=rs[b2][:, :n], in1=osum[:, hd, t0:t0 + n], op=ALU.mult),
                     reads=[rs[b2]] + osb, writes=[rs[b2]])
                yield
                s.op("dve", lambda e: e.scalar_tensor_tensor(out=yb[b2][:, :n], in0=rs[b2][:, :n], scalar=ng[:, 0:1], in1=gt[b2][:, :n],
                                                             op0=ALU.mult, op1=ALU.mult), reads=[rs[b2], ng, gt[b2]], writes=[yb[b2]])
                s.dma("sp", Z["y"][512 + hd * 128:512 + (hd + 1) * 128, t0:t0 + n], yb[b2][:, :n], reads=[yb[b2]], writes=[Z["y"]])
                yield
            items = [(hd, t0, n) for hd in range(4) for (t0, n) in TILES_A]
            for g0 in range(0, len(items), NSL):
                self.round_robin([gepi(hd, t0, n, j) for j, (hd, t0, n) in enumerate(items[g0:g0 + NSL])])
            s.stack = s_old

    def phase_s5(self, l):
        s, I, Z = self.s, self.I, self.Z
        NCH = T // 16
        with ExitStack() as st:
            s_old = s.stack
            s.stack = st
            s.barrier()
            TT = lambda eng, out, a, b, op, rd, wr: s.op(eng, lambda e: e.tensor_tensor(out=out, in0=a, in1=b, op=op), reads=rd, writes=wr)
            ident = s.sbuf("s_ident", [128, 128], F32)
            s.dma("sp", ident[:], I["ident"][:, :], reads=[I["ident"]], writes=[ident])
            msk = s.sbuf("s_msk", [128, 2, 2, 256], F32)
            s.dma("sp", msk[:], I["s5mask"][:, :, :, :], reads=[I["s5mask"]], writes=[msk])
            sgn = s.sbuf("s_sgn", [128, 2], F32)
            s.op("dve", lambda e: e.memset(sgn[0:64, 0:1], -1.0), writes=[sgn])
            s.op("dve", lambda e: e.memset(sgn[64:128, 0:1], 1.0), writes=[sgn])
            s.op("dve", lambda e: e.memset(sgn[0:64, 1:2], 1.0), writes=[sgn])
            s.op("dve", lambda e: e.memset(sgn[64:128, 1:2], -1.0), writes=[sgn])
            dsk = s.sbuf("s_dsk", [128, 2], F32)
            bgl = s.sbuf("s_bgl", [128, 2], F32)
            s.dma("sp", dsk[:], I["s5d"][l], reads=[I["s5d"]], writes=[dsk])
            s.dma("sp", bgl[:], I["s5bg"][l], reads=[I["s5bg"]], writes=[bgl])
            wgl = s.sbuf("s_wgl", [128, 2, 256], BF16)
            self.load_w(wgl, I["s5wg"][l], I["s5wg"], 2, 256)
            ufm = s.sbuf("s_ufm", [128, 2, T], BF16)
            s.dma("sp", ufm[:], Z["s5u"].t.rearrange("(kc p) t -> p kc t", p=128), reads=[Z["s5u"]], writes=[ufm])
            ud = s.sbuf("s_ud", [128, 2, 16, NCH], BF16)
            for kc in range(2):
                s.op("act" if kc == 0 else "pool",
                     (lambda e: e.activation(out=ud[:, kc], in_=ufm[:, kc, :].rearrange("p (n s) -> p s n", s=16), func=AF.Copy)) if kc == 0 else
                     (lambda e: e.tensor_copy(out=ud[:, kc], in_=ufm[:, kc, :].rearrange("p (n s) -> p s n", s=16))),
                     reads=[ufm], writes=[ud])
            for g in range(16):
                kc, g8 = g // 8, g % 8
                s.dma("sp", Z["us"][g].rearrange("s h n -> h s n"), ud[g8 * 16:(g8 + 1) * 16, kc, :, :], reads=[ud], writes=[Z["us"]])
            def tl(name, shape):
                return s.sbuf("s_" + name, [128] + shape, F32)
            lre, lim, dt = tl("lre", [2, 16]), tl("lim", [2, 16]), tl("dt", [2, 16])
            for (dst, nm) in ((lre, "s5lre"), (lim, "s5lim"), (dt, "s5ldt")):
                s.dma("sp", dst[:], I[nm][l].rearrange("d p g -> p d g"), reads=[I[nm]], writes=[dst])
            f2 = lambda b_: b_[:].rearrange("p d g -> p (d g)")
            s.op("act", lambda e: e.activation(out=f2(dt), in_=f2(dt), func=AF.Exp), reads=[dt], writes=[dt])
            rho, th, mg, cs, sn = tl("rho", [2, 16]), tl("th", [2, 16]), tl("mg", [2, 16]), tl("cs", [2, 16]), tl("sn", [2, 16])
            ar, ai, t1, t2 = tl("ar", [2, 16]), tl("ai", [2, 16]), tl("t1", [2, 16]), tl("t2", [2, 16])
            TT("dve", f2(rho), f2(lre), f2(dt), ALU.mult, [lre, dt], [rho])
            TT("dve", f2(th), f2(lim), f2(dt), ALU.mult, [lim, dt], [th])
            hp = s.sbuf("s_hp", [128, 1], F32)
            s.op("dve", lambda e: e.memset(hp[:], float(np.pi / 2)), writes=[hp])
            s.op("act", lambda e: e.activation(out=f2(mg), in_=f2(rho), func=AF.Exp, scale=1.0 / 32), reads=[rho], writes=[mg])
            s.op("act", lambda e: e.activation(out=f2(cs), in_=f2(th), func=AF.Sin, scale=1.0 / 32, bias=hp[:, 0:1]), reads=[th, hp], writes=[cs])
            s.op("act", lambda e: e.activation(out=f2(sn), in_=f2(th), func=AF.Sin, scale=1.0 / 32), reads=[th], writes=[sn])
            TT("dve", f2(ar), f2(mg), f2(cs), ALU.mult, [mg, cs], [ar])
            TT("dve", f2(ai), f2(mg), f2(sn), ALU.mult, [mg, sn], [ai])

            def cmul(orr, oi, xr, xi, yr, yi, rd, wr, ta, tb_):
                TT("dve", ta, xr, yr, ALU.mult, rd, [t1])
                TT("dve", tb_, xi, yi, ALU.mult, rd, [t2])
                TT("dve", orr, ta, tb_, ALU.subtract, [t1, t2], wr)
                TT("dve", ta, xr, yi, ALU.mult, rd, [t1])
                TT("dve", tb_, xi, yr, ALU.mult, rd, [t2])
                TT("dve", oi, ta, tb_, ALU.add, [t1, t2], wr)

            def csq(xr, xi):
                TT("dve", f2(t1), f2(xr), f2(xr), ALU.mult, [xr], [t1])
                TT("dve", f2(t2), f2(xi), f2(xi), ALU.mult, [xi], [t2])
                TT("dve", f2(xi), f2(xr), f2(xi), ALU.mult, [xr, xi], [xi])
                TT("dve", f2(xr), f2(t1), f2(t2), ALU.subtract, [t1, t2], [xr])
                s.op("dve", lambda e: e.tensor_scalar(out=f2(xi), in0=f2(xi), scalar1=2.0, scalar2=None, op0=ALU.mult), reads=[xi], writes=[xi])
            for _ in range(5):
                csq(ar, ai)
            den, fr, fi, am1 = tl("den", [2, 16]), tl("fr", [2, 16]), tl("fi", [2, 16]), tl("am1", [2, 16])
            TT("dve", f2(t1), f2(lre), f2(lre), ALU.mult, [lre], [t1])
            TT("dve", f2(t2), f2(lim), f2(lim), ALU.mult, [lim], [t2])
            TT("dve", f2(den), f2(t1), f2(t2), ALU.add, [t1, t2], [den])
            s.op("dve", lambda e: e.reciprocal(out=f2(den), in_=f2(den)), reads=[den], writes=[den])
            s.op("dve", lambda e: e.tensor_scalar(out=f2(am1), in0=f2(ar), scalar1=-1.0, scalar2=None, op0=ALU.add), reads=[ar], writes=[am1])
            TT("dve", f2(t1), f2(am1), f2(lre), ALU.mult, [am1, lre], [t1])
            TT("dve", f2(t2), f2(ai), f2(lim), ALU.mult, [ai, lim], [t2])
            TT("dve", f2(fr), f2(t1), f2(t2), ALU.add, [t1, t2], [fr])
            TT("dve", f2(fr), f2(fr), f2(den), ALU.mult, [fr, den], [fr])
            TT("dve", f2(t1), f2(ai), f2(lre), ALU.mult, [ai, lre], [t1])
            TT("dve", f2(t2), f2(am1), f2(lim), ALU.mult, [am1, lim], [t2])
            TT("dve", f2(fi), f2(t1), f2(t2), ALU.subtract, [t1, t2], [fi])
            TT("dve", f2(fi), f2(fi), f2(den), ALU.mult, [fi, den], [fi])
            R1, R2, C1, C2 = tl("R1", [2, 16, 16]), tl("R2", [2, 16, 16]), tl("C1", [2, 16, 16]), tl("C2", [2, 16, 16])
            BB2, BBx, t3 = tl("BB2", [2, 16, 16]), tl("BBx", [2, 16, 16]), tl("t3", [2, 16, 16])
            for (dst, nm) in ((R1, "s5R1"), (R2, "s5R2"), (C1, "s5C1"), (C2, "s5C2")):
                for d in range(2):
                    s.dma("sp", dst[:, d], I[nm][l, d], reads=[I[nm]], writes=[dst])
            f3 = lambda b_: b_[:].rearrange("p d g h -> p (d g) h")
            f3f = lambda b_: b_[:].rearrange("p d g h -> p (d g h)")
            s.op("dve", lambda e: e.tensor_scalar(out=f3f(R2), in0=f3f(R2), scalar1=sgn[:, 0:1], scalar2=None, op0=ALU.mult), reads=[R2, sgn], writes=[R2])
            s.op("dve", lambda e: e.tensor_scalar(out=f3f(C1), in0=f3f(C1), scalar1=sgn[:, 1:2], scalar2=None, op0=ALU.mult), reads=[C1, sgn], writes=[C1])
            s.op("dve", lambda e: e.tensor_scalar(out=f3f(C2), in0=f3f(C2), scalar1=-1.0, scalar2=None, op0=ALU.mult), reads=[C2], writes=[C2])
            frb = f2(fr).unsqueeze(2).to_broadcast([128, 32, 16])
            fib = f2(fi).unsqueeze(2).to_broadcast([128, 32, 16])
            TT("dve", f3(BB2), f3(R1), frb, ALU.mult, [R1, fr], [BB2])
            TT("dve", f3(t3), f3(R2), fib, ALU.mult, [R2, fi], [t3])
            TT("dve", f3(BB2), f3(BB2), f3(t3), ALU.add, [BB2, t3], [BB2])
            TT("dve", f3(BBx), f3(R2), frb, ALU.mult, [R2, fr], [BBx])
            TT("dve", f3(t3), f3(R1), fib, ALU.mult, [R1, fi], [t3])
            TT("dve", f3(BBx), f3(BBx), f3(t3), ALU.subtract, [BBx, t3], [BBx])
            pwr, pwi = tl("pwr", [2, 16, 17]), tl("pwi", [2, 16, 17])
            pk = lambda b_, k0, k1: b_[:, :, :, k0:k1].rearrange("p d g k -> p (d g) k")
            tp1, tp2 = tl("tp1", [2, 16, 8]), tl("tp2", [2, 16, 8])
            tk = lambda b_, n_: b_[:, :, :, 0:n_].rearrange("p d g k -> p (d g) k")
            s.op("dve", lambda e: e.memset(pk(pwr, 0, 1), 1.0), writes=[pwr])
            s.op("dve", lambda e: e.memset(pk(pwi, 0, 1), 0.0), writes=[pwi])
            s.op("dve", lambda e: e.tensor_copy(out=pk(pwr, 1, 2), in_=f2(ar).unsqueeze(2)), reads=[ar], writes=[pwr])
            s.op("dve", lambda e: e.tensor_copy(out=pk(pwi, 1, 2), in_=f2(ai).unsqueeze(2)), reads=[ai], writes=[pwi])
            m_ = 1
            while m_ < 16:
                yr = pk(pwr, m_, m_ + 1).to_broadcast([128, 32, m_])
                yi = pk(pwi, m_, m_ + 1).to_broadcast([128, 32, m_])
                TT("dve", tk(tp1, m_), pk(pwr, 1, m_ + 1), yr, ALU.mult, [pwr], [tp1])
                TT("dve", tk(tp2, m_), pk(pwi, 1, m_ + 1), yi, ALU.mult, [pwi], [tp2])
                TT("dve", pk(pwr, m_ + 1, 2 * m_ + 1), tk(tp1, m_), tk(tp2, m_), ALU.subtract, [tp1, tp2, pwr], [pwr])
                TT("dve", tk(tp1, m_), pk(pwr, 1, m_ + 1), yi, ALU.mult, [pwr, pwi], [tp1])
                TT("dve", tk(tp2, m_), pk(pwi, 1, m_ + 1), yr, ALU.mult, [pwr, pwi], [tp2])
                TT("dve", pk(pwi, m_ + 1, 2 * m_ + 1), tk(tp1, m_), tk(tp2, m_), ALU.add, [tp1, tp2, pwi], [pwi])
                m_ *= 2
            nr, ni = tl("nr", [2, 16]), tl("ni", [2, 16])
            TT("dve", f2(t1), f2(ar), f2(ar), ALU.mult, [ar], [t1])
            TT("dve", f2(t2), f2(ai), f2(ai), ALU.mult, [ai], [t2])
            TT("dve", f2(t1), f2(t1), f2(t2), ALU.add, [t1, t2], [t1])
            s.op("dve", lambda e: e.reciprocal(out=f2(t1), in_=f2(t1)), reads=[t1], writes=[t1])
            TT("dve", f2(nr), f2(ar), f2(t1), ALU.mult, [ar, t1], [nr])
            TT("dve", f2(ni), f2(ai), f2(t1), ALU.mult, [ai, t1], [ni])
            s.op("dve", lambda e: e.tensor_scalar(out=f2(ni), in0=f2(ni), scalar1=-1.0, scalar2=None, op0=ALU.mult), reads=[ni], writes=[ni])
            for _ in range(4):
                csq(nr, ni)
            ckr, cki, ckn = tl("ckr", [2, 16, 9]), tl("cki", [2, 16, 9]), tl("ckn", [2, 16, 9])
            xr, xi = tl("xr", [2, 16]), tl("xi", [2, 16])
            s.op("dve", lambda e: e.tensor_copy(out=xr[:], in_=pwr[:, :, :, 16]), reads=[pwr], writes=[xr])
            s.op("dve", lambda e: e.tensor_copy(out=xi[:], in_=pwi[:, :, :, 16]), reads=[pwi], writes=[xi])
            for k in range(9):
                s.op("act", lambda e: e.activation(out=ckr[:, :, :, k], in_=xr[:], func=AF.Copy), reads=[xr], writes=[ckr])
                s.op("act", lambda e: e.activation(out=cki[:, :, :, k], in_=xi[:], func=AF.Copy), reads=[xi], writes=[cki])
                if k < 8:
                    csq(xr, xi)
            s.op("dve", lambda e: e.tensor_scalar(out=ckn[:].rearrange("p d g k -> p (d g k)"), in0=cki[:].rearrange("p d g k -> p (d g k)"),
                                                  scalar1=-1.0, scalar2=None, op0=ALU.mult), reads=[cki], writes=[ckn])
            tb = {}
            for d in range(2):
                tb[d] = dict(BB2=BB2, BBx=BBx, C1=C1, C2=C2, pwr=pwr, pwi=pwi, nr=nr, ni=ni, ckr=ckr, cki=cki, ckn=ckn)

            N = NCH
            CH = {}
            stc = ExitStack()
            s.stack = stc
            for par in range(3):
                for d in range(2):
                    nm = f"{par}{d}"
                    CH[(par, d)] = dict(
                        PinT=s.sbuf("s_PinT" + nm, [128, 16, 16], F32), PinX=s.sbuf("s_PinX" + nm, [128, 16, 16], F32),
                        Pin2=s.sbuf("s_Pin2" + nm, [128, 16, 16], F32), Qm=s.sbuf("s_Qm" + nm, [128, 16, 16], F32),
                        tq=s.sbuf("s_tq" + nm, [128, 16, 16], F32), tq2=s.sbuf("s_tq2" + nm, [128, 16, 16], F32),
                        PinL=s.sbuf("s_PinL" + nm, [128, 2, 128], BF16), PinLX=s.sbuf("s_PinLX" + nm, [128, 2, 128], BF16),
                        ML=s.sbuf("s_ML" + nm, [128, 2, 256], BF16), Qb=s.sbuf("s_Qb" + nm, [128, 256], BF16),
                        W=[s.sbuf(f"s_W{nm}{i}", [128, NCH], F32) for i in range(2)],
                        WX=[s.sbuf(f"s_WX{nm}{i}", [128, NCH], F32) for i in range(2)],
                        Sp=s.sbuf("s_Sp" + nm, [128, NCH], BF16))
            Ug = [s.sbuf(f"s_Ug{i}", [128, 2, NCH], BF16) for i in range(3)]
            Ysb = [s.sbuf(f"s_Ysb{i}", [128, NCH], F32) for i in range(6)]

            def chain(g, d, par):
                c_ = CH[(par, d)]
                t = tb[d]
                u_g = Ug[par]
                PinT, PinX, Pin2, Qm, tq, tq2 = c_["PinT"], c_["PinX"], c_["Pin2"], c_["Qm"], c_["tq"], c_["tq2"]
                PinL, PinLX, ML, Qb, W, WX, Sp = c_["PinL"], c_["PinLX"], c_["ML"], c_["Qb"], c_["W"], c_["WX"], c_["Sp"]
                if d == 0:
                    pin_r, pin_i = t["pwr"][:, d, g, 15::-1], t["pwi"][:, d, g, 15::-1]
                    q_r, q_i = t["pwr"][:, d, g, 1:17], t["pwi"][:, d, g, 1:17]
                else:
                    pin_r, pin_i = t["pwr"][:, d, g, 0:16], t["pwi"][:, d, g, 0:16]
                    q_r, q_i = t["pwr"][:, d, g, 16:0:-1], t["pwi"][:, d, g, 16:0:-1]
                b16 = lambda ap: ap.unsqueeze(2).to_broadcast([128, 16, 16])
                bs = lambda ap: ap.unsqueeze(1).to_broadcast([128, 16, 16])
                BB2g, BBxg, C1g, C2g = t["BB2"][:, d, g, :], t["BBx"][:, d, g, :], t["C1"][:, d, g, :], t["C2"][:, d, g, :]
                rdT = [t["pwr"], t["pwi"], t["BB2"], t["BBx"]]
                TT("dve", PinT[:], b16(pin_r), bs(BB2g), ALU.mult, rdT, [PinT]); yield
                TT("pool", tq[:], b16(pin_i), bs(BBxg), ALU.mult, rdT, [tq]); yield
                TT("pool", PinT[:], PinT[:], tq[:], ALU.add, [PinT, tq], [PinT])
                TT("dve", PinX[:], b16(pin_r), bs(BBxg), ALU.mult, rdT, [PinX]); yield
                TT("pool", tq2[:], b16(pin_i), bs(BB2g), ALU.mult, rdT, [tq2]); yield
                TT("pool", PinX[:], PinX[:], tq2[:], ALU.subtract, [PinX, tq2], [PinX])
                rdC = [t["pwr"], t["pwi"], t["C1"], t["C2"]]
                TT("dve", Qm[:], b16(q_r), bs(C1g), ALU.mult, rdC, [Qm]); yield
                TT("pool", tq[:], b16(q_i), bs(C2g), ALU.mult, rdC, [tq]); yield
                TT("pool", Qm[:], Qm[:], tq[:], ALU.add, [Qm, tq], [Qm])
                s.op("dve", lambda e: e.tensor_scalar(out=Pin2[:], in0=PinT[:], scalar1=t["nr"][:, d, g:g + 1], scalar2=None, op0=ALU.mult),
                     reads=[PinT, t["nr"]], writes=[Pin2]); yield
                s.op("dve", lambda e: e.scalar_tensor_tensor(out=Pin2[:].rearrange("p a b -> p (a b)"), in0=PinX[:].rearrange("p a b -> p (a b)"),
                                                             scalar=t["ni"][:, d, g:g + 1], in1=Pin2[:].rearrange("p a b -> p (a b)"),
                                                             op0=ALU.mult, op1=ALU.add), reads=[PinX, t["ni"], Pin2], writes=[Pin2]); yield
                s.op("act", lambda e: e.activation(out=Qb[:], in_=Qm[:].rearrange("p a b -> p (a b)"), func=AF.Copy), reads=[Qm], writes=[Qb])
                for (src, dstL) in ((PinT, PinL), (PinX, PinLX)):
                    pt = self.next_ps()
                    sv = src[:].rearrange("p a b -> p (a b)")
                    for kt in range(2):
                        s.op("pe", lambda e: e.matmul(pt[:, kt * 128:(kt + 1) * 128], lhsT=sv[:, kt * 128:(kt + 1) * 128], rhs=ident[:],
                                                      start=True, stop=True), reads=[src, ident], writes=[pt])
                    s.op("act", lambda e: e.activation(out=dstL[:].rearrange("p a b -> p (a b)"), in_=pt[:, 0:256], func=AF.Copy), reads=[pt], writes=[dstL])
                yield
                p2v = Pin2[:].rearrange("p a b -> p (a b)")
                for kt in range(2):
                    pm = self.next_ps()
                    s.op("pe", lambda e: e.matmul(pm[:, 0:256], lhsT=p2v[:, kt * 128:(kt + 1) * 128], rhs=Qm[:].rearrange("p a b -> p (a b)"),
                                                  start=True, stop=True), reads=[Pin2, Qm], writes=[pm])
                    s.op("dve", lambda e: e.tensor_tensor(out=ML[:, kt, :], in0=pm[:, 0:256], in1=msk[:, d, kt, :], op=ALU.mult),
                         reads=[pm, msk], writes=[ML]); yield
                pw_, px_ = self.next_ps(), self.next_ps()
                for (pp, L_) in ((pw_, PinL), (px_, PinLX)):
                    for kt in range(2):
                        s.op("pe", lambda e: e.matmul(pp[:, 0:N], lhsT=L_[:, kt, :], rhs=u_g[:, kt, :], start=(kt == 0), stop=(kt == 1)),
                             reads=[L_, u_g], writes=[pp])
                for (pp, dstW, eng) in ((pw_, W[0], "act"), (px_, WX[0], "pool")):
                    if d == 0:
                        parts = [(dstW[:, 0:N], pp[:, 0:N])]
                    else:
                        parts = [(dstW[:, 0:16], pp[:, 15::-1]), (dstW[:, 16:N], pp[:, N - 1:15:-1])]
                    for (o_, i_) in parts:
                        s.op("act", lambda e: e.activation(out=o_, in_=i_, func=AF.Copy), reads=[pp], writes=[dstW])
                yield
                cur = 0
                for k in range(9):
                    dk = 1 << k
                    a_, b_ = cur, 1 - cur
                    cr = t["ckr"][:, d, g, k:k + 1]
                    ci = t["cki"][:, d, g, k:k + 1]
                    cn = t["ckn"][:, d, g, k:k + 1]
                    rd = [W[a_], WX[a_], t["ckr"], t["cki"], t["ckn"]]
                    s.op("dve", lambda e: e.scalar_tensor_tensor(out=W[b_][:, dk:N], in0=W[a_][:, 0:N - dk], scalar=cr, in1=W[a_][:, dk:N],
                                                                 op0=ALU.mult, op1=ALU.add), reads=rd, writes=[W[b_]]); yield
                    if k < 8:
                        s.op("dve", lambda e: e.scalar_tensor_tensor(out=WX[b_][:, dk:N], in0=WX[a_][:, 0:N - dk], scalar=cr, in1=WX[a_][:, dk:N],
                                                                     op0=ALU.mult, op1=ALU.add), reads=rd, writes=[WX[b_]]); yield
                    s.op("dve", lambda e: e.scalar_tensor_tensor(out=W[b_][:, dk:N], in0=WX[a_][:, 0:N - dk], scalar=ci, in1=W[b_][:, dk:N],
                                                                 op0=ALU.mult, op1=ALU.add), reads=rd + [W[b_]], writes=[W[b_]]); yield
                    s.op("act", lambda e: e.activation(out=W[b_][:, 0:dk], in_=W[a_][:, 0:dk], func=AF.Copy), reads=[W[a_]], writes=[W[b_]])
                    if k < 8:
                        s.op("dve", lambda e: e.scalar_tensor_tensor(out=WX[b_][:, dk:N], in0=W[a_][:, 0:N - dk], scalar=cn, in1=WX[b_][:, dk:N],
                                                                     op0=ALU.mult, op1=ALU.add), reads=rd + [WX[b_]], writes=[WX[b_]]); yield
                        s.op("act", lambda e: e.activation(out=WX[b_][:, 0:dk], in_=WX[a_][:, 0:dk], func=AF.Copy), reads=[WX[a_]], writes=[WX[b_]])
                    cur = b_
                Wf = W[cur]
                if d == 0:
                    s.op("pool", lambda e: e.memset(Sp[:, 0:1], 0.0), writes=[Sp])
                    s.op("act", lambda e: e.activation(out=Sp[:, 1:N], in_=Wf[:, 0:N - 1], func=AF.Copy), reads=[Wf], writes=[Sp])
                else:
                    s.op("pool", lambda e: e.memset(Sp[:, 15:16], 0.0), writes=[Sp])
                    s.op("act", lambda e: e.activation(out=Sp[:, 0:15], in_=Wf[:, 14::-1], func=AF.Copy), reads=[Wf], writes=[Sp])
                    s.op("act", lambda e: e.activation(out=Sp[:, 16:N], in_=Wf[:, N - 2:14:-1], func=AF.Copy), reads=[Wf], writes=[Sp])
                yield

            def fin(g, par):
                u_g = Ug[par]
                for mt in range(2):
                    py = self.next_ps()
                    first = True
                    for d in range(2):
                        c_ = CH[(par, d)]
                        for kt in range(2):
                            s.op("pe", lambda e: e.matmul(py[:, 0:N], lhsT=c_["ML"][:, kt, mt * 128:(mt + 1) * 128], rhs=u_g[:, kt, :],
                                                          start=first, stop=False), reads=[c_["ML"], u_g], writes=[py])
                            first = False
                        s.op("pe", lambda e: e.matmul(py[:, 0:N], lhsT=c_["Qb"][:, mt * 128:(mt + 1) * 128], rhs=c_["Sp"][:, :],
                                                      start=False, stop=(d == 1)), reads=[c_["Qb"], c_["Sp"]], writes=[py])
                    yb_ = Ysb[par * 2 + mt]
                    s.op("act", lambda e: e.activation(out=yb_[:], in_=py[:, 0:N], func=AF.Copy), reads=[py], writes=[yb_])
                    s.dma("sp", Z["ys"][g, mt * 8:(mt + 1) * 8].rearrange("j h n -> (j h) n"), yb_[:], reads=[yb_], writes=[Z["ys"]])

            for g0 in range(0, 16, 3):
                gens = []
                pars = [par for par in range(3) if g0 + par < 16]
                for par in pars:
                    g = g0 + par
                    s.dma("sp", Ug[par][:], Z["us"][g].rearrange("s h n -> (s h) n").rearrange("(kt p) n -> p kt n", p=128),
                          reads=[Z["us"]], writes=[Ug[par]])
                    for d in range(2):
                        gens.append(chain(g, d, par))
                self.round_robin(gens)
                for par in pars:
                    fin(g0 + par, par)
            s.barrier()
            stc.close()
            s.stack = st
            st3 = ExitStack()
            s.stack = st3
            Yfm = s.sbuf("s_Yfm", [128, 2, 16, NCH], F32)
            for g in range(16):
                kc, g8 = g // 8, g % 8
                s.dma("sp", Yfm[g8 * 16:(g8 + 1) * 16, kc, :, :], Z["ys"][g].rearrange("j h n -> h j n"), reads=[Z["ys"]], writes=[Yfm])
            NSL = 2
            yg = [[s.sbuf(f"s_yg{sl}{i}", [128, 512], F32) for i in range(2)] for sl in range(NSL)]
            ygb = [[s.sbuf(f"s_ygb{sl}{i}", [128, 512], BF16) for i in range(2)] for sl in range(NSL)]
            ytmp = [[s.sbuf(f"s_ytmp{sl}{i}", [128, 512], F32) for i in range(2)] for sl in range(NSL)]
            sgl = [[s.sbuf(f"s_sgl{sl}{i}", [128, 512], F32) for i in range(2)] for sl in range(NSL)]
            yo = [[s.sbuf(f"s_yo{sl}{i}", [128, 512], BF16) for i in range(2)] for sl in range(NSL)]

            def epi(t0, n, sl):
                n0, nn = t0 // 16, n // 16
                for kc in range(2):
                    x, tmp = yg[sl][kc], ytmp[sl][kc]
                    s.op("dve", lambda e: e.scalar_tensor_tensor(out=x[:, :n].rearrange("p (n j) -> p n j", j=16),
                                                                 in0=ufm[:, kc, t0:t0 + n].rearrange("p (n j) -> p n j", j=16), scalar=dsk[:, kc:kc + 1],
                                                                 in1=Yfm[:, kc, :, n0:n0 + nn].rearrange("p j n -> p n j"), op0=ALU.mult, op1=ALU.add),
                         reads=[ufm, dsk, Yfm], writes=[x])
                    yield
                    s.op("act", lambda e: e.activation(out=tmp[:, :n], in_=x[:, :n], func=AF.Square), reads=[x], writes=[tmp])
                    yield
                    s.op("dve", lambda e: e.tensor_scalar(out=tmp[:, :n], in0=tmp[:, :n], scalar1=0.044715, scalar2=1.0, op0=ALU.mult, op1=ALU.add),
                         reads=[tmp], writes=[tmp])
                    yield
                    s.op("pool", lambda e: e.tensor_tensor(out=tmp[:, :n], in0=tmp[:, :n], in1=x[:, :n], op=ALU.mult), reads=[tmp, x], writes=[tmp])
                    yield
                    s.op("act", lambda e: e.activation(out=tmp[:, :n], in_=tmp[:, :n], func=AF.Sigmoid, scale=1.5957691216), reads=[tmp], writes=[tmp])
                    yield
                    s.op("pool", lambda e: e.tensor_tensor(out=x[:, :n], in0=tmp[:, :n], in1=x[:, :n], op=ALU.mult), reads=[tmp, x], writes=[x])
                    yield
                    s.op("act", lambda e: e.activation(out=ygb[sl][kc][:, :n], in_=x[:, :n], func=AF.Copy), reads=[x], writes=[ygb[sl][kc]])
                    yield
                for mo in range(2):
                    pg = self.ps[sl * 2 + mo]
                    for kc in range(2):
                        s.op("pe", lambda e: e.matmul(pg[:, :n], lhsT=wgl[:, kc, mo * 128:(mo + 1) * 128], rhs=ygb[sl][kc][:, :n], start=(kc == 0), stop=(kc == 1)),
                             reads=[wgl, ygb[sl][kc]], writes=[pg])
                    yield
                    s.op("act", lambda e: e.activation(out=sgl[sl][mo][:, :n], in_=pg[:, :n], func=AF.Sigmoid, bias=bgl[:, mo:mo + 1]),
                         reads=[pg, bgl], writes=[sgl[sl][mo]])
                    yield
                    s.op("pool", lambda e: e.tensor_tensor(out=yo[sl][mo][:, :n], in0=sgl[sl][mo][:, :n], in1=yg[sl][mo][:, :n], op=ALU.mult),
                         reads=[sgl[sl][mo], yg[sl][mo]], writes=[yo[sl][mo]])
                    s.dma("sp", Z["y"][mo * 128:(mo + 1) * 128, t0:t0 + n], yo[sl][mo][:, :n], reads=[yo[sl][mo]], writes=[Z["y"]])
                    yield
            for g0 in range(0, len(TILES_A), NSL):
                self.round_robin([epi(t0, n, j) for j, (t0, n) in enumerate(TILES_A[g0:g0 + NSL])])
            s.barrier()
            st3.close()
            s.stack = s_old

    def load_w(self, dst, src_ap, src_buf, nkc, ncols, step=1024):
        s = self.s
        v = src_ap.rearrange("(kc p) n -> p kc n", p=128)
        for k0 in range(0, nkc, 8):
            k1 = min(nkc, k0 + 8)
            for c0 in range(0, ncols, step):
                c1 = min(ncols, c0 + step)
                s.dma("pool", dst[:, k0:k1, c0:c1], v[:, k0:k1, c0:c1], reads=[src_buf], writes=[dst])

    def phase_c1(self, l, hsrc):
        s, I, Z = self.s, self.I, self.Z
        with ExitStack() as st:
            s_old = s.stack
            s.stack = st
            s.barrier()
            wbr = s.sbuf("wbr", [128, 8, D], BF16)
            wout = s.sbuf("wout", [128, 8, D], BF16)
            self.load_w(wbr, I["w_br"][l], I["w_br"], 8, D)
            self.load_w(wout, I["w_out"][l], I["w_out"], 8, D)
            hf = [s.sbuf(f"hfC{i}", [128, 8, 512], F32) for i in range(2)]
            yt = [s.sbuf(f"ytC{i}", [128, 8, 512], BF16) for i in range(2)]
            gt = [s.sbuf(f"gtC{i}", [128, 24, 512], BF16) for i in range(2)]
            mt_ = s.sbuf("mC", [128, 8, 512], BF16)
            tts = [[s.sbuf(f"t{j}C{r}", [128, 512], F32) for j in range(3)] for r in range(3)]
            sq = s.sbuf("sqC", [128, 8, 512], BF16)
            u = s.sbuf("uC", [128, 8, 512], BF16)
            rstd = s.sbuf("rstdC", [128, 512], F32)
            hview = hsrc.t.rearrange("(kc p) t -> p kc t", p=128)
            hdst = Z["h"].t.rearrange("(kc p) t -> p kc t", p=128)
            yview = Z["y"].t.rearrange("(kc p) t -> p kc t", p=128)
            gview = Z["m"].t.rearrange("(kc p) t -> p kc t", p=128)
            u2view = Z["u2"].t.rearrange("(kc p) t -> p kc t", p=128)
            mod = self.mod[l]

            tiles_ = TILES_A[1:] if l == DEPTH - 1 else TILES_A

            def load(i):
                t0, n = tiles_[i]
                s.dma("sp", hf[i % 2][:, :, :n], hview[:, :, t0:t0 + n], reads=[hsrc], writes=[hf[i % 2]])
                s.dma("sp", yt[i % 2][:, :, :n], yview[:, :, t0:t0 + n], reads=[Z["y"]], writes=[yt[i % 2]])
                for a in range(0, 24, 8):
                    s.dma("sp", gt[i % 2][:, a:a + 8, :n], gview[:, a:a + 8, t0:t0 + n], reads=[Z["m"]], writes=[gt[i % 2]])
            def ngen(h_t_, n_, t0_):
                yield from self.norm_mod_gen(h_t_, sq, u, rstd, n_, l, 1, t0_ == 0)
                s.dma("sp", u2view[:, :, t0_:t0_ + n_], u[:, :, :n_], reads=[u], writes=[Z["u2"]])
                yield
            load(0)
            if len(tiles_) > 1:
                load(1)
            pend = None
            for ti, (t0, n) in enumerate(tiles_):
                h_t, y_t, g_t = hf[ti % 2], yt[ti % 2], gt[ti % 2]
                col = 1 if t0 == 0 else 0
                for mo in range(8):
                    if pend is not None:
                        for _ in range(4):
                            next(pend, None)
                    pa, pb, pc = self.next_ps(), self.next_ps(), self.next_ps()
                    for (pp, k0, k1) in ((pa, 0, 2), (pb, 2, 4), (pc, 4, 8)):
                        for kc in range(k0, k1):
                            s.op("pe", lambda e: e.matmul(pp[:, :n], lhsT=wbr[:, kc, mo * 128:(mo + 1) * 128], rhs=y_t[:, kc, :n],
                                                          start=(kc == k0), stop=(kc == k1 - 1)), reads=[wbr, y_t], writes=[pp])
                    t1, t2, t3 = tts[mo % 3]
                    s.op("dve", lambda e: e.tensor_tensor(out=t1[:, :n], in0=pa[:, :n], in1=g_t[:, mo, :n], op=ALU.mult),
                         reads=[pa, g_t], writes=[t1])
                    s.op("dve", lambda e: e.tensor_tensor(out=t2[:, :n], in0=pb[:, :n], in1=g_t[:, 8 + mo, :n], op=ALU.mult),
                         reads=[pb, g_t], writes=[t2])
                    s.op("dve", lambda e: e.tensor_tensor(out=t3[:, :n], in0=pc[:, :n], in1=g_t[:, 16 + mo, :n], op=ALU.mult),
                         reads=[pc, g_t], writes=[t3])
                    s.op("pool", lambda e: e.tensor_tensor(out=t1[:, :n], in0=t1[:, :n], in1=t2[:, :n], op=ALU.add),
                         reads=[t1, t2], writes=[t1])
                    s.op("pool", lambda e: e.tensor_tensor(out=mt_[:, mo, :n], in0=t1[:, :n], in1=t3[:, :n], op=ALU.add),
                         reads=[t1, t3], writes=[mt_])
                if pend is not None:
                    for _ in pend:
                        pass
                    pend = None
                    if ti + 1 < len(tiles_):
                        load(ti + 1)
                for mo in range(8):
                    pp = self.next_ps()
                    for kc in range(8):
                        s.op("pe", lambda e: e.matmul(pp[:, :n], lhsT=wout[:, kc, mo * 128:(mo + 1) * 128], rhs=mt_[:, kc, :n],
                                                      start=(kc == 0), stop=(kc == 7)), reads=[wout, mt_], writes=[pp])
                    s.op("dve", lambda e: e.scalar_tensor_tensor(out=h_t[:, mo, :n], in0=pp[:, :n], scalar=mod[:, 16 + mo, col:col + 1],
                                                                 in1=h_t[:, mo, :n], op0=ALU.mult, op1=ALU.add),
                         reads=[pp, mod, h_t], writes=[h_t])
                s.dma("sp", hdst[:, :, t0:t0 + n], h_t[:, :, :n], reads=[h_t], writes=[Z["h"]])
                pend = ngen(h_t, n, t0)
            for _ in pend:
                pass
            s.stack = s_old

    def phase_c2(self, l, final):
        s, I, Z = self.s, self.I, self.Z
        NT = 456
        tiles = [(0, 256, 0, 256)] + [(256 + NT * i, min(NT, T - 256 - NT * i), 256, T) for i in range(9)]
        if final:
            tiles = tiles[1:]
        with ExitStack() as st:
            s_old = s.stack
            s.stack = st
            s.barrier()
            wup = s.sbuf("wup", [128, 8, 2 * FH], BF16)
            wdn = s.sbuf("wdn", [128, 22, D], BF16)
            wupb = [Buf(wup.t, f"wup_blk{i}") for i in range(4)]
            wdnb = [Buf(wdn.t, f"wdn_blk{i}") for i in range(3)]
            vup = I["w_up"][l].rearrange("(kc p) n -> p kc n", p=128)
            for bi in (0, 2, 1, 3):
                s.dma("pool", wup[:, :, bi * 1408:(bi + 1) * 1408], vup[:, :, bi * 1408:(bi + 1) * 1408], reads=[I["w_up"]], writes=[wupb[bi]])
            vdn = I["w_down"][l].rearrange("(kc p) n -> p kc n", p=128)
            for bi, k0 in enumerate((0, 8, 16)):
                k1 = min(22, k0 + 8)
                s.dma("pool", wdn[:, k0:k1, :], vdn[:, k0:k1, :], reads=[I["w_down"]], writes=[wdnb[bi]])
            cw = s.sbuf("cw", [128, 44, 3], F32)
            cb = s.sbuf("cb", [128, 44], F32)
            s.dma("sp", cw[:], I["fcw"][l], reads=[I["fcw"]], writes=[cw])
            s.dma("sp", cb[:], I["fcb"][l], reads=[I["fcb"]], writes=[cb])
            gf = s.sbuf("gfin", [128, 8], F32)
            s.dma("sp", gf[:], I["lnf"][:, :], reads=[I["lnf"]], writes=[gf])
            NW = NT + 2
            ut = [s.sbuf(f"utF{i}", [128, 8, NW], BF16) for i in range(2)]
            hf = [s.sbuf("hfF0", [128, 8, NT], F32)] * 2
            hid = s.sbuf("hidF", [128, 22, NT], BF16)
            cg = [s.sbuf(f"cgF{i}", [128, NT], F32) for i in range(2)]
            cv = [s.sbuf(f"cvF{i}", [128, NT], F32) for i in range(2)]
            sq = s.sbuf("sqF", [128, 8, NT], BF16)
            rstd = s.sbuf("rstdF", [128, NT], F32)
            hview = Z["h"].t.rearrange("(kc p) t -> p kc t", p=128)
            uview = Z["u2"].t.rearrange("(kc p) t -> p kc t", p=128)
            oview = self.out.t.rearrange("(kc p) t -> p kc t", p=128)
            mod = self.mod[l]

            def load(i):
                t0, n, s0, s1 = tiles[i]
                u_t = ut[i % 2]
                lo, hi = max(t0 - 1, s0), min(t0 + n + 1, s1)
                if t0 - 1 < s0:
                    s.op("pool", lambda e: e.memset(u_t[:, :, 0:1], 0.0), writes=[u_t])
                if t0 + n + 1 > s1:
                    s.op("pool", lambda e: e.memset(u_t[:, :, n + 1:n + 2], 0.0), writes=[u_t])
                o = lo - (t0 - 1)
                s.dma("sp", u_t[:, :, o:o + hi - lo], uview[:, :, lo:hi], reads=[Z["u2"]], writes=[u_t])
            load(0)
            for ti, (t0, n, s0, s1) in enumerate(tiles):
                if ti + 1 < len(tiles):
                    load(ti + 1)
                u_t, h_t = ut[ti % 2], hf[ti % 2]
                s.dma("sp", h_t[:, :, :n], hview[:, :, t0:t0 + n], reads=[Z["h"]], writes=[h_t])
                col = 1 if t0 == 0 else 0
                for hc in range(22):
                    cgt, cvt = cg[hc % 2], cv[hc % 2]
                    for (dst, cch) in ((cgt, hc), (cvt, 22 + hc)):
                        pp = self.next_ps()
                        for kc in range(8):
                            s.op("pe", lambda e: e.matmul(pp[:, :n + 2], lhsT=wup[:, kc, cch * 128:(cch + 1) * 128], rhs=u_t[:, kc, :n + 2],
                                                          start=(kc == 0), stop=(kc == 7)), reads=[wupb[(cch * 128) // 1408], wupb[(cch * 128 + 127) // 1408], u_t], writes=[pp])
                        s.op("act", lambda e: e.activation(out=dst[:, :n], in_=pp[:, 1:n + 1], func=AF.Identity,
                                                           scale=cw[:, cch, 1:2], bias=cb[:, cch:cch + 1]),
                             reads=[pp, cw, cb], writes=[dst])
                        s.op("dve", lambda e: e.scalar_tensor_tensor(out=dst[:, :n], in0=pp[:, 0:n], scalar=cw[:, cch, 0:1],
                                                                     in1=dst[:, :n], op0=ALU.mult, op1=ALU.add),
                             reads=[pp, cw, dst], writes=[dst])
                        s.op("dve", lambda e: e.scalar_tensor_tensor(out=dst[:, :n], in0=pp[:, 2:n + 2], scalar=cw[:, cch, 2:3],
                                                                     in1=dst[:, :n], op0=ALU.mult, op1=ALU.add),
                             reads=[pp, cw, dst], writes=[dst])
                    s.op("act", lambda e: e.activation(out=cgt[:, :n], in_=cgt[:, :n], func=AF.Silu), reads=[cgt], writes=[cgt])
                    s.op("pool", lambda e: e.tensor_tensor(out=hid[:, hc, :n], in0=cgt[:, :n], in1=cvt[:, :n], op=ALU.mult),
                         reads=[cgt, cvt], writes=[hid])
                if "Zhid" in self.debug:
                    for a in range(0, 22, 6):
                        b_ = min(22, a + 6)
                        s.dma("sp", Z["hid"].t.rearrange("(kc p) t -> p kc t", p=128)[:, a:b_, t0:t0 + n], hid[:, a:b_, :n], reads=[hid], writes=[Z["hid"]])
                for mo in range(8):
                    pp = self.next_ps()
                    for hc in range(22):
                        s.op("pe", lambda e: e.matmul(pp[:, :n], lhsT=wdn[:, hc, mo * 128:(mo + 1) * 128], rhs=hid[:, hc, :n],
                                                      start=(hc == 0), stop=(hc == 21)), reads=[wdnb[hc // 8], hid], writes=[pp])
                    s.op("dve", lambda e: e.scalar_tensor_tensor(out=h_t[:, mo, :n], in0=pp[:, :n], scalar=mod[:, 40 + mo, col:col + 1],
                                                                 in1=h_t[:, mo, :n], op0=ALU.mult, op1=ALU.add),
                         reads=[pp, mod, h_t], writes=[h_t])
                if not final:
                    s.dma("sp", hview[:, :, t0:t0 + n], h_t[:, :, :n], reads=[h_t], writes=[Z["h"]])
                else:
                    if "h" in self.debug:
                        s.dma("sp", hview[:, :, t0:t0 + n], h_t[:, :, :n], reads=[h_t], writes=[Z["h"]])
                    pm = self.ps[6]
                    s.op("act", lambda e: e.activation(out=sq[:, :, :n], in_=h_t[:, :, :n], func=AF.Square), reads=[h_t], writes=[sq])
                    for kc in range(8):
                        s.op("pe", lambda e: e.matmul(pm[:, :n], lhsT=self.ones_bf[:], rhs=sq[:, kc, :n], start=(kc == 0), stop=(kc == 7)),
                             reads=[sq, self.ones_bf], writes=[pm])
                    s.op("act", lambda e: e.activation(out=rstd[:, :n], in_=pm[:, :n], func=AF.Sqrt, bias=self.eps_t[:, 0:1], scale=1.0 / D),
                         reads=[pm, self.eps_t], writes=[rstd])
                    s.op("dve", lambda e: e.reciprocal(out=rstd[:, :n], in_=rstd[:, :n]), reads=[rstd], writes=[rstd])
                    for kc in range(8):
                        s.op("dve", lambda e: e.scalar_tensor_tensor(out=h_t[:, kc, :n], in0=h_t[:, kc, :n], scalar=gf[:, kc:kc + 1],
                                                                     in1=rstd[:, :n], op0=ALU.mult, op1=ALU.mult),
                             reads=[h_t, gf, rstd], writes=[h_t])
                    s.dma("sp", oview[:, :, t0 - 256:t0 - 256 + n], h_t[:, :, :n], reads=[h_t], writes=[self.out])
            s.stack = s_old


def prep_inputs(inp, b):
    f = np.float32
    m = {}
    m["hT"] = np.ascontiguousarray(np.concatenate([inp["ctx"][b].T, inp["x"][b].T], axis=1)).astype(f, copy=False)
    cc = np.stack([inp["c"][b].reshape(8, 128).T, inp["c_ctx"].reshape(8, 128).T], axis=-1)
    m["cc"] = np.ascontiguousarray(cc).astype(f, copy=False)
    return m


def prep_shared(inp):
    f = np.float32
    m = {}
    m["ln1"] = np.ascontiguousarray(inp["ln1_g"].reshape(DEPTH, 8, 128).transpose(0, 2, 1))
    m["ln2"] = np.ascontiguousarray(inp["ln2_g"].reshape(DEPTH, 8, 128).transpose(0, 2, 1))
    m["lnf"] = np.ascontiguousarray(inp["final_g"].reshape(8, 128).T)
    m["ada_w"] = inp["ada_w"]
    m["ada_b"] = np.ascontiguousarray(inp["ada_b"].reshape(DEPTH, 48, 128).transpose(0, 2, 1))
    m["w_in"] = inp["w_in"]
    m["w_br"] = np.concatenate([inp["w_br_s5"], inp["w_br_lru"], inp["w_br_gla"]], axis=1)
    m["w_out"] = inp["w_out"]
    m["w_up"] = inp["ffn_w_up"]
    m["w_down"] = inp["ffn_w_down"]
    m["fcw"] = inp["ffn_conv_w"].transpose(0, 2, 1).reshape(DEPTH, 44, 128, 3).transpose(0, 2, 1, 3)
    m["fcb"] = inp["ffn_conv_b"].reshape(DEPTH, 44, 128).transpose(0, 2, 1)
    m["lcw"] = inp["lru_conv_w"].reshape(DEPTH, 4, 2, 128).transpose(0, 3, 2, 1)
    m["lcb"] = inp["lru_conv_b"].reshape(DEPTH, 2, 128).transpose(0, 2, 1)
    for nm, src in (("lba", "lru_ba"), ("lbi", "lru_bi"), ("llam", "lru_lam")):
        m[nm] = inp[src].reshape(DEPTH, 2, 2, 128).transpose(0, 3, 1, 2)
    for nm, src in (("lwa", "lru_wa"), ("lwi", "lru_wi")):
        bd = np.zeros((DEPTH, 128, 4, 128), f)
        for d in range(2):
            for ct in range(2):
                for nb in range(2):
                    bd[:, nb * 64:(nb + 1) * 64, d * 2 + ct, nb * 64:(nb + 1) * 64] = inp[src][:, d, 2 * ct + nb]
        m[nm] = bd
    w2p = np.zeros((DEPTH, 33, 2, 256), f)
    for d in range(2):
        w2p[:, 16 * d:16 * d + 16, d, :] = inp["gla_w2"][:, d]
        w2p[:, 32, d, :] = inp["gla_b2"][:, d]
    m["gw2p"] = w2p
    m["gng"] = inp["gla_norm_g"].reshape(DEPTH, 128, 1)
    gc = np.zeros((128, 2, 3, 256), f)
    tt = np.arange(128)
    same = (tt[:, None] // 64) == (tt[None, :] // 64)
    jj = np.arange(64)
    for d in range(2):
        if d == 0:
            tri = same & (tt[:, None] <= tt[None, :])
            suf = same & (tt[:, None] > tt[None, :])
            mk = jj[:, None] <= jj[None, :]
        else:
            tri = same & (tt[:, None] >= tt[None, :])
            suf = same & (tt[:, None] < tt[None, :])
            mk = jj[:, None] >= jj[None, :]
        gc[:, d, 0, 0:128] = tri
        gc[:, d, 0, 128:256] = suf
        gc[:, d, 1, :] = np.tile(np.tile(mk, (2, 1)), (1, 4))
        gc[:, d, 2, 0] = tt < 64
        gc[:, d, 2, 1] = tt >= 64
    m["gconst"] = gc
    m["ident"] = np.eye(128, dtype=f)
    pp = np.arange(128)
    s_of = (pp[:, None] // 16)[None, :, :] + 8 * np.arange(2)[:, None, None]
    j_of = (np.arange(256) // 16)[None, None, :]
    mk = np.zeros((128, 2, 2, 256), f)
    mk[:, 0] = (j_of >= s_of).transpose(1, 0, 2)
    mk[:, 1] = (s_of >= j_of).transpose(1, 0, 2)
    m["s5mask"] = mk
    m["s5d"] = inp["s5_d"].reshape(DEPTH, 2, 128).transpose(0, 2, 1)
    m["s5bg"] = inp["s5_b_glu"].reshape(DEPTH, 2, 128).transpose(0, 2, 1)
    m["s5wg"] = inp["s5_w_glu"]
    rep = lambda a: np.concatenate([a, a], axis=2)
    m["s5lre"] = rep(inp["s5_lam_re"].transpose(0, 1, 3, 2))
    m["s5lim"] = rep(inp["s5_lam_im"].transpose(0, 1, 3, 2))
    m["s5ldt"] = np.broadcast_to(inp["s5_log_dt"][:, :, None, :], (DEPTH, 2, 128, 16))
    bre = inp["s5_b_re"].transpose(0, 1, 3, 2, 4)
    bim = inp["s5_b_im"].transpose(0, 1, 3, 2, 4)
    m["s5R1"] = np.concatenate([bre, bim], axis=2)
    m["s5R2"] = np.concatenate([bim, bre], axis=2)
    cre = inp["s5_c_re"].transpose(0, 1, 4, 2, 3)
    cim = inp["s5_c_im"].transpose(0, 1, 4, 2, 3)
    m["s5C1"] = np.concatenate([cre, cim], axis=2)
    m["s5C2"] = np.concatenate([cim, cre], axis=2)
    return {k: np.ascontiguousarray(v, dtype=f) for k, v in m.items()}


def kernel(**inputs):
    inp = {k: np.asarray(v) for k, v in inputs.items()}
    kb = K()
    nc = kb.build()
    shared = prep_shared(inp)
    in_maps = []
    for b in range(NCORES):
        m = dict(shared)
        m.update(prep_inputs(inp, b))
        in_maps.append(m)
    res = run_bass_kernel_spmd(nc, in_maps, core_ids=list(range(NCORES)))
    out = np.stack([np.ascontiguousarray(r["outT"].T) for r in res.results], axis=0)
    return out.astype(np.float32, copy=False)
```
